# Optimizing a Trainium2 kernel written in Bass

```python
import math
import jax, jax.numpy as jnp
from jax import lax
import numpy as np

D_MODEL = 2048
BATCH = 4
SEQ = 2048
DEPTH = 4

N_MIXERS = 4
MIX_WIDTH = D_MODEL
GROUP_W = MIX_WIDTH // N_MIXERS
S5_CH_PER_GROUP = 16
S5_GROUPS = GROUP_W // S5_CH_PER_GROUP
S5_STATE = 64
S5_DT_MIN = 0.001
S5_DT_MAX = 0.1
POOL_WINDOWS = (2, 4, 8, 16)
POOL_CH = GROUP_W // len(POOL_WINDOWS)
CONV_WIDTH = 31
ATT_HEADS = 8
ATT_HEAD_DIM = GROUP_W // ATT_HEADS
DILATED_PATTERNS = ((128, 1), (512, 4), (2048, 16))
ATT_BLOCK = 128
REL_BUCKETS = 32
REL_MAX_DIST = 2048
MEM_LEN = 256
X_HEADS = 4
X_HEAD_DIM = 128
X_WIDTH = X_HEADS * X_HEAD_DIM
D_FF = 4 * D_MODEL
NORM_EPS = 1e-6
NEG_INF = -1e30
IN_WIDTH = GROUP_W + GROUP_W + 2 * GROUP_W + 3 * GROUP_W

kernel_name = 'hymba_style_multimixer_trunk'

F32 = jnp.float32


def rmsnorm(x, g):
    xf = x.astype(F32)
    y = xf * lax.rsqrt(jnp.mean(xf * xf, axis=-1, keepdims=True) + NORM_EPS)
    return (y * g.astype(F32)).astype(x.dtype)


def layernorm(x, g, b):
    xf = x.astype(F32)
    xc = xf - jnp.mean(xf, axis=-1, keepdims=True)
    y = xc * lax.rsqrt(jnp.mean(xc * xc, axis=-1, keepdims=True) + NORM_EPS)
    return (y * g.astype(F32) + b.astype(F32)).astype(x.dtype)


def group_rmsnorm(y, g, out_dtype):
    Bsz, L, _ = y.shape
    yf = y.astype(F32).reshape(Bsz, L, N_MIXERS, GROUP_W)
    yf = yf * lax.rsqrt(jnp.mean(yf * yf, axis=-1, keepdims=True) + NORM_EPS)
    return (yf.reshape(Bsz, L, MIX_WIDTH) * g.astype(F32)).astype(out_dtype)


def _cmul(ar, ai, br, bi):
    return ar * br - ai * bi, ar * bi + ai * br


def s5_mixer(u, lam_re, lam_im, log_dt, b_re, b_im, c_re, c_im, d_skip, w_glu):
    Bsz, L, _ = u.shape
    uf = u.astype(F32).reshape(Bsz, L, S5_GROUPS, S5_CH_PER_GROUP)
    lr = lam_re.astype(F32)
    li = lam_im.astype(F32)
    dt = jnp.exp(log_dt.astype(F32))[:, None]
    mag = jnp.exp(lr * dt)
    ab_r, ab_i = mag * jnp.cos(li * dt), mag * jnp.sin(li * dt)
    den = lr * lr + li * li
    nr, ni = ab_r - 1.0, ab_i
    f_r = (nr * lr + ni * li) / den
    f_i = (ni * lr - nr * li) / den
    bb_r, bb_i = _cmul(f_r[..., None], f_i[..., None], b_re.astype(F32), b_im.astype(F32))
    bu_r = jnp.einsum('gnc,blgc->blgn', bb_r, uf)
    bu_i = jnp.einsum('gnc,blgc->blgn', bb_i, uf)
    a_r = jnp.broadcast_to(ab_r, bu_r.shape)
    a_i = jnp.broadcast_to(ab_i, bu_i.shape)

    def combine(e1, e2):
        a1r, a1i, b1r, b1i = e1
        a2r, a2i, b2r, b2i = e2
        ar, ai = _cmul(a2r, a2i, a1r, a1i)
        br, bi = _cmul(a2r, a2i, b1r, b1i)
        return ar, ai, br + b2r, bi + b2i

    _, _, xr, xi = lax.associative_scan(combine, (a_r, a_i, bu_r, bu_i), axis=1)
    y = jnp.einsum('gcn,blgn->blgc', c_re.astype(F32), xr) - jnp.einsum('gcn,blgn->blgc', c_im.astype(F32), xi)
    y = y.reshape(Bsz, L, GROUP_W) + d_skip.astype(F32) * uf.reshape(Bsz, L, GROUP_W)
    g = jax.nn.gelu(y)
    out = g * jax.nn.sigmoid(jnp.einsum('blc,cd->bld', g, w_glu.astype(F32)))
    return out.astype(u.dtype)


def pool_mixer(u, pool_w, pool_scale):
    Bsz, L, _ = u.shape
    uf = u.astype(F32).reshape(Bsz, L, len(POOL_WINDOWS), POOL_CH)
    cs = jnp.cumsum(uf, axis=1)
    t = jnp.arange(L)
    pooled = []
    for gi, w in enumerate(POOL_WINDOWS):
        c = cs[:, :, gi]
        shifted = jnp.pad(c, ((0, 0), (w, 0), (0, 0)))[:, :L]
        cnt = jnp.minimum(t + 1, w).astype(F32)[None, :, None]
        pooled.append((c - shifted) / cnt - uf[:, :, gi])
    p = jnp.stack(pooled, axis=2)
    y = jnp.einsum('blgc,gcd->blgd', p, pool_w.astype(F32)).reshape(Bsz, L, GROUP_W)
    return (y * pool_scale.astype(F32)).astype(u.dtype)


def conv_mixer(u, w_dw, b_dw, ln_g, ln_b, w_pw):
    val, gate = jnp.split(u, 2, axis=-1)
    h = val * jax.nn.sigmoid(gate)
    h = lax.conv_general_dilated(h, w_dw[:, None, :], window_strides=(1,),
                                 padding=((CONV_WIDTH - 1, 0),),
                                 dimension_numbers=('NWC', 'WIO', 'NWC'),
                                 feature_group_count=GROUP_W) + b_dw
    h = jax.nn.silu(layernorm(h, ln_g, ln_b))
    return jnp.einsum('blc,cd->bld', h, w_pw)


def _t5_bucket(dist):
    n = np.maximum(dist, 0)
    max_exact = REL_BUCKETS // 2
    large = max_exact + (np.log(np.maximum(n, 1) / max_exact) / np.log(REL_MAX_DIST / max_exact)
                         * (REL_BUCKETS - max_exact)).astype(np.int64)
    large = np.minimum(large, REL_BUCKETS - 1)
    return np.where(n < max_exact, n, large).astype(np.int32)


def _dilated_branch(q, k, v, rel_bias, window, dilation):
    Bsz, L, H, E = q.shape
    Ls = L // dilation
    nb = -(-Ls // ATT_BLOCK)
    pad = nb * ATT_BLOCK - Ls

    def to_sub(t):
        t = t.reshape(Bsz, Ls, dilation, H, E).transpose(0, 2, 3, 1, 4)
        return jnp.pad(t, ((0, 0), (0, 0), (0, 0), (0, pad), (0, 0)))

    def band(t):
        t = jnp.pad(t, ((0, 0), (0, 0), (0, 0), (ATT_BLOCK, 0), (0, 0)))
        t = t.reshape(Bsz, dilation, H, nb + 1, ATT_BLOCK, E)
        return jnp.concatenate([t[:, :, :, :-1], t[:, :, :, 1:]], axis=4)

    qb = to_sub(q).reshape(Bsz, dilation, H, nb, ATT_BLOCK, E)
    kb = band(to_sub(k))
    vb = band(to_sub(v))
    s = jnp.einsum('bdhnqe,bdhnke->bdhnqk', qb, kb, preferred_element_type=F32) * (E ** -0.5)
    a_idx = np.arange(ATT_BLOCK)[:, None]
    b_idx = np.arange(2 * ATT_BLOCK)[None, :]
    sub_dist = a_idx + ATT_BLOCK - b_idx
    key_idx = np.arange(nb)[:, None, None] * ATT_BLOCK - ATT_BLOCK + b_idx[None]
    valid = (sub_dist >= 0) & (sub_dist <= window // dilation) & (key_idx >= 0)
    bucket = _t5_bucket(sub_dist * dilation)
    bias = jnp.transpose(rel_bias[jnp.asarray(bucket)], (2, 0, 1)).astype(F32)
    s = s + bias[None, None, :, None]
    s = jnp.where(jnp.asarray(valid)[None, None, None], s, NEG_INF)
    m = jnp.max(s, axis=-1, keepdims=True)
    p = jnp.exp(s - m)
    den = jnp.sum(p, axis=-1, keepdims=True)
    o = jnp.einsum('bdhnqk,bdhnke->bdhnqe', p, vb.astype(F32)) / den
    lse = (m + jnp.log(den))[..., 0]
    o = o.reshape(Bsz, dilation, H, nb * ATT_BLOCK, E)[:, :, :, :Ls]
    o = o.transpose(0, 3, 1, 2, 4).reshape(Bsz, L, H, E)
    lse = lse.reshape(Bsz, dilation, H, nb * ATT_BLOCK)[:, :, :, :Ls]
    lse = lse.transpose(0, 3, 1, 2).reshape(Bsz, L, H)
    return o, lse


def dilated_attention(qkv, rel_bias):
    Bsz, L, _ = qkv.shape
    q, k, v = [t.reshape(Bsz, L, ATT_HEADS, ATT_HEAD_DIM) for t in jnp.split(qkv, 3, axis=-1)]
    outs, lses = [], []
    for window, dilation in DILATED_PATTERNS:
        o, lse = _dilated_branch(q, k, v, rel_bias, window, dilation)
        outs.append(o)
        lses.append(lse)
    wts = jax.nn.softmax(jnp.stack(lses, axis=0), axis=0)
    o = jnp.einsum('pblh,pblhe->blhe', wts, jnp.stack(outs, axis=0))
    return o.reshape(Bsz, L, GROUP_W).astype(qkv.dtype)


def cross_attention(h, mem_n, w_xq, w_xk, w_xv, w_xo):
    Bsz, L, _ = h.shape
    M = mem_n.shape[1]
    q = jnp.einsum('bld,de->ble', h, w_xq).reshape(Bsz, L, X_HEADS, X_HEAD_DIM)
    k = jnp.einsum('bmd,de->bme', mem_n, w_xk).reshape(Bsz, M, X_HEADS, X_HEAD_DIM)
    v = jnp.einsum('bmd,de->bme', mem_n, w_xv).reshape(Bsz, M, X_HEADS, X_HEAD_DIM)
    s = jnp.einsum('blhe,bmhe->bhlm', q, k, preferred_element_type=F32) * (X_HEAD_DIM ** -0.5)
    p = jax.nn.softmax(s, axis=-1)
    o = jnp.einsum('bhlm,bmhe->blhe', p, v.astype(F32)).reshape(Bsz, L, X_WIDTH).astype(h.dtype)
    return jnp.einsum('ble,ed->bld', o, w_xo)


def setup_inputs(seed: int = 0) -> dict:
    key = jax.random.key(seed)
    ks = iter(jax.random.split(key, 48))

    def nrm(shape, scale):
        return jax.random.normal(next(ks), shape, F32) * scale

    def gain(shape):
        return 1.0 + nrm(shape, 0.02)

    x = nrm((BATCH, SEQ, D_MODEL), 1.0)
    mem = nrm((BATCH, MEM_LEN, D_MODEL), 1.0)
    rel_bias = nrm((REL_BUCKETS, ATT_HEADS), 0.2)
    mem_norm_g = gain((D_MODEL,))
    norm_mix_g = gain((DEPTH, D_MODEL))
    w_in = nrm((DEPTH, D_MODEL, IN_WIDTH), D_MODEL ** -0.5)
    s5_lam_re = -0.5 * jnp.exp(nrm((DEPTH, S5_GROUPS, S5_STATE), 0.02))
    s5_lam_im = math.pi * jnp.arange(S5_STATE, dtype=F32) + nrm((DEPTH, S5_GROUPS, S5_STATE), 0.02)
    s5_log_dt = jax.random.uniform(next(ks), (DEPTH, S5_GROUPS), F32,
                                   math.log(S5_DT_MIN), math.log(S5_DT_MAX))
    s5_b_re = nrm((DEPTH, S5_GROUPS, S5_STATE, S5_CH_PER_GROUP), (2 * S5_CH_PER_GROUP) ** -0.5)
    s5_b_im = nrm((DEPTH, S5_GROUPS, S5_STATE, S5_CH_PER_GROUP), (2 * S5_CH_PER_GROUP) ** -0.5)
    s5_c_re = nrm((DEPTH, S5_GROUPS, S5_CH_PER_GROUP, S5_STATE), 0.5)
    s5_c_im = nrm((DEPTH, S5_GROUPS, S5_CH_PER_GROUP, S5_STATE), 0.5)
    s5_d = nrm((DEPTH, GROUP_W), 1.0)
    s5_w_glu = nrm((DEPTH, GROUP_W, GROUP_W), GROUP_W ** -0.5)
    pool_w = nrm((DEPTH, len(POOL_WINDOWS), POOL_CH, POOL_CH), POOL_CH ** -0.5)
    pool_scale = gain((DEPTH, GROUP_W))
    conv_w_dw = nrm((DEPTH, CONV_WIDTH, GROUP_W), CONV_WIDTH ** -0.5)
    conv_b_dw = nrm((DEPTH, GROUP_W), 0.02)
    conv_ln_g = gain((DEPTH, GROUP_W))
    conv_ln_b = nrm((DEPTH, GROUP_W), 0.02)
    conv_w_pw = nrm((DEPTH, GROUP_W, GROUP_W), GROUP_W ** -0.5)
    grp_norm_g = gain((DEPTH, MIX_WIDTH))
    w_out = nrm((DEPTH, MIX_WIDTH, D_MODEL), MIX_WIDTH ** -0.5)
    norm_x_g = gain((DEPTH, D_MODEL))
    w_xq = nrm((DEPTH, D_MODEL, X_WIDTH), D_MODEL ** -0.5)
    w_xk = nrm((DEPTH, D_MODEL, X_WIDTH), D_MODEL ** -0.5)
    w_xv = nrm((DEPTH, D_MODEL, X_WIDTH), D_MODEL ** -0.5)
    w_xo = nrm((DEPTH, X_WIDTH, D_MODEL), X_WIDTH ** -0.5)
    norm_mlp_g = gain((DEPTH, D_MODEL))
    w_up = nrm((DEPTH, D_MODEL, D_FF), D_MODEL ** -0.5)
    w_down = nrm((DEPTH, D_FF, D_MODEL), D_FF ** -0.5)
    norm_final_g = gain((D_MODEL,))
    return {'x': x, 'mem': mem, 'rel_bias': rel_bias, 'mem_norm_g': mem_norm_g,
            'norm_mix_g': norm_mix_g, 'w_in': w_in,
            's5_lam_re': s5_lam_re, 's5_lam_im': s5_lam_im, 's5_log_dt': s5_log_dt,
            's5_b_re': s5_b_re, 's5_b_im': s5_b_im, 's5_c_re': s5_c_re, 's5_c_im': s5_c_im,
            's5_d': s5_d, 's5_w_glu': s5_w_glu,
            'pool_w': pool_w, 'pool_scale': pool_scale,
            'conv_w_dw': conv_w_dw, 'conv_b_dw': conv_b_dw, 'conv_ln_g': conv_ln_g,
            'conv_ln_b': conv_ln_b, 'conv_w_pw': conv_w_pw,
            'grp_norm_g': grp_norm_g, 'w_out': w_out,
            'norm_x_g': norm_x_g, 'w_xq': w_xq, 'w_xk': w_xk, 'w_xv': w_xv, 'w_xo': w_xo,
            'norm_mlp_g': norm_mlp_g, 'w_up': w_up, 'w_down': w_down,
            'norm_final_g': norm_final_g}


def reference(x, mem, rel_bias, mem_norm_g, norm_mix_g, w_in,
              s5_lam_re, s5_lam_im, s5_log_dt, s5_b_re, s5_b_im, s5_c_re, s5_c_im, s5_d, s5_w_glu,
              pool_w, pool_scale, conv_w_dw, conv_b_dw, conv_ln_g, conv_ln_b, conv_w_pw,
              grp_norm_g, w_out, norm_x_g, w_xq, w_xk, w_xv, w_xo,
              norm_mlp_g, w_up, w_down, norm_final_g):
    mem_n = rmsnorm(mem, mem_norm_g)
    h = x
    for l in range(DEPTH):
        xn = rmsnorm(h, norm_mix_g[l])
        proj = jnp.einsum('bld,dc->blc', xn, w_in[l])
        u_a, u_b, u_c, qkv = jnp.split(proj, [GROUP_W, 2 * GROUP_W, 4 * GROUP_W], axis=-1)
        y_a = s5_mixer(u_a, s5_lam_re[l], s5_lam_im[l], s5_log_dt[l], s5_b_re[l], s5_b_im[l],
                       s5_c_re[l], s5_c_im[l], s5_d[l], s5_w_glu[l])
        y_b = pool_mixer(u_b, pool_w[l], pool_scale[l])
        y_c = conv_mixer(u_c, conv_w_dw[l], conv_b_dw[l], conv_ln_g[l], conv_ln_b[l], conv_w_pw[l])
        y_d = dilated_attention(qkv, rel_bias)
        y = group_rmsnorm(jnp.concatenate([y_a, y_b, y_c, y_d], axis=-1), grp_norm_g[l], h.dtype)
        h = h + jnp.einsum('blc,cd->bld', y, w_out[l])
        h = h + cross_attention(rmsnorm(h, norm_x_g[l]), mem_n, w_xq[l], w_xk[l], w_xv[l], w_xo[l])
        hn = rmsnorm(h, norm_mlp_g[l])
        h = h + jnp.einsum('blf,fd->bld', jnp.square(jax.nn.relu(jnp.einsum('bld,df->blf', hn, w_up[l]))), w_down[l])
    return rmsnorm(h, norm_final_g)
```

```python
import numpy as np
import ml_dtypes
import concourse.bass as bass
import concourse.mybir as mybir
from concourse.bass_utils import run_bass_kernel_spmd
from contextlib import ExitStack

F32 = mybir.dt.float32
BF16 = mybir.dt.bfloat16
ALU = mybir.AluOpType
AF = mybir.ActivationFunctionType
AX = mybir.AxisListType

NL = 4
D = 2048
L = 2048
DFF = 8192
EPS = 1e-6
PI = float(np.pi)


class Res:
    __slots__ = ("name", "last_w", "readers")

    def __init__(self, name):
        self.name = name
        self.last_w = None
        self.readers = []


class Op:
    __slots__ = ("eng", "fn", "deps", "is_dma", "sem", "count_after", "has_dep", "dsem", "bar", "inc")

    def __init__(self, eng, fn, is_dma, dsem):
        self.eng = eng
        self.fn = fn
        self.deps = set()
        self.is_dma = is_dma
        self.has_dep = False
        self.dsem = dsem
        self.bar = False
        self.inc = 16


class Prog:
    ENGS = ("pe", "act", "dve", "pool", "sp")
    RING = 8

    def __init__(self, nc):
        self.nc = nc
        self.ops = []
        self.last_eng = {}
        self.last_dma = {}

    def _add(self, eng, fn, reads, writes, is_dma=False, dsem=None):
        op = Op(eng, fn, is_dma, dsem)
        oid = len(self.ops)
        for r in reads:
            if r.last_w is not None:
                op.deps.add(r.last_w)
        for r in writes:
            if r.last_w is not None:
                op.deps.add(r.last_w)
            for x in r.readers:
                op.deps.add(x)
        for r in reads:
            r.readers.append(oid)
        for r in writes:
            r.last_w = oid
            r.readers = []
        op.deps.discard(oid)
        self.ops.append(op)
        if is_dma:
            lst = self.last_dma.setdefault(dsem, [])
            lst.append(oid)
            if len(lst) > self.RING:
                lst.pop(0)
        else:
            self.last_eng[eng] = oid
        return oid

    def pe(self, fn, reads=(), writes=()):
        return self._add("pe", fn, reads, writes)

    def act(self, fn, reads=(), writes=()):
        return self._add("act", fn, reads, writes)

    def dve(self, fn, reads=(), writes=()):
        return self._add("dve", fn, reads, writes)

    def pool(self, fn, reads=(), writes=()):
        return self._add("pool", fn, reads, writes)

    def dma(self, q, fn, reads=(), writes=(), cls="d"):
        return self._add(q, fn, reads, writes, is_dma=True, dsem=(q, cls))

    def cc(self, fn, reads=(), writes=()):
        oid = self._add("pool", fn, reads, writes, is_dma=True, dsem=("pool", "cc"))
        self.ops[oid].inc = 1
        return oid

    def barrier(self):
        deps = set(self.last_eng.values())
        for lst in self.last_dma.values():
            deps |= set(lst)
        for e in self.ENGS:
            op = Op(e, None, False, None)
            op.bar = True
            op.deps = set(deps)
            self.ops.append(op)

    def emit(self):
        nc = self.nc
        ops = self.ops
        for op in ops:
            for d in op.deps:
                ops[d].has_dep = True
        per_eng = {e: [] for e in self.ENGS}
        for i, op in enumerate(ops):
            per_eng[op.eng].append(i)
        for e in self.ENGS:
            for i in reversed(per_eng[e]):
                if not ops[i].is_dma and not ops[i].bar:
                    ops[i].has_dep = True
                    break
        comp_count = {e: 0 for e in self.ENGS}
        dma_idx = {}
        dma_count = {}
        prev_wait = {}
        for i, op in enumerate(ops):
            if op.bar:
                continue
            if op.is_dma:
                k = op.dsem
                n = dma_idx.get(k, 0)
                dma_idx[k] = n + 1
                ring = 4 if k[1] == "cc" else self.RING
                sk = ("dma",) + k + (str(n % ring),)
                dma_count[sk] = dma_count.get(sk, 0) + op.inc
                op.sem = sk
                op.count_after = dma_count[sk]
                prev_wait[i] = (sk, op.count_after - op.inc)
            else:
                if op.has_dep:
                    comp_count[op.eng] += 1
                op.sem = ("eng", op.eng)
                op.count_after = comp_count[op.eng]
        keys = [("eng", e) for e in self.ENGS] + list(dma_count.keys())
        with ExitStack() as st:
            sems = {}
            for k in keys:
                sems[k] = st.enter_context(nc.semaphore("s_" + "_".join(k)))
            block = st.enter_context(nc.Block())

            def make(e):
                def body(eng):
                    waited = {}
                    for i in per_eng[e]:
                        op = ops[i]
                        need = {}
                        for d in op.deps:
                            p = ops[d]
                            if (not p.is_dma) and p.eng == e and e == "pe" and not op.bar:
                                continue
                            need[p.sem] = max(need.get(p.sem, 0), p.count_after)
                        if i in prev_wait and prev_wait[i][1] > 0:
                            k_, v_ = prev_wait[i]
                            need[k_] = max(need.get(k_, 0), v_)
                        for k, v in need.items():
                            if waited.get(k, 0) >= v:
                                continue
                            eng.wait_ge(sems[k], v)
                            waited[k] = v
                        if op.bar:
                            continue
                        ins = op.fn(eng)
                        if op.is_dma:
                            ins.then_inc(sems[op.sem], op.inc)
                        elif op.has_dep:
                            ins.then_inc(sems[op.sem], 1)
                    if e == "sp":
                        for k in keys:
                            v = dma_count[k] if k[0] == "dma" else comp_count[k[1]]
                            if v > 0 and waited.get(k, 0) < v:
                                eng.wait_ge(sems[k], v)
                return body

            block.tensor(make("pe"))
            block.scalar(make("act"))
            block.vector(make("dve"))
            block.gpsimd(make("pool"))
            block.sync(make("sp"))
        return len(ops)


def _prod(s):
    r = 1
    for v in s:
        r *= v
    return r


class Arena:
    def __init__(self, t, nbytes):
        self.t = t
        self.cap = nbytes
        self.top = 0

    def alloc(self, shape, dt):
        n = _prod(shape)
        esz = 4 if dt == F32 else 2
        nb = (n * esz + 31) // 32 * 32
        off = self.top
        self.top += nb
        assert self.top <= self.cap, f"arena overflow {self.top} > {self.cap}"
        v = self.t[:, off // 4:(off + nb) // 4]
        if dt != F32:
            v = v.bitcast(dt)
        v = v[:, :n]
        if len(shape) == 2:
            v = v.rearrange("p (a b) -> p a b", a=shape[0])
        elif len(shape) == 3:
            v = v.rearrange("p (a b c) -> p a b c", a=shape[0], b=shape[1])
        return v

    def mark(self):
        return self.top

    def reset(self, m=0):
        self.top = m


def _t5_bucket(n):
    n = np.maximum(n, 0)
    me = 16
    large = me + (np.log(np.maximum(n, 1) / me) / np.log(2048 / me) * (32 - me)).astype(np.int64)
    large = np.minimum(large, 31)
    return np.where(n < me, n, large).astype(np.int64)


NEM = 31 * 255
NEMP = 7936


def _static_tables():
    i = np.arange(31)[:, None]
    m = np.arange(255)[None, :]
    d = 16 * (m - 127) + (i - 15)
    bk = _t5_bucket(d)
    oh = np.zeros((32, NEMP), np.float32)
    oh[bk.reshape(-1), np.arange(NEM)] = 1.0
    mult = ((d >= 0) & (d <= 128)).astype(np.float32)
    mult += ((d >= 0) & (d % 4 == 0) & (d <= 512))
    mult += ((d >= 0) & (d % 16 == 0))
    mt = np.zeros((8, NEMP), np.float32)
    mt[:, :NEM] = mult.reshape(-1)[None]
    ident = np.eye(128, dtype=np.float32)
    J = np.ascontiguousarray(ident[::-1])
    invcnt = np.tile((1.0 / (np.arange(16) + 1.0)).astype(np.float32)[None], (128, 1))
    return dict(c_oh=oh, c_mult=mt, c_ident=ident, c_J=J, c_invcnt=invcnt)


def build(n_layers=NL, debug=False, stages=None):
    nc = bass.Bass("TRN2", target_bir_lowering=False)
    P = Prog(nc)
    RES = {}

    def R(*key):
        if key not in RES:
            RES[key] = Res(str(key))
        return RES[key]

    def on(s):
        return stages is None or s in stages

    def din(name, shape, dt=F32):
        return nc.dram_tensor(name, list(shape), dt, kind="ExternalInput").ap()

    def dscr(name, shape, dt=F32):
        if debug:
            return nc.dram_tensor(name, list(shape), dt, kind="ExternalOutput").ap()
        return nc.dram_tensor(name, list(shape), dt).ap()

    x = din("x", [L, D])
    mem = din("mem", [256, D])
    rel_bias = din("rel_bias", [32, 4])
    mem_norm_g = din("mem_norm_g", [1, D])
    gvec = din("gvec", [128, 5 * NL * 16 + 16])
    cvec = din("cvec", [128, 5 * NL * 4])
    cwdw = din("cwdw", [128, NL * 4 * 31])
    w_in = din("w_in", [NL, D, 3584])
    s5_sr = din("s5_sr", [NL, 128, 24])
    s5_bT = din("s5_bT", [NL, 2, 128, 1024])
    s5_cT = din("s5_cT", [NL, 2, 128, 1024])
    s5_w_glu = din("s5_w_glu", [NL, 512, 512])
    w_ua = din("w_ua", [NL, D, 256])
    pool_w = din("pool_w", [NL, 4, 128, 128])
    conv_w_pw = din("conv_w_pw", [NL, 512, 512])
    w_out = din("w_out", [NL, D, D])
    w_xq = din("w_xq", [NL, D, 512])
    w_xk = din("w_xk", [NL, D, 512])
    w_xv = din("w_xv", [NL, D, 512])
    w_xo = din("w_xo", [NL, 512, D])
    w_up = din("w_up", [NL, D, DFF])
    w_down = din("w_down", [NL, DFF, D])
    c_oh = din("c_oh", [32, NEMP])
    c_mult = din("c_mult", [8, NEMP])
    w_qkv = din("w_qkv", [NL, D, 768])
    c_ident = din("c_ident", [128, 128])
    c_J = din("c_J", [128, 128])
    c_invcnt = din("c_invcnt", [128, 16])
    sel_in = din("sel", [128, 2])
    y_out = nc.dram_tensor("y", [L // 2, D], F32, kind="ExternalOutput").ap()

    hx_t = [nc.dram_tensor(f"hx{q}", [1024, 1024], F32) for q in range(4)]
    xin_t = [nc.dram_tensor(f"xin{q}", [512, 1024], F32) for q in range(4)]

    def hxv(q, r):
        return hx_t[q].ap().rearrange("(r ci p) t -> r p ci t", r=2, ci=4, p=128)[r]
    uaT_d = dscr("uaT_d", [256, L], BF16)
    xs_in = nc.dram_tensor("xs_in", [256, L], F32)
    xs_out = nc.dram_tensor("xs_out", [512, L], F32)
    ubT_d = dscr("ubT_d", [512, L])
    hgT_d = dscr("hgT_d", [512, L], BF16)
    qz_d = dscr("qz_d", [4, 128, L], BF16)
    kT_d = dscr("kT_d", [2, 128, L], BF16)
    vz_d = dscr("vz_d", [16, 128, 512], BF16)
    xa_in = nc.dram_tensor("xa_in", [256, L], F32)
    xa_out = nc.dram_tensor("xa_out", [512, L], F32)
    ynT_d = dscr("ynT_d", [D, L], BF16)
    emtab_d = dscr("emtab_d", [4, NEMP])
    em_d = dscr("em_d", [4, 128, 3968], BF16)
    dbg_d = dscr("dbg_d", [D, L]) if debug else None

    with ExitStack() as st:
        ARENA_B = 180 * 1024
        PERS_B = 27 * 1024
        arena_t = st.enter_context(nc.sbuf_tensor("arena", [128, ARENA_B // 4], F32))
        pers_t = st.enter_context(nc.sbuf_tensor("pers", [128, PERS_B // 4], F32))
        ps = st.enter_context(nc.psum_tensor("ps", [128, 8, 512], F32))
        A = Arena(arena_t, ARENA_B)
        PA = Arena(pers_t, PERS_B)

        def PS(b):
            return ps[:, b, :]

        def RP(b):
            return R("ps", b)

        class Alt:
            def __init__(self):
                self.i = 0

            def copy(self, out, in_, reads, writes):
                self.i += 1
                if self.i % 2:
                    P.act(lambda e: e.activation(out=out, in_=in_, func=AF.Copy), reads=reads, writes=writes)
                else:
                    P.dve(lambda e: e.tensor_copy(out=out, in_=in_), reads=reads, writes=writes)

        alt = Alt()

        ident = PA.alloc([128], F32)
        Jm = PA.alloc([128], F32)
        ones_f = PA.alloc([128], F32)
        ones_b = PA.alloc([128], BF16)
        ohh = PA.alloc([2, 128], BF16)
        ident_b = PA.alloc([128], BF16)
        invcnt = PA.alloc([16], F32)
        gv = PA.alloc([5 * NL * 16 + 16], F32)
        cv = PA.alloc([5 * NL * 4], F32)
        cw = PA.alloc([NL * 4 * 31], F32)
        memnT = PA.alloc([16, 256], BF16)
        KxT = PA.alloc([4, 256], BF16)
        Vx = PA.alloc([2, 512], BF16)
        rmask = PA.alloc([2048], F32)
        selt = PA.alloc([2], F32)
        r_const = R("const")

        def gcol(kind, l, c):
            if kind == 4:
                o = 4 * NL * 16 + c
            else:
                o = (kind * NL + l) * 16 + c
            return gv[:, o:o + 1]

        def ccol(kind, l, c):
            o = (kind * NL + l) * 4 + c
            return cv[:, o:o + 1]

        wq = "pool"

        def setup():
            P.dma("sp", lambda e: e.dma_start(out=ident, in_=c_ident), writes=[r_const])
            P.dma("sp", lambda e: e.dma_start(out=Jm, in_=c_J), writes=[r_const])
            P.dma("sp", lambda e: e.dma_start(out=invcnt, in_=c_invcnt), writes=[r_const])
            P.dma("sp", lambda e: e.dma_start(out=gv, in_=gvec), writes=[r_const])
            P.dma("sp", lambda e: e.dma_start(out=cv, in_=cvec), writes=[r_const])
            P.dma("sp", lambda e: e.dma_start(out=cw, in_=cwdw), writes=[r_const])
            P.dma("sp", lambda e: e.dma_start(out=selt, in_=sel_in), writes=[r_const])
            P.dve(lambda e: e.memset(ones_f, 1.0), writes=[r_const])
            P.dve(lambda e: e.memset(ones_b, 1.0), writes=[r_const])
            P.dve(lambda e: e.memset(ohh, 0.0), writes=[r_const])
            P.dve(lambda e: e.memset(ohh[:, 0, 0:64], 1.0), writes=[r_const])
            P.dve(lambda e: e.memset(ohh[:, 1, 64:128], 1.0), writes=[r_const])
            P.dve(lambda e: e.tensor_copy(out=ident_b, in_=ident), reads=[r_const], writes=[r_const])
            P.dve(lambda e: e.memset(rmask, 1.0), writes=[r_const])
            P.dve(lambda e: e.memset(rmask.rearrange("p (j t) -> p j t", t=128)[:, :, 0:1], 0.0), writes=[r_const])
            P.barrier()
            A.reset()
            xt = [A.alloc([4, 2048], F32) for _ in range(2)]
            hst = [A.alloc([16, 512], F32) for _ in range(2)]
            xv = x.rearrange("(g a p) f -> g p a f", a=4, p=128)
            for tg in range(4):
                b = tg % 2
                P.dma("sp", lambda e, b=b, tg=tg: e.dma_start(out=xt[b], in_=xv[tg]), writes=[R("xt", b)])
                for fc in range(16):
                    pb = fc % 4
                    for a in range(4):
                        P.pe(lambda e, b=b, a=a, fc=fc, pb=pb: e.transpose(out=PS(pb)[:, a * 128:(a + 1) * 128],
                                                                        in_=xt[b][:, a, fc * 128:(fc + 1) * 128], identity=ident),
                             reads=[R("xt", b), r_const], writes=[RP(pb)])
                    alt.copy(hst[b][:, fc, :], PS(pb), [RP(pb)], [R("hst", b)])
                for q in range(4):
                    P.dma("sp", lambda e, b=b, tg=tg, q=q: e.dma_start(out=hxv(q, tg // 2)[:, :, (tg % 2) * 512:(tg % 2) * 512 + 512], in_=hst[b][:, 4 * q:4 * q + 4, :]),
                          reads=[R("hst", b)], writes=[R("hx", q)])
            P.barrier()
            A.reset()
            memt = A.alloc([2, 2048], F32)
            gmem = A.alloc([2048], F32)
            tmp = A.alloc([2048], F32)
            memn = A.alloc([2, 2048], F32)
            ss = A.alloc([4], F32)
            r_m = R("memtmp")
            P.dma("sp", lambda e: e.dma_start(out=memt, in_=mem.rearrange("(a p) f -> p a f", p=128)), writes=[r_m])
            P.dma("sp", lambda e: e.dma_start(out=gmem, in_=mem_norm_g.partition_broadcast(128)), writes=[r_m])
            for a in range(2):
                P.act(lambda e, a=a: e.activation(out=tmp, in_=memt[:, a, :], func=AF.Square), reads=[r_m], writes=[r_m])
                P.dve(lambda e, a=a: e.reduce_sum(out=ss[:, a:a + 1], in_=tmp, axis=AX.X), reads=[r_m], writes=[r_m])
                P.dve(lambda e, a=a: e.tensor_scalar(out=ss[:, a:a + 1], in0=ss[:, a:a + 1], scalar1=1.0 / D, scalar2=EPS,
                                                     op0=ALU.mult, op1=ALU.add), reads=[r_m], writes=[r_m])
                P.act(lambda e, a=a: e.activation(out=ss[:, a:a + 1], in_=ss[:, a:a + 1], func=AF.Sqrt), reads=[r_m], writes=[r_m])
                P.dve(lambda e, a=a: e.reciprocal(out=ss[:, a:a + 1], in_=ss[:, a:a + 1]), reads=[r_m], writes=[r_m])
                P.dve(lambda e, a=a: e.scalar_tensor_tensor(out=memn[:, a, :], in0=memt[:, a, :], scalar=ss[:, a:a + 1], in1=gmem,
                                                            op0=ALU.mult, op1=ALU.mult), reads=[r_m], writes=[r_m])
            for c in range(16):
                pb = c % 4
                for a in range(2):
                    P.pe(lambda e, a=a, c=c, pb=pb: e.transpose(out=PS(pb)[:, a * 128:(a + 1) * 128],
                                                              in_=memn[:, a, c * 128:(c + 1) * 128], identity=ident),
                         reads=[r_m, r_const], writes=[RP(pb)])
                alt.copy(memnT[:, c, :], PS(pb)[:, 0:256], [RP(pb)], [R("memnT")])
            P.barrier()
            A.reset()
            rb = A.alloc([8], F32)
            oh = A.alloc([NEMP], F32)
            mt = A.alloc([NEMP], F32)
            emt = A.alloc([NEMP], F32)
            r_e = R("emtmp")
            P.dma("sp", lambda e: e.dma_start(out=rb[0:32, 0:4], in_=rel_bias), writes=[r_e])
            P.dma("sp", lambda e: e.dma_start(out=oh[0:32, :], in_=c_oh), writes=[r_e])
            P.dma("sp", lambda e: e.dma_start(out=mt[0:4, :], in_=c_mult[0:4, :]), writes=[r_e])
            for i in range(16):
                n = min(512, NEMP - i * 512)
                pb = i % 4
                P.pe(lambda e, i=i, n=n, pb=pb: e.matmul(PS(pb)[0:4, 0:n], rb[0:32, 0:4], oh[0:32, i * 512:i * 512 + n], start=True, stop=True),
                     reads=[r_e], writes=[RP(pb)])
                P.act(lambda e, i=i, n=n, pb=pb: e.activation(out=emt[0:4, i * 512:i * 512 + n], in_=PS(pb)[0:4, 0:n], func=AF.Exp),
                      reads=[RP(pb)], writes=[R("emt", i)])
                P.dve(lambda e, i=i, n=n: e.tensor_tensor(out=emt[0:4, i * 512:i * 512 + n], in0=emt[0:4, i * 512:i * 512 + n],
                                                          in1=mt[0:4, i * 512:i * 512 + n], op=ALU.mult),
                      reads=[R("emt", i), r_e], writes=[R("emt", i)])
            P.dma("sp", lambda e: e.dma_start(out=emtab_d, in_=emt[0:4, :]), reads=[R("emt", i) for i in range(16)], writes=[R("emtab_d")])
            hk = [A.alloc([31, 128], F32) for _ in range(2)]
            emh = [A.alloc([3968], BF16) for _ in range(2)]
            for h in range(4):
                b = h % 2
                src = bass.AP(emtab_d.tensor, h * NEMP, [[1, 128], [255, 31], [1, 128]])
                P.dma("sp", lambda e, b=b, src=src: e.dma_start(out=hk[b], in_=src), reads=[R("emtab_d")], writes=[R("hk", b)])
                hkf = hk[b].rearrange("p a b -> p (a b)")
                for q in range(8):
                    n = min(512, 3968 - q * 512)
                    pb = 4 + q % 4
                    P.pe(lambda e, q=q, n=n, pb=pb, hkf=hkf: e.matmul(PS(pb)[:, 0:n], Jm, hkf[:, q * 512:q * 512 + n], start=True, stop=True),
                         reads=[R("hk", b), r_const], writes=[RP(pb)])
                    alt.copy(emh[b][:, q * 512:q * 512 + n], PS(pb)[:, 0:n], [RP(pb)], [R("emh", b)])
                P.dma("sp", lambda e, b=b, h=h: e.dma_start(out=em_d[h], in_=emh[b]), reads=[R("emh", b)], writes=[R("em_d", h)])
            P.barrier()

        def rstd_from_sum(pb, out, tmp, nfeat, nn=512):
            P.dve(lambda e: e.tensor_scalar(out=tmp, in0=PS(pb)[:, 0:nn], scalar1=1.0 / nfeat, scalar2=EPS, op0=ALU.mult, op1=ALU.add),
                  reads=[RP(pb)], writes=[R(id(tmp))])
            P.act(lambda e: e.activation(out=tmp, in_=tmp, func=AF.Sqrt), reads=[R(id(tmp))], writes=[R(id(tmp))])
            P.dve(lambda e: e.reciprocal(out=out, in_=tmp), reads=[R(id(tmp))], writes=[R(id(out))])

        def stage_A(l):
            A.reset()
            xnT = A.alloc([16, 2048], BF16)
            mA = A.mark()
            hin = [A.alloc([16, 512], F32) for _ in range(2)]
            sqt = [A.alloc([512], F32) for _ in range(3)]
            tmpv = A.alloc([512], F32)
            rstd = A.alloc([512], F32)
            r_xn = [R("xnT", t) for t in range(4)]
            for tt in range(4):
                b = tt % 2
                for q in range(4):
                    P.dma("sp", lambda e, b=b, tt=tt, q=q: e.dma_start(out=hin[b][:, 4 * q:4 * q + 4, :], in_=hxv(q, tt // 2)[:, :, (tt % 2) * 512:(tt % 2) * 512 + 512]),
                          reads=[R("hx", q)], writes=[R("hin", b)])
                for c in range(16):
                    s = c % 3
                    P.act(lambda e, b=b, c=c, s=s: e.activation(out=sqt[s], in_=hin[b][:, c, :], func=AF.Square),
                          reads=[R("hin", b)], writes=[R("sqt", s)])
                    P.pe(lambda e, c=c, s=s: e.matmul(PS(7), ones_f, sqt[s], start=(c == 0), stop=(c == 15)),
                         reads=[R("sqt", s), r_const], writes=[RP(7)])
                rstd_from_sum(7, rstd, tmpv, D)
                for c in range(16):
                    P.dve(lambda e, b=b, c=c, tt=tt: e.scalar_tensor_tensor(out=xnT[:, c, tt * 512:(tt + 1) * 512], in0=hin[b][:, c, :],
                                                                           scalar=gcol(0, l, c), in1=rstd, op0=ALU.mult, op1=ALU.mult),
                          reads=[R("hin", b), R(id(rstd)), r_const], writes=[r_xn[tt]])
            P.barrier()
            A.reset(mA)
            wsl = [A.alloc([16, 512], BF16) for _ in range(2)]
            st_b = [A.alloc([4, 512], BF16) for _ in range(2)]
            st_f = [A.alloc([4, 512], F32) for _ in range(2)]
            qst = [A.alloc([2, 512], BF16) for _ in range(2)]
            vst = [A.alloc([4, 128], BF16) for _ in range(2)]
            sgt = [A.alloc([512], F32) for _ in range(2)]
            wv = w_in[l].rearrange("(k p) n -> p k n", p=128)
            blocks = ["ua", "ub", "vg0", "vg1", "qk", "v"]
            for bi, bn in enumerate(blocks):
                s = bi % 2
                if bn in ("vg0", "vg1"):
                    o = 0 if bn == "vg0" else 256
                    P.dma(wq, lambda e, s=s, o=o: e.dma_start(out=wsl[s][:, :, 0:256], in_=wv[:, :, 1024 + o:1024 + o + 256]), writes=[R("wsl", s)])
                    P.dma(wq, lambda e, s=s, o=o: e.dma_start(out=wsl[s][:, :, 256:512], in_=wv[:, :, 1536 + o:1536 + o + 256]), writes=[R("wsl", s)])
                elif bn == "ua":
                    P.dma(wq, lambda e, s=s: e.dma_start(out=wsl[s][:, :, 0:256], in_=w_ua[l].rearrange("(k p) n -> p k n", p=128)), writes=[R("wsl", s)])
                elif bn == "qk":
                    P.dma(wq, lambda e, s=s: e.dma_start(out=wsl[s], in_=w_qkv[l].rearrange("(k p) n -> p k n", p=128)[:, :, 0:512]), writes=[R("wsl", s)])
                elif bn == "v":
                    P.dma(wq, lambda e, s=s: e.dma_start(out=wsl[s][:, :, 0:256], in_=w_qkv[l].rearrange("(k p) n -> p k n", p=128)[:, :, 512:768]), writes=[R("wsl", s)])
                else:
                    c0 = {"ub": 512}[bn]
                    P.dma(wq, lambda e, s=s, c0=c0: e.dma_start(out=wsl[s], in_=wv[:, :, c0:c0 + 512]), writes=[R("wsl", s)])
                if bn == "v":
                    for rp in range(16):
                        pb = rp % 4
                        sb_ = rp % 2
                        for k in range(16):
                            lhs = xnT[:, k, :].rearrange("p (s r) -> p r s", r=16)[:, rp, :]
                            P.pe(lambda e, k=k, pb=pb, lhs=lhs, s=s: e.matmul(PS(pb)[:, 0:256], lhs, wsl[s][:, k, 0:256], start=(k == 0), stop=(k == 15)),
                                 reads=r_xn + [R("wsl", s)], writes=[RP(pb)])
                        if rp < 2:
                            P.dve(lambda e, sb_=sb_: e.memset(vst[sb_], 0.0), writes=[R("vst", sb_)])
                        psv = PS(pb)[:, 0:256].rearrange("p (c e d) -> p c e d", e=2, d=64)
                        vv = vst[sb_].rearrange("p (c e) d -> p c e d", e=2)
                        P.act(lambda e, psv=psv, vv=vv: e.activation(out=vv[:, :, 0, 0:64], in_=psv[:, :, 0, :], func=AF.Copy),
                              reads=[RP(pb)], writes=[R("vst", sb_)])
                        P.dve(lambda e, psv=psv, vv=vv: e.tensor_copy(out=vv[:, :, 1, 64:128], in_=psv[:, :, 1, :]),
                              reads=[RP(pb)], writes=[R("vst", sb_)])
                        P.dma("sp", lambda e, sb_=sb_, rp=rp: e.dma_start(out=vz_d[rp], in_=vst[sb_].rearrange("p h d -> p (h d)")),
                              reads=[R("vst", sb_)], writes=[R("vz_d", rp)])
                    continue
                for tt in range(4):
                    sb_ = tt % 2
                    perm = bn == "qk"
                    for m in range(2 if bn == "ua" else 4):
                        pb = (tt * 4 + m) % 6
                        for k in range(16):
                            if perm:
                                rhs = xnT[:, k, :].rearrange("p (s r) -> p r s", r=16)[:, 4 * tt:4 * tt + 4, :]
                            else:
                                rhs = xnT[:, k, tt * 512:(tt + 1) * 512]
                            P.pe(lambda e, k=k, pb=pb, rhs=rhs, s=s, m=m: e.matmul(PS(pb), wsl[s][:, k, m * 128:(m + 1) * 128], rhs,
                                                                                start=(k == 0), stop=(k == 15)),
                                 reads=r_xn + [R("wsl", s)], writes=[RP(pb)])
                        if bn == "ua":
                            alt.copy(st_b[sb_][:, m, :], PS(pb), [RP(pb)], [R("st_b", sb_)])
                        elif bn == "ub":
                            alt.copy(st_f[sb_][:, m, :], PS(pb), [RP(pb)], [R("st_f", sb_)])
                        elif bn == "qk" and m >= 2:
                            alt.copy(st_b[sb_][:, m - 2, :], PS(pb), [RP(pb)], [R("st_b", sb_)])
                        elif bn == "qk":
                            qb = (tt * 4 + m) % 2
                            if tt * 4 + m < 2:
                                P.dve(lambda e, qb=qb: e.memset(qst[qb], 0.0), writes=[R("qst", qb)])
                            P.act(lambda e, qb=qb, pb=pb: e.activation(out=qst[qb][0:64, 0, :], in_=PS(pb)[0:64, :], func=AF.Copy),
                                  reads=[RP(pb)], writes=[R("qst", qb)])
                            P.dve(lambda e, qb=qb, pb=pb: e.tensor_copy(out=qst[qb][64:128, 1, :], in_=PS(pb)[64:128, :]),
                                  reads=[RP(pb)], writes=[R("qst", qb)])
                            dst = qz_d.rearrange("h p t -> p h t")[:, 2 * m:2 * m + 2, tt * 512:(tt + 1) * 512]
                            P.dma("sp", lambda e, qb=qb, dst=dst: e.dma_start(out=dst, in_=qst[qb]), reads=[R("qst", qb)], writes=[R("qz_d", m, tt)])
                        else:
                            pass
                    if bn in ("vg0", "vg1"):
                        for mm in range(2):
                            pv = (tt * 4 + mm) % 6
                            pg = (tt * 4 + mm + 2) % 6
                            sg = mm
                            P.act(lambda e, pg=pg, sg=sg: e.activation(out=sgt[sg], in_=PS(pg), func=AF.Sigmoid), reads=[RP(pg)], writes=[R("sgt", sg)])
                            P.dve(lambda e, pv=pv, sg=sg, sb_=sb_, mm=mm: e.tensor_tensor(out=st_b[sb_][:, mm, :], in0=PS(pv), in1=sgt[sg], op=ALU.mult),
                                  reads=[RP(pv), R("sgt", sg)], writes=[R("st_b", sb_)])
                        c0 = 0 if bn == "vg0" else 2
                        dst = hgT_d.rearrange("(c p) t -> p c t", p=128)[:, c0:c0 + 2, tt * 512:(tt + 1) * 512]
                        P.dma("sp", lambda e, sb_=sb_, dst=dst: e.dma_start(out=dst, in_=st_b[sb_][:, 0:2, :]), reads=[R("st_b", sb_)], writes=[R("hgT_d", c0, tt)])
                    elif bn == "ua":
                        dst = uaT_d.rearrange("(c p) t -> p c t", p=128)[:, :, tt * 512:(tt + 1) * 512]
                        P.dma("sp", lambda e, sb_=sb_, dst=dst: e.dma_start(out=dst, in_=st_b[sb_][:, 0:2, :]), reads=[R("st_b", sb_)], writes=[R("uaT_d", tt)])
                    elif bn == "ub":
                        dst = ubT_d.rearrange("(c p) t -> p c t", p=128)[:, :, tt * 512:(tt + 1) * 512]
                        P.dma("sp", lambda e, sb_=sb_, dst=dst: e.dma_start(out=dst, in_=st_f[sb_]), reads=[R("st_f", sb_)], writes=[R("ubT_d", tt)])
                    elif bn == "qk":
                        dst = kT_d.rearrange("c p t -> p c t")[:, :, tt * 512:(tt + 1) * 512]
                        P.dma("sp", lambda e, sb_=sb_, dst=dst: e.dma_start(out=dst, in_=st_b[sb_][:, 0:2, :]), reads=[R("st_b", sb_)], writes=[R("kT_d", tt)])
            P.barrier()

        def group_norm_store(l, mix, yt, r_y, sqt, tmpv, rstd, stb):
            dstv = ynT_d.rearrange("(c p) t -> p c t", p=128)
            for tt in range(4):
                sb_ = tt % 2
                for c in range(4):
                    s = c % 2
                    P.act(lambda e, c=c, s=s, tt=tt: e.activation(out=sqt[s], in_=yt[:, c, tt * 512:(tt + 1) * 512], func=AF.Square),
                          reads=[r_y], writes=[R("gsq", s)])
                    P.pe(lambda e, c=c, s=s: e.matmul(PS(7), ones_f, sqt[s], start=(c == 0), stop=(c == 3)),
                         reads=[R("gsq", s), r_const], writes=[RP(7)])
                rstd_from_sum(7, rstd, tmpv, 512)
                for c in range(4):
                    P.dve(lambda e, c=c, tt=tt, sb_=sb_: e.scalar_tensor_tensor(out=stb[sb_][:, c, :], in0=yt[:, c, tt * 512:(tt + 1) * 512],
                                                                               scalar=gcol(1, l, mix * 4 + c), in1=rstd, op0=ALU.mult, op1=ALU.mult),
                          reads=[r_y, R(id(rstd)), r_const], writes=[R("gstb", sb_)])
                P.dma("sp", lambda e, sb_=sb_, tt=tt: e.dma_start(out=dstv[:, mix * 4:mix * 4 + 4, tt * 512:(tt + 1) * 512], in_=stb[sb_]),
                      reads=[R("gstb", sb_)], writes=[R("ynT_d", mix, tt)])

        def horner(out, y, coefs, r_):
            n = len(coefs) - 1
            P.dve(lambda e: e.tensor_scalar(out=out, in0=y, scalar1=float(coefs[n]), scalar2=None, op0=ALU.mult), reads=[r_], writes=[r_])
            for k in range(n - 1, 0, -1):
                P.dve(lambda e, k=k: e.scalar_tensor_tensor(out=out, in0=out, scalar=float(coefs[k]), in1=y, op0=ALU.add, op1=ALU.mult), reads=[r_], writes=[r_])
            P.dve(lambda e: e.tensor_scalar(out=out, in0=out, scalar1=float(coefs[0]), scalar2=None, op0=ALU.add), reads=[r_], writes=[r_])

        FACT = [1.0]
        for _k in range(1, 14):
            FACT.append(FACT[-1] * _k)

        def exp_acc(out, x, nsq, t, r_):
            P.dve(lambda e: e.tensor_scalar(out=t, in0=x, scalar1=1.0 / (2 ** nsq), scalar2=None, op0=ALU.mult), reads=[r_], writes=[r_])
            horner(out, t, [1.0 / FACT[k] for k in range(10)], r_)
            for _ in range(nsq):
                P.dve(lambda e: e.tensor_tensor(out=out, in0=out, in1=out, op=ALU.mult), reads=[r_], writes=[r_])

        def sincos_acc(ang, co, si, t1, t2, ti, r_):
            TWO_PI = 2 * PI
            C1 = 6.28125
            C2 = TWO_PI - C1
            D_ = lambda fn: P.dve(fn, reads=[r_], writes=[r_])
            D_(lambda e: e.tensor_scalar(out=t1, in0=ang, scalar1=1.0 / TWO_PI, scalar2=None, op0=ALU.mult))
            D_(lambda e: e.tensor_copy(out=ti, in_=t1))
            D_(lambda e: e.tensor_copy(out=t1, in_=ti))
            D_(lambda e: e.scalar_tensor_tensor(out=t2, in0=t1, scalar=-C1, in1=ang, op0=ALU.mult, op1=ALU.add))
            D_(lambda e: e.scalar_tensor_tensor(out=t2, in0=t1, scalar=-C2, in1=t2, op0=ALU.mult, op1=ALU.add))
            D_(lambda e: e.tensor_scalar(out=t2, in0=t2, scalar1=0.25, scalar2=None, op0=ALU.mult))
            D_(lambda e: e.tensor_tensor(out=t1, in0=t2, in1=t2, op=ALU.mult))
            horner(si, t1, [1.0, -1.0 / FACT[3], 1.0 / FACT[5], -1.0 / FACT[7], 1.0 / FACT[9], -1.0 / FACT[11]], r_)
            D_(lambda e: e.tensor_tensor(out=si, in0=si, in1=t2, op=ALU.mult))
            horner(co, t1, [1.0, -1.0 / FACT[2], 1.0 / FACT[4], -1.0 / FACT[6], 1.0 / FACT[8], -1.0 / FACT[10], 1.0 / FACT[12]], r_)
            for _ in range(2):
                D_(lambda e: e.tensor_tensor(out=t1, in0=si, in1=si, op=ALU.mult))
                D_(lambda e: e.scalar_tensor_tensor(out=si, in0=si, scalar=2.0, in1=co, op0=ALU.mult, op1=ALU.mult))
                D_(lambda e: e.tensor_scalar(out=co, in0=t1, scalar1=-2.0, scalar2=1.0, op0=ALU.mult, op1=ALU.add))

        def s5_params(lr, li, ldt, outs, tmps, r_):
            t1, t2, t3, ti = tmps
            mag, co, si, abr, abi, fr, fi = [outs[k] for k in ("mag", "cos", "sin", "abr", "abi", "fr", "fi")]
            TT_ = lambda o, a, b, op: P.dve(lambda e: e.tensor_tensor(out=o, in0=a, in1=b, op=op), reads=[r_], writes=[r_])
            exp_acc(t2, ldt, 4, t1, r_)
            TT_(t3, lr, t2, ALU.mult)
            exp_acc(mag, t3, 0, t1, r_)
            TT_(t3, li, t2, ALU.mult)
            sincos_acc(t3, co, si, t1, fr, ti, r_)
            TT_(abr, mag, co, ALU.mult)
            TT_(abi, mag, si, ALU.mult)
            TT_(t1, lr, lr, ALU.mult)
            TT_(t2, li, li, ALU.mult)
            TT_(t1, t1, t2, ALU.add)
            P.dve(lambda e: e.reciprocal(out=t1, in_=t1), reads=[r_], writes=[r_])
            P.dve(lambda e: e.tensor_scalar(out=t2, in0=abr, scalar1=-1.0, scalar2=None, op0=ALU.add), reads=[r_], writes=[r_])
            TT_(t3, t2, lr, ALU.mult)
            TT_(fr, abi, li, ALU.mult)
            TT_(fr, fr, t3, ALU.add)
            TT_(fr, fr, t1, ALU.mult)
            TT_(t3, t2, li, ALU.mult)
            TT_(fi, abi, lr, ALU.mult)
            TT_(fi, fi, t3, ALU.subtract)
            TT_(fi, fi, t1, ALU.mult)

        def stage_S5(l):
            A.reset()
            r_p = R("s5p")
            cT = A.alloc([8, 128], F32)
            sT = A.alloc([8, 128], F32)
            BR = A.alloc([8, 128], BF16)
            BI = A.alloc([8, 128], BF16)
            CR = A.alloc([8, 128], BF16)
            CI = A.alloc([8, 128], BF16)
            sm = {k: A.alloc([8], F32) for k in ("lr", "li", "ldt", "mag", "cos", "sin", "abr", "abi", "fr", "fi", "t1", "t2", "t3",
                                                  "er", "ei", "e2r", "e2i", "rT", "q1", "q2", "ti")}
            cJ = A.alloc([8, 16], F32)
            sJ = A.alloc([8, 16], F32)
            m0 = A.mark()
            fl = {k: A.alloc([1024], F32) for k in ("fr", "fi", "t1", "t2", "t3", "b0", "b1")}
            srv = s5_sr[l]
            P.dma("sp", lambda e: e.dma_start(out=sm["lr"], in_=srv[:, 0:8]), writes=[r_p])
            P.dma("sp", lambda e: e.dma_start(out=sm["li"], in_=srv[:, 8:16]), writes=[r_p])
            P.dma("sp", lambda e: e.dma_start(out=sm["ldt"], in_=srv[:, 16:24]), writes=[r_p])
            s5_params(sm["lr"], sm["li"], sm["ldt"], sm, (sm["t1"], sm["t2"], sm["t3"], sm["ti"].bitcast(mybir.dt.int32)), r_p)
            TTp = lambda o, a, b, op: P.dve(lambda e: e.tensor_tensor(out=o, in0=a, in1=b, op=op), reads=[r_p], writes=[r_p])
            P.dve(lambda e: e.memset(cT[:, :, 0:1], 1.0), writes=[r_p])
            P.dve(lambda e: e.memset(sT[:, :, 0:1], 0.0), writes=[r_p])
            P.dve(lambda e: e.tensor_copy(out=sm["er"], in_=sm["cos"]), reads=[r_p], writes=[r_p])
            P.dve(lambda e: e.tensor_copy(out=sm["ei"], in_=sm["sin"]), reads=[r_p], writes=[r_p])
            tA = fl["t1"].rearrange("p (a b) -> p a b", a=8)
            tB = fl["t2"].rearrange("p (a b) -> p a b", a=8)

            def cplx_extend(cTab, sTab, n, er, ei, width):
                erb = er.unsqueeze(2).broadcast_to([128, 8, n])
                eib = ei.unsqueeze(2).broadcast_to([128, 8, n])
                TTp(tA[:, :, 0:n], cTab[:, :, 0:n], erb, ALU.mult)
                TTp(tB[:, :, 0:n], sTab[:, :, 0:n], eib, ALU.mult)
                TTp(cTab[:, :, n:2 * n], tA[:, :, 0:n], tB[:, :, 0:n], ALU.subtract)
                TTp(tA[:, :, 0:n], cTab[:, :, 0:n], eib, ALU.mult)
                TTp(tB[:, :, 0:n], sTab[:, :, 0:n], erb, ALU.mult)
                TTp(sTab[:, :, n:2 * n], tA[:, :, 0:n], tB[:, :, 0:n], ALU.add)

            def cplx_square(er, ei, q1, q2):
                TTp(q1, er, er, ALU.mult)
                TTp(q2, ei, ei, ALU.mult)
                TTp(q2, q1, q2, ALU.subtract)
                TTp(q1, er, ei, ALU.mult)
                P.dve(lambda e: e.tensor_scalar(out=ei, in0=q1, scalar1=2.0, scalar2=None, op0=ALU.mult), reads=[r_p], writes=[r_p])
                P.dve(lambda e: e.tensor_copy(out=er, in_=q2), reads=[r_p], writes=[r_p])

            n = 1
            P.dve(lambda e: e.tensor_copy(out=sm["rT"], in_=sm["mag"]), reads=[r_p], writes=[r_p])
            while n < 128:
                cplx_extend(cT, sT, n, sm["er"], sm["ei"], 128)
                cplx_square(sm["er"], sm["ei"], sm["q1"], sm["q2"])
                TTp(sm["rT"], sm["rT"], sm["rT"], ALU.mult)
                n *= 2
            P.dve(lambda e: e.memset(cJ[:, :, 0:1], 1.0), writes=[r_p])
            P.dve(lambda e: e.memset(sJ[:, :, 0:1], 0.0), writes=[r_p])
            n = 1
            while n < 16:
                cplx_extend(cJ, sJ, n, sm["er"], sm["ei"], 16)
                cplx_square(sm["er"], sm["ei"], sm["q1"], sm["q2"])
                n *= 2
            Dg = fl["t3"].rearrange("p (a b) -> p a b", a=8)
            for key, b0_ in (("fr", 0), ("fi", 4)):
                TTp(Dg, ident.unsqueeze(1).broadcast_to([128, 8, 128]), sm[key].unsqueeze(2).broadcast_to([128, 8, 128]), ALU.mult)
                for q in range(2):
                    P.pe(lambda e, q=q, b0_=b0_: e.matmul(PS(b0_ + q), ones_f, fl["t3"][:, q * 512:(q + 1) * 512], start=True, stop=True),
                         reads=[r_p, r_const], writes=[RP(b0_ + q)])
                    P.act(lambda e, q=q, b0_=b0_, key=key: e.activation(out=fl[key][:, q * 512:(q + 1) * 512], in_=PS(b0_ + q), func=AF.Copy),
                          reads=[RP(b0_ + q)], writes=[r_p])
            P.dma("sp", lambda e: e.dma_start(out=fl["b0"], in_=s5_bT[l, 0]), writes=[r_p])
            P.dma("sp", lambda e: e.dma_start(out=fl["b1"], in_=s5_bT[l, 1]), writes=[r_p])
            BRf = BR.rearrange("p a b -> p (a b)")
            BIf = BI.rearrange("p a b -> p (a b)")
            TTp(fl["t1"], fl["b0"], fl["fr"], ALU.mult)
            TTp(fl["t2"], fl["b1"], fl["fi"], ALU.mult)
            TTp(BRf, fl["t1"], fl["t2"], ALU.subtract)
            TTp(fl["t1"], fl["b0"], fl["fi"], ALU.mult)
            TTp(fl["t2"], fl["b1"], fl["fr"], ALU.mult)
            TTp(BIf, fl["t1"], fl["t2"], ALU.add)
            P.dma("sp", lambda e: e.dma_start(out=fl["b0"], in_=s5_cT[l, 0]), reads=[r_p], writes=[r_p])
            P.dma("sp", lambda e: e.dma_start(out=fl["b1"], in_=s5_cT[l, 1]), reads=[r_p], writes=[r_p])
            P.dve(lambda e: e.tensor_copy(out=CR.rearrange("p a b -> p (a b)"), in_=fl["b0"]), reads=[r_p], writes=[r_p])
            P.dve(lambda e: e.tensor_copy(out=CI.rearrange("p a b -> p (a b)"), in_=fl["b1"]), reads=[r_p], writes=[r_p])
            P.barrier()
            A.reset(m0)
            uaT = A.alloc([2, 2048], BF16)
            gf = A.alloc([4, 2048], F32)
            gb = A.alloc([4, 2048], BF16)
            sq2 = [A.alloc([512], F32) for _ in range(2)]
            xR = [A.alloc([2048], BF16) for _ in range(2)]
            xI = [A.alloc([2048], BF16) for _ in range(2)]
            zz = {k: A.alloc([16], F32) for k in ("ZR", "ZI", "WR", "WI", "VR", "VI", "XR", "XI", "a1", "a2", "rTt")}
            mW = A.mark()
            wR = A.alloc([2048], F32)
            wI = A.alloc([2048], F32)
            vR = A.alloc([2048], F32)
            vI = A.alloc([2048], F32)
            t1 = A.alloc([2048], F32)
            t2 = A.alloc([2048], F32)
            rmul = A.alloc([2048], F32)
            r_m = R("s5m")
            P.dma("sp", lambda e: e.dma_start(out=uaT, in_=uaT_d.rearrange("(c p) t -> p c t", p=128)),
                  reads=[R("uaT_d", t) for t in range(4)], writes=[R("uaT")])
            DV = lambda fn, rd, wr: P.dve(fn, reads=rd, writes=wr)
            v3 = lambda ap: ap.rearrange("p (j t) -> p j t", t=128)
            for gp in range(8):
                cc = gp // 4
                xb = gp % 2
                ctb = cT[:, gp, :].unsqueeze(1).broadcast_to([128, 8, 128])
                stb_ = sT[:, gp, :].unsqueeze(1).broadcast_to([128, 8, 128])
                for half in range(2):
                    for tq in range(2):
                        tok = slice(half * 1024 + tq * 512, half * 1024 + tq * 512 + 512)
                        P.pe(lambda e, gp=gp, cc=cc, tq=tq, tok=tok: e.matmul(PS(tq), BR[:, gp, :], uaT[:, cc, tok], start=True, stop=True),
                             reads=[R("uaT"), r_p], writes=[RP(tq)])
                        P.pe(lambda e, gp=gp, cc=cc, tq=tq, tok=tok: e.matmul(PS(2 + tq), BI[:, gp, :], uaT[:, cc, tok], start=True, stop=True),
                             reads=[R("uaT"), r_p], writes=[RP(2 + tq)])
                    for tq in range(2):
                        buR = PS(tq).rearrange("p (j t) -> p j t", t=128)
                        buI = PS(2 + tq).rearrange("p (j t) -> p j t", t=128)
                        ctb4 = cT[:, gp, :].unsqueeze(1).broadcast_to([128, 4, 128])
                        stb4 = sT[:, gp, :].unsqueeze(1).broadcast_to([128, 4, 128])
                        hs = slice(half * 1024 + tq * 512, half * 1024 + tq * 512 + 512)
                        rdp = [RP(tq), RP(2 + tq), r_p]
                        DV(lambda e, buR=buR, ctb4=ctb4, hs=hs: e.tensor_tensor(out=v3(t1[:, hs]), in0=buR, in1=ctb4, op=ALU.mult), rdp, [R("t1")])
                        DV(lambda e, buI=buI, stb4=stb4, hs=hs: e.tensor_tensor(out=v3(t2[:, hs]), in0=buI, in1=stb4, op=ALU.mult), rdp, [R("t2")])
                        DV(lambda e, hs=hs: e.tensor_tensor(out=wR[:, hs], in0=t1[:, hs], in1=t2[:, hs], op=ALU.add), [R("t1"), R("t2")], [R("wR")])
                        DV(lambda e, buI=buI, ctb4=ctb4, hs=hs: e.tensor_tensor(out=v3(t1[:, hs]), in0=buI, in1=ctb4, op=ALU.mult), rdp, [R("t1")])
                        DV(lambda e, buR=buR, stb4=stb4, hs=hs: e.tensor_tensor(out=v3(t2[:, hs]), in0=buR, in1=stb4, op=ALU.mult), rdp, [R("t2")])
                        DV(lambda e, hs=hs: e.tensor_tensor(out=wI[:, hs], in0=t1[:, hs], in1=t2[:, hs], op=ALU.subtract), [R("t1"), R("t2")], [R("wI")])
                DV(lambda e, gp=gp: e.tensor_scalar(out=rmul, in0=rmask, scalar1=sm["mag"][:, gp:gp + 1], scalar2=None, op0=ALU.mult),
                   [r_const, r_p], [R("rmul")])
                DV(lambda e: e.tensor_tensor_scan(out=vR, data0=rmul, data1=wR, initial=0.0, op0=ALU.mult, op1=ALU.add), [R("rmul"), R("wR")], [R("vR")])
                DV(lambda e: e.tensor_tensor_scan(out=vI, data0=rmul, data1=wI, initial=0.0, op0=ALU.mult, op1=ALU.add), [R("rmul"), R("wI")], [R("vI")])
                vRl = v3(vR)[:, :, 127]
                vIl = v3(vI)[:, :, 127]
                c127 = cT[:, gp, 127:128]
                s127 = sT[:, gp, 127:128]
                rz = R("zz")
                DV(lambda e, vIl=vIl, s127=s127: e.tensor_scalar(out=zz["a1"], in0=vIl, scalar1=s127, scalar2=None, op0=ALU.mult), [R("vI"), r_p], [rz])
                DV(lambda e, vRl=vRl, c127=c127: e.scalar_tensor_tensor(out=zz["ZR"], in0=vRl, scalar=c127, in1=zz["a1"], op0=ALU.mult, op1=ALU.subtract),
                   [R("vR"), r_p, rz], [rz])
                DV(lambda e, vIl=vIl, c127=c127: e.tensor_scalar(out=zz["a1"], in0=vIl, scalar1=c127, scalar2=None, op0=ALU.mult), [R("vI"), r_p, rz], [rz])
                DV(lambda e, vRl=vRl, s127=s127: e.scalar_tensor_tensor(out=zz["ZI"], in0=vRl, scalar=s127, in1=zz["a1"], op0=ALU.mult, op1=ALU.add),
                   [R("vR"), r_p, rz], [rz])
                cj = cJ[:, gp, :]
                sj = sJ[:, gp, :]
                Z = lambda fn: DV(fn, [rz, r_p], [rz])
                Z(lambda e, cj=cj, sj=sj: e.tensor_tensor(out=zz["a1"], in0=zz["ZR"], in1=cj, op=ALU.mult))
                Z(lambda e, cj=cj, sj=sj: e.tensor_tensor(out=zz["a2"], in0=zz["ZI"], in1=sj, op=ALU.mult))
                Z(lambda e, cj=cj, sj=sj: e.tensor_tensor(out=zz["WR"], in0=zz["a1"], in1=zz["a2"], op=ALU.add))
                Z(lambda e, cj=cj, sj=sj: e.tensor_tensor(out=zz["a1"], in0=zz["ZI"], in1=cj, op=ALU.mult))
                Z(lambda e, cj=cj, sj=sj: e.tensor_tensor(out=zz["a2"], in0=zz["ZR"], in1=sj, op=ALU.mult))
                Z(lambda e, cj=cj, sj=sj: e.tensor_tensor(out=zz["WI"], in0=zz["a1"], in1=zz["a2"], op=ALU.subtract))
                Z(lambda e, gp=gp: e.tensor_scalar(out=zz["rTt"], in0=rmask[:, 1:17], scalar1=sm["rT"][:, gp:gp + 1], scalar2=None, op0=ALU.mult))
                Z(lambda e: e.tensor_tensor_scan(out=zz["VR"], data0=zz["rTt"], data1=zz["WR"], initial=0.0, op0=ALU.mult, op1=ALU.add))
                Z(lambda e: e.tensor_tensor_scan(out=zz["VI"], data0=zz["rTt"], data1=zz["WI"], initial=0.0, op0=ALU.mult, op1=ALU.add))
                Z(lambda e, cj=cj, sj=sj: e.tensor_tensor(out=zz["a1"], in0=zz["VR"], in1=cj, op=ALU.mult))
                Z(lambda e, cj=cj, sj=sj: e.tensor_tensor(out=zz["a2"], in0=zz["VI"], in1=sj, op=ALU.mult))
                Z(lambda e, cj=cj, sj=sj: e.tensor_tensor(out=zz["XR"], in0=zz["a1"], in1=zz["a2"], op=ALU.subtract))
                Z(lambda e, cj=cj, sj=sj: e.tensor_tensor(out=zz["a1"], in0=zz["VR"], in1=sj, op=ALU.mult))
                Z(lambda e, cj=cj, sj=sj: e.tensor_tensor(out=zz["a2"], in0=zz["VI"], in1=cj, op=ALU.mult))
                Z(lambda e, cj=cj, sj=sj: e.tensor_tensor(out=zz["XI"], in0=zz["a1"], in1=zz["a2"], op=ALU.add))
                if debug and gp == 0 and on("dbgS5"):
                    for i_, k_ in enumerate(("ZR", "ZI", "WR", "WI", "VR", "VI", "XR", "XI", "rTt")):
                        P.dma("sp", lambda e, i_=i_, k_=k_: e.dma_start(out=dbg_d[1024:1152, i_ * 16:(i_ + 1) * 16], in_=zz[k_]), reads=[rz], writes=[R("dbg2")])
                    P.dma("sp", lambda e: e.dma_start(out=dbg_d[1024:1152, 160:176], in_=cJ[:, 0, :]), reads=[r_p], writes=[R("dbg2")])
                    P.dma("sp", lambda e: e.dma_start(out=dbg_d[1024:1152, 176:192], in_=sJ[:, 0, :]), reads=[r_p], writes=[R("dbg2")])
                    P.dma("sp", lambda e: e.dma_start(out=dbg_d[1024:1152, 192:208], in_=sm["rT"]), reads=[r_p], writes=[R("dbg2")])
                    P.dma("sp", lambda e: e.dma_start(out=dbg_d[1024:1152, 208:224], in_=sm["mag"]), reads=[r_p], writes=[R("dbg2")])
                    P.dma("sp", lambda e: e.dma_start(out=dbg_d[1024:1152, 224:240], in_=sm["cos"]), reads=[r_p], writes=[R("dbg2")])
                    P.dma("sp", lambda e: e.dma_start(out=dbg_d[1024:1152, 240:256], in_=sm["sin"]), reads=[r_p], writes=[R("dbg2")])
                    P.dma("sp", lambda e: e.dma_start(out=dbg_d[1152:1280, 0:128], in_=cT[:, 0, :]), reads=[r_p], writes=[R("dbg2")])
                    P.dma("sp", lambda e: e.dma_start(out=dbg_d[1152:1280, 128:256], in_=sT[:, 0, :]), reads=[r_p], writes=[R("dbg2")])
                abr = sm["abr"][:, gp:gp + 1]
                abi = sm["abi"][:, gp:gp + 1]
                Z(lambda e, abi=abi: e.tensor_scalar(out=zz["a1"], in0=zz["XI"], scalar1=abi, scalar2=None, op0=ALU.mult))
                Z(lambda e, abr=abr: e.scalar_tensor_tensor(out=zz["a2"], in0=zz["XR"], scalar=abr, in1=zz["a1"], op0=ALU.mult, op1=ALU.subtract))
                DV(lambda e: e.tensor_tensor(out=v3(wR)[:, 1:16, 0], in0=v3(wR)[:, 1:16, 0], in1=zz["a2"][:, 0:15], op=ALU.add), [rz, R("wR")], [R("wR")])
                Z(lambda e, abr=abr: e.tensor_scalar(out=zz["a1"], in0=zz["XI"], scalar1=abr, scalar2=None, op0=ALU.mult))
                Z(lambda e, abi=abi: e.scalar_tensor_tensor(out=zz["a2"], in0=zz["XR"], scalar=abi, in1=zz["a1"], op0=ALU.mult, op1=ALU.add))
                DV(lambda e: e.tensor_tensor(out=v3(wI)[:, 1:16, 0], in0=v3(wI)[:, 1:16, 0], in1=zz["a2"][:, 0:15], op=ALU.add), [rz, R("wI")], [R("wI")])
                DV(lambda e: e.tensor_tensor_scan(out=vR, data0=rmul, data1=wR, initial=0.0, op0=ALU.mult, op1=ALU.add), [R("rmul"), R("wR")], [R("vR")])
                DV(lambda e: e.tensor_tensor_scan(out=vI, data0=rmul, data1=wI, initial=0.0, op0=ALU.mult, op1=ALU.add), [R("rmul"), R("wI")], [R("vI")])
                ct16 = cT[:, gp, :].unsqueeze(1).broadcast_to([128, 16, 128])
                st16 = sT[:, gp, :].unsqueeze(1).broadcast_to([128, 16, 128])
                DV(lambda e, ct16=ct16: e.tensor_tensor(out=v3(t1), in0=v3(vR), in1=ct16, op=ALU.mult), [R("vR"), r_p], [R("t1")])
                DV(lambda e, st16=st16: e.tensor_tensor(out=v3(t2), in0=v3(vI), in1=st16, op=ALU.mult), [R("vI"), r_p], [R("t2")])
                DV(lambda e, xb=xb: e.tensor_tensor(out=xR[xb], in0=t1, in1=t2, op=ALU.subtract), [R("t1"), R("t2")], [R("xR", xb)])
                DV(lambda e, st16=st16: e.tensor_tensor(out=v3(t1), in0=v3(vR), in1=st16, op=ALU.mult), [R("vR"), r_p], [R("t1")])
                DV(lambda e, ct16=ct16: e.tensor_tensor(out=v3(t2), in0=v3(vI), in1=ct16, op=ALU.mult), [R("vI"), r_p], [R("t2")])
                DV(lambda e, xb=xb: e.scalar_tensor_tensor(out=xI[xb], in0=t1, scalar=-1.0, in1=t2, op0=ALU.mult, op1=ALU.subtract), [R("t1"), R("t2")], [R("xI", xb)])
                for tt in range(4):
                    tok = slice(tt * 512, tt * 512 + 512)
                    P.pe(lambda e, gp=gp, xb=xb, tt=tt, tok=tok: e.matmul(PS(4 + tt), CR[:, gp, :], xR[xb][:, tok], start=(gp % 4 == 0), stop=False),
                         reads=[R("xR", xb), r_p], writes=[RP(4 + tt)])
                    P.pe(lambda e, gp=gp, xb=xb, tt=tt, tok=tok: e.matmul(PS(4 + tt), CI[:, gp, :], xI[xb][:, tok], start=False, stop=(gp % 4 == 3)),
                         reads=[R("xI", xb), r_p], writes=[RP(4 + tt)])
                if gp % 4 == 3:
                    for tt in range(4):
                        tok = slice(tt * 512, tt * 512 + 512)
                        yv = gf[:, cc, tok]
                        rg = R("gf", cc, tt)
                        DV(lambda e, cc=cc, tok=tok, tt=tt, yv=yv: e.scalar_tensor_tensor(out=yv, in0=uaT[:, cc, tok], scalar=ccol(0, l, cc), in1=PS(4 + tt),
                                                                                      op0=ALU.mult, op1=ALU.add), [R("uaT"), RP(4 + tt), r_const], [rg])
            P.dma("sp", lambda e: e.dma_start(out=xs_in.ap().rearrange("(c p) t -> p c t", p=128), in_=gf[:, 0:2, :]),
                  reads=[R("gf", c_, t_) for c_ in range(2) for t_ in range(4)], writes=[R("xs_in")])
            P.cc(lambda e: e.collective_compute("AllGather", ALU.bypass, replica_groups=[[0, 1], [2, 3], [4, 5], [6, 7]],
                                                ins=[xs_in.ap().opt()], outs=[xs_out.ap().opt()]),
                 reads=[R("xs_in")], writes=[R("xs_out")])
            P.dma("sp", lambda e: e.dma_start(out=gf, in_=xs_out.ap().rearrange("(c p) t -> p c t", p=128)),
                  reads=[R("xs_out")], writes=[R("gf", c_, t_) for c_ in range(4) for t_ in range(4)])
            for cc in range(4):
                for tt in range(4):
                    tok = slice(tt * 512, tt * 512 + 512)
                    yv = gf[:, cc, tok]
                    rg = R("gf", cc, tt)
                    s = tt % 2
                    P.act(lambda e, yv=yv, s=s: e.activation(out=sq2[s], in_=yv, func=AF.Square), reads=[rg], writes=[R("sq2", s)])
                    DV(lambda e, s=s: e.tensor_scalar(out=sq2[s], in0=sq2[s], scalar1=0.044715, scalar2=1.0, op0=ALU.mult, op1=ALU.add), [R("sq2", s)], [R("sq2", s)])
                    DV(lambda e, s=s, yv=yv: e.tensor_tensor(out=sq2[s], in0=sq2[s], in1=yv, op=ALU.mult), [R("sq2", s), rg], [R("sq2", s)])
                    P.act(lambda e, s=s: e.activation(out=sq2[s], in_=sq2[s], func=AF.Sigmoid, scale=1.5957691216057308), reads=[R("sq2", s)], writes=[R("sq2", s)])
                    DV(lambda e, s=s, yv=yv: e.tensor_tensor(out=yv, in0=sq2[s], in1=yv, op=ALU.mult), [R("sq2", s), rg], [rg])
                    P.act(lambda e, yv=yv, cc=cc, tok=tok: e.activation(out=gb[:, cc, tok], in_=yv, func=AF.Copy), reads=[rg], writes=[R("gb", cc, tt)])
            P.barrier()
            A.reset(mW)
            wgl = A.alloc([4, 512], BF16)
            tmpv = A.alloc([512], F32)
            rstd = A.alloc([512], F32)
            stb = [A.alloc([4, 512], BF16) for _ in range(2)]
            P.dma(wq, lambda e: e.dma_start(out=wgl, in_=s5_w_glu[l].rearrange("(k p) n -> p k n", p=128)), writes=[R("wgl")])
            for tt in range(4):
                tok = slice(tt * 512, tt * 512 + 512)
                for m in range(4):
                    pb = (tt * 4 + m) % 4
                    for k in range(4):
                        P.pe(lambda e, k=k, m=m, pb=pb, tok=tok: e.matmul(PS(pb), wgl[:, k, m * 128:(m + 1) * 128], gb[:, k, tok], start=(k == 0), stop=(k == 3)),
                             reads=[R("wgl")] + [R("gb", k, tt)], writes=[RP(pb)])
                    s = m % 2
                    P.act(lambda e, pb=pb, s=s: e.activation(out=sq2[s], in_=PS(pb), func=AF.Sigmoid), reads=[RP(pb)], writes=[R("sq2", s)])
                    DV(lambda e, s=s, m=m, tok=tok: e.tensor_tensor(out=gf[:, m, tok], in0=gf[:, m, tok], in1=sq2[s], op=ALU.mult),
                       [R("sq2", s), R("gf", m, tt)], [R("gf", m, tt), R("ya")])
            if debug and on("dbgS5"):
                P.dma("sp", lambda e: e.dma_start(out=dbg_d.rearrange("(c p) t -> p c t", p=128)[:, 0:4, :], in_=gf), reads=[R("ya")], writes=[R("dbg")])
            group_norm_store(l, 0, gf, R("ya"), sq2, tmpv, rstd, stb)
            P.barrier()

        def stage_pool(l):
            A.reset()
            ub = A.alloc([4, 2048], F32)
            ta = A.alloc([2048], F32)
            tb = A.alloc([2048], F32)
            pb16 = A.alloc([4, 2048], BF16)
            yb = A.alloc([4, 2048], F32)
            pw = A.alloc([4, 128], BF16)
            sq2 = [A.alloc([512], F32) for _ in range(2)]
            tmpv = A.alloc([512], F32)
            rstd = A.alloc([512], F32)
            stb = [A.alloc([4, 512], BF16) for _ in range(2)]
            P.dma("sp", lambda e: e.dma_start(out=ub, in_=ubT_d.rearrange("(c p) t -> p c t", p=128)), reads=[R("ubT_d", t) for t in range(4)], writes=[R("ub")])
            P.dma(wq, lambda e: e.dma_start(out=pw, in_=pool_w[l].rearrange("g c d -> c g d")), writes=[R("pw")])
            for gi in range(4):
                w = 2 ** (gi + 1)
                src = ub[:, gi, :]
                bufs = [ta, tb]
                rn = ["ta", "tb"]
                sh = 1
                i = 0
                cur, rc = src, R("ub")
                while sh < w:
                    dst, rd = bufs[i % 2], R(rn[i % 2])
                    P.dve(lambda e, dst=dst, cur=cur, sh=sh: e.tensor_tensor(out=dst[:, sh:], in0=cur[:, sh:], in1=cur[:, :2048 - sh], op=ALU.add), reads=[rc], writes=[rd])
                    P.dve(lambda e, dst=dst, cur=cur, sh=sh: e.tensor_copy(out=dst[:, 0:sh], in_=cur[:, 0:sh]), reads=[rc], writes=[rd])
                    cur, rc = dst, rd
                    sh *= 2
                    i += 1
                P.dve(lambda e, cur=cur, src=src, gi=gi, w=w: e.scalar_tensor_tensor(out=pb16[:, gi, :], in0=cur, scalar=1.0 / w, in1=src, op0=ALU.mult, op1=ALU.subtract),
                      reads=[rc, R("ub")], writes=[R("pb16", gi)])
                other = bufs[i % 2]
                ro = R(rn[i % 2])
                P.dve(lambda e, cur=cur, other=other, w=w: e.tensor_tensor(out=other[:, 0:w], in0=cur[:, 0:w], in1=invcnt[:, 0:w], op=ALU.mult), reads=[rc, r_const], writes=[ro])
                P.dve(lambda e, other=other, src=src, gi=gi, w=w: e.tensor_tensor(out=pb16[:, gi, 0:w], in0=other[:, 0:w], in1=src[:, 0:w], op=ALU.subtract),
                      reads=[ro, R("ub")], writes=[R("pb16", gi)])
                for tt in range(4):
                    tok = slice(tt * 512, tt * 512 + 512)
                    pbk = (gi * 4 + tt) % 6
                    P.pe(lambda e, gi=gi, tok=tok, pbk=pbk: e.matmul(PS(pbk), pw[:, gi, :], pb16[:, gi, tok], start=True, stop=True),
                         reads=[R("pw"), R("pb16", gi)], writes=[RP(pbk)])
                    P.act(lambda e, gi=gi, tok=tok, pbk=pbk: e.activation(out=yb[:, gi, tok], in_=PS(pbk), func=AF.Identity, scale=ccol(1, l, gi)),
                          reads=[RP(pbk), r_const], writes=[R("yb")])
            if debug and on("dbgPool"):
                P.dma("sp", lambda e: e.dma_start(out=dbg_d.rearrange("(c p) t -> p c t", p=128)[:, 4:8, :], in_=yb), reads=[R("yb")], writes=[R("dbg")])
            group_norm_store(l, 1, yb, R("yb"), sq2, tmpv, rstd, stb)
            P.barrier()

        def stage_conv(l):
            A.reset()
            hg = A.alloc([4, 2080], BF16)
            dg = A.alloc([124, 128], BF16)
            z = A.alloc([4, 2048], F32)
            hs = A.alloc([4, 2048], BF16)
            wpw = A.alloc([4, 512], BF16)
            sq2 = [A.alloc([512], F32) for _ in range(2)]
            mean = A.alloc([512], F32)
            tmpv = A.alloc([512], F32)
            rstd = A.alloc([512], F32)
            stb = [A.alloc([4, 512], BF16) for _ in range(2)]
            P.dve(lambda e: e.memset(hg[:, :, 0:32], 0.0), writes=[R("hg")])
            P.dma("sp", lambda e: e.dma_start(out=hg[:, :, 32:2080], in_=hgT_d.rearrange("(c p) t -> p c t", p=128)),
                  reads=[R("hgT_d", c0, t) for c0 in (0, 2) for t in range(4)], writes=[R("hg")])
            P.dma(wq, lambda e: e.dma_start(out=wpw, in_=conv_w_pw[l].rearrange("(k p) n -> p k n", p=128)), writes=[R("wpw")])
            for c in range(4):
                for j in range(31):
                    o = (l * 4 + c) * 31 + j
                    P.dve(lambda e, c=c, j=j, o=o: e.tensor_scalar(out=dg[:, c * 31 + j, :], in0=ident, scalar1=cw[:, o:o + 1], scalar2=None, op0=ALU.mult),
                          reads=[r_const], writes=[R("dg", c)])
            for c in range(4):
                for tt in range(4):
                    pbk = (c * 4 + tt) % 4
                    for j in range(31):
                        off = 32 + tt * 512 - 30 + j
                        P.pe(lambda e, c=c, j=j, off=off, pbk=pbk: e.matmul(PS(pbk), dg[:, c * 31 + j, :], hg[:, c, off:off + 512], start=(j == 0), stop=(j == 30)),
                             reads=[R("dg", c), R("hg")], writes=[RP(pbk)])
                    P.act(lambda e, c=c, tt=tt, pbk=pbk: e.activation(out=z[:, c, tt * 512:(tt + 1) * 512], in_=PS(pbk), func=AF.Identity, bias=ccol(2, l, c)),
                          reads=[RP(pbk), r_const], writes=[R("z", tt)])
            for tt in range(4):
                tok = slice(tt * 512, tt * 512 + 512)
                for c in range(4):
                    P.pe(lambda e, c=c, tok=tok: e.matmul(PS(4), ones_f, z[:, c, tok], start=(c == 0), stop=(c == 3)), reads=[R("z", tt), r_const], writes=[RP(4)])
                for c in range(4):
                    s = c % 2
                    P.act(lambda e, c=c, s=s, tok=tok: e.activation(out=sq2[s], in_=z[:, c, tok], func=AF.Square), reads=[R("z", tt)], writes=[R("sq2", s)])
                    P.pe(lambda e, c=c, s=s: e.matmul(PS(5), ones_f, sq2[s], start=(c == 0), stop=(c == 3)), reads=[R("sq2", s), r_const], writes=[RP(5)])
                rm = R("lnstat")
                P.dve(lambda e: e.tensor_scalar(out=mean, in0=PS(4), scalar1=1.0 / 512, scalar2=None, op0=ALU.mult), reads=[RP(4)], writes=[rm])
                P.dve(lambda e: e.tensor_tensor(out=tmpv, in0=mean, in1=mean, op=ALU.mult), reads=[rm], writes=[rm])
                P.dve(lambda e: e.scalar_tensor_tensor(out=tmpv, in0=PS(5), scalar=1.0 / 512, in1=tmpv, op0=ALU.mult, op1=ALU.subtract), reads=[RP(5), rm], writes=[rm])
                P.dve(lambda e: e.tensor_scalar(out=tmpv, in0=tmpv, scalar1=EPS, scalar2=None, op0=ALU.add), reads=[rm], writes=[rm])
                P.act(lambda e: e.activation(out=tmpv, in_=tmpv, func=AF.Sqrt), reads=[rm], writes=[rm])
                P.dve(lambda e: e.reciprocal(out=rstd, in_=tmpv), reads=[rm], writes=[rm])
                for c in range(4):
                    s = c % 2
                    P.dve(lambda e, c=c, s=s, tok=tok: e.tensor_tensor(out=sq2[s], in0=z[:, c, tok], in1=mean, op=ALU.subtract), reads=[R("z", tt), rm], writes=[R("sq2", s)])
                    P.dve(lambda e, s=s: e.tensor_tensor(out=sq2[s], in0=sq2[s], in1=rstd, op=ALU.mult), reads=[R("sq2", s), rm], writes=[R("sq2", s)])
                    P.act(lambda e, c=c, s=s, tok=tok: e.activation(out=hs[:, c, tok], in_=sq2[s], func=AF.Silu, scale=ccol(3, l, c), bias=ccol(4, l, c)),
                          reads=[R("sq2", s), r_const], writes=[R("hs", tt)])
            for tt in range(4):
                tok = slice(tt * 512, tt * 512 + 512)
                for m in range(4):
                    pbk = (tt * 4 + m) % 4
                    for k in range(4):
                        P.pe(lambda e, k=k, m=m, pbk=pbk, tok=tok: e.matmul(PS(pbk), wpw[:, k, m * 128:(m + 1) * 128], hs[:, k, tok], start=(k == 0), stop=(k == 3)),
                             reads=[R("wpw"), R("hs", tt)], writes=[RP(pbk)])
                    alt.copy(z[:, m, tok], PS(pbk), [RP(pbk)], [R("z", tt), R("yc")])
            if debug and on("dbgConv"):
                P.dma("sp", lambda e: e.dma_start(out=dbg_d.rearrange("(c p) t -> p c t", p=128)[:, 8:12, :], in_=z), reads=[R("yc")], writes=[R("dbg")])
            group_norm_store(l, 2, z, R("yc"), sq2, tmpv, rstd, stb)
            P.barrier()

        def stage_attn(l):
            A.reset()
            em = [A.alloc([3968], BF16) for _ in range(4)]
            qz = [A.alloc([2048], BF16) for _ in range(4)]
            kT = [A.alloc([2048], BF16) for _ in range(2)]
            vz = [A.alloc([16, 128], BF16) for _ in range(4)]
            ex = [A.alloc([512], BF16) for _ in range(4)]
            pt = [A.alloc([512], BF16) for _ in range(4)]
            rcp = [A.alloc([512], F32) for _ in range(2)]
            yd = A.alloc([4, 2048], F32)
            sq2 = [A.alloc([512], F32) for _ in range(2)]
            tmpv = A.alloc([512], F32)
            rstd = A.alloc([512], F32)
            stb = [A.alloc([4, 512], BF16) for _ in range(2)]
            items = []

            def load(c):
                kb = c % 2
                P.dma("sp", lambda e, c=c, kb=kb: e.dma_start(out=kT[kb], in_=kT_d[c]), reads=[R("kT_d", t) for t in range(4)], writes=[R("kT", kb)])
                for e_ in range(2):
                    h = 2 * c + e_
                    hb = h % 4
                    P.dma("sp", lambda e, h=h, hb=hb: e.dma_start(out=em[hb], in_=em_d[h]), reads=[R("em_d", h)], writes=[R("em", hb)])
                    P.dma("sp", lambda e, h=h, hb=hb: e.dma_start(out=qz[hb], in_=qz_d[h]), reads=[R("qz_d", c, t) for t in range(4)], writes=[R("qz", hb)])
                    P.dma("sp", lambda e, h=h, hb=hb: e.dma_start(out=vz[hb], in_=vz_d.rearrange("r p (h d) -> h p r d", d=128)[h]),
                          reads=[R("vz_d", rp) for rp in range(16)], writes=[R("vz", hb)])

            for c in range(2):
                kb = c % 2
                for j in range(4):
                    for e_ in range(2):
                        for rp in range(16):
                            items.append((c, j, e_, rp, kb))
            NIT = len(items)
            LA = 2

            def front(i):
                c, j, e_, rp, kb = items[i]
                hb = (2 * c + e_) % 4
                sb_ = i % 4
                xb = i % 4
                emv = em[hb].rearrange("p (a b) -> p a b", b=128)
                P.pe(lambda e, kb=kb, rp=rp, hb=hb, j=j, sb_=sb_: e.matmul(PS(sb_), kT[kb][:, rp * 128:(rp + 1) * 128], qz[hb][:, j * 512:(j + 1) * 512],
                                                                         start=True, stop=True),
                     reads=[R("kT", kb), R("qz", hb)], writes=[RP(sb_)])
                P.act(lambda e, sb_=sb_, xb=xb: e.activation(out=ex[xb], in_=PS(sb_), func=AF.Exp, scale=0.125), reads=[RP(sb_)], writes=[R("ex", xb)])
                d0 = 4 * j - rp + 15
                emslice = emv[:, d0:d0 + 4, :].rearrange("p a b -> p (a b)")
                P.dve(lambda e, xb=xb, emslice=emslice: e.tensor_tensor(out=pt[xb], in0=ex[xb], in1=emslice, op=ALU.mult),
                      reads=[R("ex", xb), R("em", hb)], writes=[R("pt", xb)])

            def back(i):
                c, j, e_, rp, kb = items[i]
                hb = (2 * c + e_) % 4
                xb = i % 4
                nb = 4 + (c * 4 + j) % 2
                db = 6 + (c * 4 + j) % 2
                first = (e_ == 0 and rp == 0)
                last = (e_ == 1 and rp == 15)
                P.pe(lambda e, hb=hb, rp=rp, xb=xb, nb=nb, first=first, last=last: e.matmul(PS(nb), vz[hb][:, rp, :], pt[xb], start=first, stop=last),
                     reads=[R("vz", hb), R("pt", xb)], writes=[RP(nb)])
                P.pe(lambda e, e_=e_, xb=xb, db=db, first=first, last=last: e.matmul(PS(db), ohh[:, e_, :], pt[xb], start=first, stop=last),
                     reads=[r_const, R("pt", xb)], writes=[RP(db)])
                if last:
                    rb_ = (c * 4 + j) % 2
                    P.dve(lambda e, db=db, rb_=rb_: e.reciprocal(out=rcp[rb_], in_=PS(db)), reads=[RP(db)], writes=[R("rcp", rb_)])
                    dst = yd[:, c, :].rearrange("p (s r) -> p r s", r=16)[:, 4 * j:4 * j + 4, :]
                    P.dve(lambda e, nb=nb, rb_=rb_, dst=dst: e.tensor_tensor(out=dst, in0=PS(nb).rearrange("p (r s) -> p r s", r=4),
                                                                           in1=rcp[rb_].rearrange("p (r s) -> p r s", r=4), op=ALU.mult),
                          reads=[RP(nb), R("rcp", rb_)], writes=[R("yd")])

            load(0)
            for i in range(NIT + LA):
                if i < NIT and i % 128 == LA + 1 and i // 128 + 1 < 2:
                    load(i // 128 + 1)
                if i < NIT:
                    front(i)
                if i - LA >= 0:
                    back(i - LA)
            P.dma("sp", lambda e: e.dma_start(out=xa_in.ap().rearrange("(c p) t -> p c t", p=128), in_=yd[:, 0:2, :]), reads=[R("yd")], writes=[R("xa_in")])
            P.cc(lambda e: e.collective_compute("AllGather", ALU.bypass, replica_groups=[[0, 1], [2, 3], [4, 5], [6, 7]],
                                                ins=[xa_in.ap().opt()], outs=[xa_out.ap().opt()]),
                 reads=[R("xa_in")], writes=[R("xa_out")])
            P.dma("sp", lambda e: e.dma_start(out=yd, in_=xa_out.ap().rearrange("(c p) t -> p c t", p=128)), reads=[R("xa_out")], writes=[R("yd")])
            if debug and on("dbgAttn"):
                P.dma("sp", lambda e: e.dma_start(out=dbg_d.rearrange("(c p) t -> p c t", p=128)[:, 12:16, :], in_=yd), reads=[R("yd")], writes=[R("dbg")])
            group_norm_store(l, 3, yd, R("yd"), sq2, tmpv, rstd, stb)
            P.barrier()

        def stage_post(l, last):
            A.reset()
            T0 = 0
            acc = A.alloc([16, 1024], F32)
            actb = A.alloc([16, 1024], BF16)
            NS = 3
            wsl = [A.alloc([8192], BF16) for _ in range(NS)]
            sqt = [A.alloc([512], F32) for _ in range(3)]
            tmpv = A.alloc([512], F32)
            rstd = A.alloc([512], F32)
            ynv = ynT_d.rearrange("(c p) t -> p c t", p=128)
            racc = [R("acc", t) for t in range(2)]
            ract = [R("actb", t) for t in range(2)]
            m1 = A.mark()
            hs0 = [A.alloc([2, 512], F32) for _ in range(2)]
            hs1 = [A.alloc([2, 512], F32) for _ in range(2)]
            ys0 = [A.alloc([2, 512], BF16) for _ in range(2)]
            ys1 = [A.alloc([2, 512], BF16) for _ in range(2)]
            si_ = 0
            for t in range(2):
                for q8 in range(8):
                    q = q8 // 2
                    c0 = (q8 % 2) * 2
                    cg = 2 * q8
                    b = si_ % 2
                    si_ += 1
                    tsl = slice(t * 512, (t + 1) * 512)
                    P.dma("sp", lambda e, b=b, q=q, c0=c0, tsl=tsl: e.dma_start(out=hs0[b], in_=hxv(q, 0)[:, c0:c0 + 2, tsl]), reads=[R("hx", q)], writes=[R("hs0", b)])
                    P.dma("sp", lambda e, b=b, q=q, c0=c0, tsl=tsl: e.dma_start(out=hs1[b], in_=hxv(q, 1)[:, c0:c0 + 2, tsl]), reads=[R("hx", q)], writes=[R("hs1", b)])
                    P.dve(lambda e, b=b, cg=cg, tsl=tsl: e.tensor_scalar(out=acc[:, cg:cg + 2, tsl], in0=hs0[b], scalar1=selt[:, 0:1], scalar2=None, op0=ALU.mult),
                          reads=[R("hs0", b), r_const], writes=[racc[t]])
                    P.dve(lambda e, b=b, cg=cg, tsl=tsl: e.scalar_tensor_tensor(out=acc[:, cg:cg + 2, tsl], in0=hs1[b], scalar=selt[:, 1:2], in1=acc[:, cg:cg + 2, tsl],
                                                                              op0=ALU.mult, op1=ALU.add),
                          reads=[R("hs1", b), r_const, racc[t]], writes=[racc[t]])
                    P.dma("sp", lambda e, b=b, cg=cg, t=t: e.dma_start(out=ys0[b], in_=ynv[:, cg:cg + 2, t * 512:(t + 1) * 512]),
                          reads=[R("ynT_d", q, tt_) for tt_ in range(4)], writes=[R("ys0", b)])
                    P.dma("sp", lambda e, b=b, cg=cg, t=t: e.dma_start(out=ys1[b], in_=ynv[:, cg:cg + 2, 1024 + t * 512:1024 + (t + 1) * 512]),
                          reads=[R("ynT_d", q, tt_) for tt_ in range(4)], writes=[R("ys1", b)])
                    P.dve(lambda e, b=b, cg=cg, tsl=tsl: e.tensor_scalar(out=actb[:, cg:cg + 2, tsl], in0=ys0[b], scalar1=selt[:, 0:1], scalar2=None, op0=ALU.mult),
                          reads=[R("ys0", b), r_const], writes=[ract[t]])
                    P.dve(lambda e, b=b, cg=cg, tsl=tsl: e.scalar_tensor_tensor(out=actb[:, cg:cg + 2, tsl], in0=ys1[b], scalar=selt[:, 1:2], in1=actb[:, cg:cg + 2, tsl],
                                                                              op0=ALU.mult, op1=ALU.add),
                          reads=[R("ys1", b), r_const, ract[t]], writes=[ract[t]])
            wctr = [0]

            def wload(src_ap, view):
                s = wctr[0] % NS
                wctr[0] += 1
                dst = view(wsl[s])
                P.dma(wq, lambda e, dst=dst, src_ap=src_ap: e.dma_start(out=dst, in_=src_ap), writes=[R("wslp", s)])
                return dst, R("wslp", s)

            v16 = lambda t: t.rearrange("p (k n) -> p k n", k=16)
            v4 = lambda t: t.rearrange("p (k n) -> p k n", k=4)
            pctr = [0]

            def nextbank(lo=0, n=6):
                b = lo + pctr[0] % n
                pctr[0] += 1
                return b

            def accum(o, t, pb):
                P.dve(lambda e, o=o, t=t, pb=pb: e.tensor_tensor(out=acc[:, o, t * 512:(t + 1) * 512], in0=acc[:, o, t * 512:(t + 1) * 512], in1=PS(pb), op=ALU.add),
                      reads=[RP(pb), racc[t]], writes=[racc[t]])

            def norm_to_actb(kind):
                for t in range(2):
                    for c in range(16):
                        s = c % 3
                        P.act(lambda e, c=c, s=s, t=t: e.activation(out=sqt[s], in_=acc[:, c, t * 512:(t + 1) * 512], func=AF.Square), reads=[racc[t]], writes=[R("sqt", s)])
                        P.pe(lambda e, c=c, s=s: e.matmul(PS(7), ones_f, sqt[s], start=(c == 0), stop=(c == 15)), reads=[R("sqt", s), r_const], writes=[RP(7)])
                    rstd_from_sum(7, rstd, tmpv, D)
                    for c in range(16):
                        P.dve(lambda e, c=c, t=t: e.scalar_tensor_tensor(out=actb[:, c, t * 512:(t + 1) * 512], in0=acc[:, c, t * 512:(t + 1) * 512],
                                                                      scalar=gcol(kind, l, c), in1=rstd, op0=ALU.mult, op1=ALU.mult),
                              reads=[racc[t], R(id(rstd)), r_const], writes=[ract[t]])

            wo = w_out[l].rearrange("(k p) n -> p k n", p=128)
            for ob in range(4):
                wt, rw = wload(wo[:, :, ob * 512:(ob + 1) * 512], v16)
                for t in range(2):
                    for m in range(4):
                        pb = nextbank()
                        for k in range(16):
                            P.pe(lambda e, k=k, m=m, t=t, pb=pb, wt=wt: e.matmul(PS(pb), wt[:, k, m * 128:(m + 1) * 128], actb[:, k, t * 512:(t + 1) * 512],
                                                                              start=(k == 0), stop=(k == 15)),
                                 reads=[rw, ract[t]], writes=[RP(pb)])
                        accum(ob * 4 + m, t, pb)
            if debug and on("dbgC") and l == 0:
                P.dma("sp", lambda e: e.dma_start(out=dbg_d.rearrange("(c p) t -> p c t", p=128)[:, :, T0:T0 + 1024], in_=acc), reads=racc, writes=[R("dbg")])
            P.barrier()
            A.reset(m1)
            qx = A.alloc([4, 1024], BF16)
            ox = A.alloc([4, 1024], BF16)
            pxt = [A.alloc([512], BF16) for _ in range(4)]
            rcp = [A.alloc([512], F32) for _ in range(2)]
            if True:
                wt, rw = wload(w_xk[l].rearrange("(k p) n -> p k n", p=128), v16)
                for hh in range(4):
                    pb = nextbank()
                    for k in range(16):
                        P.pe(lambda e, k=k, hh=hh, pb=pb, wt=wt: e.matmul(PS(pb)[:, 0:256], wt[:, k, hh * 128:(hh + 1) * 128], memnT[:, k, :], start=(k == 0), stop=(k == 15)),
                             reads=[rw, R("memnT")], writes=[RP(pb)])
                    alt.copy(KxT[:, hh, :], PS(pb)[:, 0:256], [RP(pb)], [R("KxT")])
                wt, rw = wload(w_xv[l].rearrange("(k p) n -> p k n", p=128), v16)
                for mb in range(2):
                    pb = nextbank()
                    for k in range(16):
                        P.pe(lambda e, k=k, mb=mb, pb=pb, wt=wt: e.matmul(PS(pb), memnT[:, k, mb * 128:(mb + 1) * 128], wt[:, k, :], start=(k == 0), stop=(k == 15)),
                             reads=[rw, R("memnT")], writes=[RP(pb)])
                    alt.copy(Vx[:, mb, :], PS(pb), [RP(pb)], [R("Vx")])
            norm_to_actb(2)
            wt, rw = wload(w_xq[l].rearrange("(k p) n -> p k n", p=128), v16)
            for t in range(2):
                for hh in range(4):
                    pb = nextbank()
                    for k in range(16):
                        P.pe(lambda e, k=k, hh=hh, t=t, pb=pb, wt=wt: e.matmul(PS(pb), wt[:, k, hh * 128:(hh + 1) * 128], actb[:, k, t * 512:(t + 1) * 512],
                                                                           start=(k == 0), stop=(k == 15)),
                             reads=[rw, ract[t]], writes=[RP(pb)])
                    alt.copy(qx[:, hh, t * 512:(t + 1) * 512], PS(pb), [RP(pb)], [R("qx", hh, t)])
            xs = 128 ** -0.5
            ip = 0
            for t in range(2):
                for hh in range(4):
                    pn = nextbank()
                    pd = nextbank()
                    for mb in range(2):
                        pb = nextbank()
                        xb = ip % 4
                        ip += 1
                        P.pe(lambda e, hh=hh, mb=mb, t=t, pb=pb: e.matmul(PS(pb), KxT[:, hh, mb * 128:(mb + 1) * 128], qx[:, hh, t * 512:(t + 1) * 512], start=True, stop=True),
                             reads=[R("KxT"), R("qx", hh, t)], writes=[RP(pb)])
                        P.act(lambda e, pb=pb, xb=xb: e.activation(out=pxt[xb], in_=PS(pb), func=AF.Exp, scale=xs), reads=[RP(pb)], writes=[R("pxt", xb)])
                        P.pe(lambda e, hh=hh, mb=mb, xb=xb, pn=pn: e.matmul(PS(pn), Vx[:, mb, hh * 128:(hh + 1) * 128], pxt[xb], start=(mb == 0), stop=(mb == 1)),
                             reads=[R("Vx"), R("pxt", xb)], writes=[RP(pn)])
                        P.pe(lambda e, mb=mb, xb=xb, pd=pd: e.matmul(PS(pd), ones_b, pxt[xb], start=(mb == 0), stop=(mb == 1)),
                             reads=[r_const, R("pxt", xb)], writes=[RP(pd)])
                    rb_ = (t * 4 + hh) % 2
                    P.dve(lambda e, pd=pd, rb_=rb_: e.reciprocal(out=rcp[rb_], in_=PS(pd)), reads=[RP(pd)], writes=[R("rcpx", rb_)])
                    P.dve(lambda e, pn=pn, rb_=rb_, hh=hh, t=t: e.tensor_tensor(out=ox[:, hh, t * 512:(t + 1) * 512], in0=PS(pn), in1=rcp[rb_], op=ALU.mult),
                          reads=[RP(pn), R("rcpx", rb_)], writes=[R("ox", t)])
            wt, rw = wload(w_xo[l].rearrange("(k p) n -> p k n", p=128), v4)
            for t in range(2):
                for o in range(16):
                    pb = nextbank()
                    for k in range(4):
                        P.pe(lambda e, k=k, o=o, t=t, pb=pb, wt=wt: e.matmul(PS(pb), wt[:, k, o * 128:(o + 1) * 128], ox[:, k, t * 512:(t + 1) * 512], start=(k == 0), stop=(k == 3)),
                             reads=[rw, R("ox", t)], writes=[RP(pb)])
                    accum(o, t, pb)
            if debug and on("dbgD") and l == 0:
                P.dma("sp", lambda e: e.dma_start(out=dbg_d.rearrange("(c p) t -> p c t", p=128)[:, :, T0:T0 + 1024], in_=acc), reads=racc, writes=[R("dbg")])
            P.barrier()
            A.reset(m1)
            hid = [A.alloc([4, 512], BF16) for _ in range(2)]
            rl = [A.alloc([512], F32) for _ in range(2)]
            norm_to_actb(3)
            wu = w_up[l].rearrange("(k p) n -> p k n", p=128)
            wd = w_down[l].rearrange("(k p) n -> p k n", p=128)
            ih = 0
            for fb in range(16):
                wtu, rwu = wload(wu[:, :, fb * 512:(fb + 1) * 512], v16)
                wtd, rwd = wload(wd[:, fb * 4:(fb + 1) * 4, :], v4)
                for t in range(2):
                    hb = ih % 2
                    ih += 1
                    for m in range(4):
                        pb = nextbank(0, 4)
                        for k in range(16):
                            P.pe(lambda e, k=k, m=m, t=t, pb=pb, wtu=wtu: e.matmul(PS(pb), wtu[:, k, m * 128:(m + 1) * 128], actb[:, k, t * 512:(t + 1) * 512],
                                                                                 start=(k == 0), stop=(k == 15)),
                                 reads=[rwu, ract[t]], writes=[RP(pb)])
                        s = m % 2
                        P.act(lambda e, pb=pb, s=s: e.activation(out=rl[s], in_=PS(pb), func=AF.Relu), reads=[RP(pb)], writes=[R("rl", s)])
                        P.dve(lambda e, s=s, hb=hb, m=m: e.tensor_tensor(out=hid[hb][:, m, :], in0=rl[s], in1=rl[s], op=ALU.mult), reads=[R("rl", s)], writes=[R("hid", hb)])
                    for o in range(16):
                        pb = 4 + (o % 3)
                        for k in range(4):
                            P.pe(lambda e, k=k, o=o, hb=hb, pb=pb, wtd=wtd: e.matmul(PS(pb), wtd[:, k, o * 128:(o + 1) * 128], hid[hb][:, k, :], start=(k == 0), stop=(k == 3)),
                                 reads=[rwd, R("hid", hb)], writes=[RP(pb)])
                        accum(o, t, pb)
            if not last:
                for q in range(4):
                    P.dma("sp", lambda e, q=q: e.dma_start(out=xin_t[q].ap().rearrange("(ci p) t -> p ci t", p=128), in_=acc[:, 4 * q:4 * q + 4, :]),
                          reads=racc, writes=[R("xin", q)])
                    P.cc(lambda e, q=q: e.collective_compute("AllGather", ALU.bypass, replica_groups=[[0, 1], [2, 3], [4, 5], [6, 7]],
                                                             ins=[xin_t[q].ap().opt()], outs=[hx_t[q].ap().opt()]),
                         reads=[R("xin", q)], writes=[R("hx", q)])
            else:
                P.barrier()
                A.reset(m1)
                ot = [A.alloc([2048], F32) for _ in range(2)]
                for t in range(2):
                    for c in range(16):
                        s = c % 3
                        P.act(lambda e, c=c, s=s, t=t: e.activation(out=sqt[s], in_=acc[:, c, t * 512:(t + 1) * 512], func=AF.Square), reads=[racc[t]], writes=[R("sqt", s)])
                        P.pe(lambda e, c=c, s=s: e.matmul(PS(7), ones_f, sqt[s], start=(c == 0), stop=(c == 15)), reads=[R("sqt", s), r_const], writes=[RP(7)])
                    rstd_from_sum(7, rstd, tmpv, D)
                    for c in range(16):
                        P.dve(lambda e, c=c, t=t: e.scalar_tensor_tensor(out=acc[:, c, t * 512:(t + 1) * 512], in0=acc[:, c, t * 512:(t + 1) * 512],
                                                                      scalar=gcol(4, 0, c), in1=rstd, op0=ALU.mult, op1=ALU.mult),
                              reads=[racc[t], R(id(rstd)), r_const], writes=[racc[t]])
                for tk in range(8):
                    ob = tk % 2
                    for cg in range(4):
                        pb = nextbank()
                        for ci in range(4):
                            c = cg * 4 + ci
                            P.pe(lambda e, c=c, ci=ci, tk=tk, pb=pb: e.transpose(out=PS(pb)[:, ci * 128:(ci + 1) * 128], in_=acc[:, c, tk * 128:(tk + 1) * 128], identity=ident),
                                 reads=[racc[tk // 4], r_const], writes=[RP(pb)])
                        alt.copy(ot[ob][:, cg * 512:(cg + 1) * 512], PS(pb), [RP(pb)], [R("ot", ob)])
                    P.dma("sp", lambda e, ob=ob, tk=tk: e.dma_start(out=y_out[T0 + tk * 128:T0 + (tk + 1) * 128, :], in_=ot[ob]), reads=[R("ot", ob)], writes=[R("y", tk)])
            P.barrier()

        if on("setup"):
            setup()
        for l in range(n_layers):
            if on("A"):
                stage_A(l)
            if on("S5"):
                stage_S5(l)
            if on("pool"):
                stage_pool(l)
            if on("conv"):
                stage_conv(l)
            if on("attn"):
                stage_attn(l)
            if on("post"):
                stage_post(l, last=(l == n_layers - 1))
        nops = P.emit()
    return nc, nops


def _pc(v, nch):
    v = np.asarray(v, np.float32)
    lead = v.shape[:-1]
    v = v.reshape(lead + (nch, 128))
    return np.moveaxis(v, -1, 0)


def prep_shared(inputs, n_layers=NL):
    I = {k: np.asarray(v) for k, v in inputs.items()}
    f32 = np.float32
    sh = {}
    sh.update(_static_tables())
    sh["mem_norm_g"] = I["mem_norm_g"].astype(f32).reshape(1, D)
    g = [_pc(I[k], 16).reshape(128, NL * 16) for k in ("norm_mix_g", "grp_norm_g", "norm_x_g", "norm_mlp_g")]
    g.append(_pc(I["norm_final_g"], 16).reshape(128, 16))
    gv = np.zeros((128, 5 * NL * 16 + 16), f32)
    for i in range(4):
        gv[:, i * NL * 16:(i + 1) * NL * 16] = g[i]
    gv[:, 4 * NL * 16:4 * NL * 16 + 16] = g[4]
    sh["gvec"] = gv
    c = [_pc(I[k], 4).reshape(128, NL * 4) for k in ("s5_d", "pool_scale", "conv_b_dw", "conv_ln_g", "conv_ln_b")]
    sh["cvec_full"] = np.ascontiguousarray(np.stack([v.reshape(128, NL, 4) for v in c], axis=1), f32)
    wdw = I["conv_w_dw"].astype(f32)
    wdw = wdw.reshape(NL, 31, 4, 128).transpose(3, 0, 2, 1)
    sh["cwdw"] = np.ascontiguousarray(wdw.reshape(128, NL * 4 * 31))
    lr = I["s5_lam_re"].astype(f32)
    li = I["s5_lam_im"].astype(f32)
    ldt = np.repeat(I["s5_log_dt"].astype(f32)[:, :, None], 64, axis=2)

    def sr(a):
        return a.reshape(NL, 16, 2, 64).transpose(0, 2, 3, 1).reshape(NL, 128, 16)
    sh["s5_sr_full"] = (sr(lr), sr(li), sr(ldt))
    bT = np.zeros((NL, 2, 128, 16, 128), f32)
    cT = np.zeros((NL, 2, 128, 16, 128), f32)
    for ri, (bk, ck) in enumerate((("s5_b_re", "s5_c_re"), ("s5_b_im", "s5_c_im"))):
        Bv = I[bk].astype(f32)
        Cv = I[ck].astype(f32)
        for gp in range(16):
            for gl in range(2):
                gg = 2 * gp + gl
                r0 = (gp % 4) * 32 + gl * 16
                bT[:, ri, r0:r0 + 16, gp, gl * 64:(gl + 1) * 64] = Bv[:, gg].transpose(0, 2, 1)
                cT[:, ri, gl * 64:(gl + 1) * 64, gp, r0:r0 + 16] = Cv[:, gg].transpose(0, 2, 1)
    sh["s5_bT_full"] = bT
    sh["s5_cT_full"] = cT
    for k in ("w_in", "s5_w_glu", "pool_w", "conv_w_pw", "w_out", "w_xq", "w_xk", "w_xv", "w_xo", "w_up", "w_down"):
        sh[k] = np.ascontiguousarray(I[k], f32)
    return sh


def per_core(sh, inputs, r):
    o = {}
    a, b, c = sh["s5_sr_full"]
    o["s5_sr"] = np.ascontiguousarray(np.concatenate([a[:, :, 8 * r:8 * r + 8], b[:, :, 8 * r:8 * r + 8], c[:, :, 8 * r:8 * r + 8]], axis=2))
    o["s5_bT"] = np.ascontiguousarray(sh["s5_bT_full"][:, :, :, 8 * r:8 * r + 8, :].reshape(NL, 2, 128, 1024))
    o["s5_cT"] = np.ascontiguousarray(sh["s5_cT_full"][:, :, :, 8 * r:8 * r + 8, :].reshape(NL, 2, 128, 1024))
    win = np.asarray(inputs["w_in"], np.float32)
    o["w_ua"] = np.ascontiguousarray(win[:, :, 256 * r:256 * r + 256])
    o["w_qkv"] = np.ascontiguousarray(np.concatenate([win[:, :, 2048 + 256 * r:2048 + 256 * r + 256], win[:, :, 2560 + 256 * r:2560 + 256 * r + 256],
                                                      win[:, :, 3072 + 256 * r:3072 + 256 * r + 256]], axis=2))
    o["rel_bias"] = np.ascontiguousarray(np.asarray(inputs["rel_bias"], np.float32)[:, 4 * r:4 * r + 4])
    cvv = sh["cvec_full"].copy()
    d = cvv[:, 0].copy()
    cvv[:, 0, :, 0:2] = d[:, :, 2 * r:2 * r + 2]
    o["cvec"] = np.ascontiguousarray(cvv.reshape(128, 5 * NL * 4))
    return o


_CACHE = {}


def kernel(**inputs):
    if "nc" not in _CACHE:
        _CACHE["nc"] = build()[0]
    nc = _CACHE["nc"]
    sh = prep_shared(inputs)
    x = np.asarray(inputs["x"], np.float32)
    mem = np.asarray(inputs["mem"], np.float32)
    in_maps = []
    for c in range(8):
        m = dict(sh)
        m["x"] = np.ascontiguousarray(x[c // 2])
        m["mem"] = np.ascontiguousarray(mem[c // 2])
        sel = np.zeros((128, 2), np.float32)
        sel[:, c % 2] = 1.0
        m["sel"] = sel
        m.update(per_core(sh, inputs, c % 2))
        for k in ("s5_sr_full", "s5_bT_full", "s5_cT_full", "cvec_full"):
            m.pop(k, None)
        in_maps.append(m)
    res = run_bass_kernel_spmd(nc, in_maps, core_ids=list(range(8)))
    out = np.empty((4, L, D), np.float32)
    for c in range(8):
        out[c // 2, (c % 2) * 1024:(c % 2 + 1) * 1024] = np.asarray(res.results[c]["y"], np.float32)
    return out
```

```python
import numpy as np
import ml_dtypes
import concourse.bass as bass
import concourse.mybir as mybir
from concourse.bass_utils import run_bass_kernel_spmd
from contextlib import ExitStack

F32 = mybir.dt.float32
BF16 = mybir.dt.bfloat16
ALU = mybir.AluOpType
AF = mybir.ActivationFunctionType
AX = mybir.AxisListType

NL = 4
D = 2048
L = 2048
DFF = 8192
EPS = 1e-6
PI = float(np.pi)


class Res:
    __slots__ = ("name", "last_w", "readers")

    def __init__(self, name):
        self.name = name
        self.last_w = None
        self.readers = []


class Op:
    __slots__ = ("eng", "fn", "deps", "is_dma", "sem", "count_after", "has_dep", "dsem", "bar", "inc")

    def __init__(self, eng, fn, is_dma, dsem):
        self.eng = eng
        self.fn = fn
        self.deps = set()
        self.is_dma = is_dma
        self.has_dep = False
        self.dsem = dsem
        self.bar = False
        self.inc = 16


class Prog:
    ENGS = ("pe", "act", "dve", "pool", "sp")
    RING = 8

    def __init__(self, nc):
        self.nc = nc
        self.ops = []
        self.last_eng = {}
        self.last_dma = {}

    def _add(self, eng, fn, reads, writes, is_dma=False, dsem=None):
        op = Op(eng, fn, is_dma, dsem)
        oid = len(self.ops)
        for r in reads:
            if r.last_w is not None:
                op.deps.add(r.last_w)
        for r in writes:
            if r.last_w is not None:
                op.deps.add(r.last_w)
            for x in r.readers:
                op.deps.add(x)
        for r in reads:
            r.readers.append(oid)
        for r in writes:
            r.last_w = oid
            r.readers = []
        op.deps.discard(oid)
        self.ops.append(op)
        if is_dma:
            lst = self.last_dma.setdefault(dsem, [])
            lst.append(oid)
            if len(lst) > self.RING:
                lst.pop(0)
        else:
            self.last_eng[eng] = oid
        return oid

    def pe(self, fn, reads=(), writes=()):
        return self._add("pe", fn, reads, writes)

    def act(self, fn, reads=(), writes=()):
        return self._add("act", fn, reads, writes)

    def dve(self, fn, reads=(), writes=()):
        return self._add("dve", fn, reads, writes)

    def pool(self, fn, reads=(), writes=()):
        return self._add("pool", fn, reads, writes)

    def dma(self, q, fn, reads=(), writes=(), cls="d"):
        return self._add(q, fn, reads, writes, is_dma=True, dsem=(q, cls))

    def cc(self, fn, reads=(), writes=()):
        oid = self._add("pool", fn, reads, writes, is_dma=True, dsem=("pool", "cc"))
        self.ops[oid].inc = 1
        return oid

    def barrier(self):
        deps = set(self.last_eng.values())
        for lst in self.last_dma.values():
            deps |= set(lst)
        for e in self.ENGS:
            op = Op(e, None, False, None)
            op.bar = True
            op.deps = set(deps)
            self.ops.append(op)

    def emit(self):
        nc = self.nc
        ops = self.ops
        for op in ops:
            for d in op.deps:
                ops[d].has_dep = True
        per_eng = {e: [] for e in self.ENGS}
        for i, op in enumerate(ops):
            per_eng[op.eng].append(i)
        for e in self.ENGS:
            for i in reversed(per_eng[e]):
                if not ops[i].is_dma and not ops[i].bar:
                    ops[i].has_dep = True
                    break
        comp_count = {e: 0 for e in self.ENGS}
        dma_idx = {}
        dma_count = {}
        prev_wait = {}
        for i, op in enumerate(ops):
            if op.bar:
                continue
            if op.is_dma:
                k = op.dsem
                n = dma_idx.get(k, 0)
                dma_idx[k] = n + 1
                ring = 1 if k[1] == "cc" else self.RING
                sk = ("dma",) + k + (str(n % ring),)
                dma_count[sk] = dma_count.get(sk, 0) + op.inc
                op.sem = sk
                op.count_after = dma_count[sk]
                prev_wait[i] = (sk, op.count_after - op.inc)
            else:
                if op.has_dep:
                    comp_count[op.eng] += 1
                op.sem = ("eng", op.eng)
                op.count_after = comp_count[op.eng]
        keys = [("eng", e) for e in self.ENGS] + list(dma_count.keys())
        with ExitStack() as st:
            sems = {}
            for k in keys:
                sems[k] = st.enter_context(nc.semaphore("s_" + "_".join(k)))
            block = st.enter_context(nc.Block())

            def make(e):
                def body(eng):
                    waited = {}
                    for i in per_eng[e]:
                        op = ops[i]
                        need = {}
                        for d in op.deps:
                            p = ops[d]
                            if (not p.is_dma) and p.eng == e and e == "pe" and not op.bar:
                                continue
                            need[p.sem] = max(need.get(p.sem, 0), p.count_after)
                        if i in prev_wait and prev_wait[i][1] > 0:
                            k_, v_ = prev_wait[i]
                            need[k_] = max(need.get(k_, 0), v_)
                        for k, v in need.items():
                            if waited.get(k, 0) >= v:
                                continue
                            eng.wait_ge(sems[k], v)
                            waited[k] = v
                        if op.bar:
                            continue
                        ins = op.fn(eng)
                        if op.is_dma:
                            ins.then_inc(sems[op.sem], op.inc)
                        elif op.has_dep:
                            ins.then_inc(sems[op.sem], 1)
                    if e == "sp":
                        for k in keys:
                            v = dma_count[k] if k[0] == "dma" else comp_count[k[1]]
                            if v > 0 and waited.get(k, 0) < v:
                                eng.wait_ge(sems[k], v)
                return body

            block.tensor(make("pe"))
            block.scalar(make("act"))
            block.vector(make("dve"))
            block.gpsimd(make("pool"))
            block.sync(make("sp"))
        return len(ops)


def _prod(s):
    r = 1
    for v in s:
        r *= v
    return r


class Arena:
    def __init__(self, t, nbytes):
        self.t = t
        self.cap = nbytes
        self.top = 0

    def alloc(self, shape, dt):
        n = _prod(shape)
        esz = 4 if dt == F32 else 2
        nb = (n * esz + 31) // 32 * 32
        off = self.top
        self.top += nb
        assert self.top <= self.cap, f"arena overflow {self.top} > {self.cap}"
        v = self.t[:, off // 4:(off + nb) // 4]
        if dt != F32:
            v = v.bitcast(dt)
        v = v[:, :n]
        if len(shape) == 2:
            v = v.rearrange("p (a b) -> p a b", a=shape[0])
        elif len(shape) == 3:
            v = v.rearrange("p (a b c) -> p a b c", a=shape[0], b=shape[1])
        return v

    def mark(self):
        return self.top

    def reset(self, m=0):
        self.top = m


def _t5_bucket(n):
    n = np.maximum(n, 0)
    me = 16
    large = me + (np.log(np.maximum(n, 1) / me) / np.log(2048 / me) * (32 - me)).astype(np.int64)
    large = np.minimum(large, 31)
    return np.where(n < me, n, large).astype(np.int64)


NEM = 31 * 255
NEMP = 7936


def _static_tables():
    i = np.arange(31)[:, None]
    m = np.arange(255)[None, :]
    d = 16 * (m - 127) + (i - 15)
    bk = _t5_bucket(d)
    oh = np.zeros((32, NEMP), np.float32)
    oh[bk.reshape(-1), np.arange(NEM)] = 1.0
    mult = ((d >= 0) & (d <= 128)).astype(np.float32)
    mult += ((d >= 0) & (d % 4 == 0) & (d <= 512))
    mult += ((d >= 0) & (d % 16 == 0))
    mt = np.zeros((8, NEMP), np.float32)
    mt[:, :NEM] = mult.reshape(-1)[None]
    ident = np.eye(128, dtype=np.float32)
    J = np.ascontiguousarray(ident[::-1])
    invcnt = np.tile((1.0 / (np.arange(16) + 1.0)).astype(np.float32)[None], (128, 1))
    return dict(c_oh=oh, c_mult=mt, c_ident=ident, c_J=J, c_invcnt=invcnt)


def build(n_layers=NL, debug=False, stages=None):
    nc = bass.Bass("TRN2", target_bir_lowering=False)
    P = Prog(nc)
    RES = {}

    def R(*key):
        if key not in RES:
            RES[key] = Res(str(key))
        return RES[key]

    def on(s):
        return stages is None or s in stages

    def din(name, shape, dt=F32):
        return nc.dram_tensor(name, list(shape), dt, kind="ExternalInput").ap()

    def dscr(name, shape, dt=F32):
        if debug:
            return nc.dram_tensor(name, list(shape), dt, kind="ExternalOutput").ap()
        return nc.dram_tensor(name, list(shape), dt).ap()

    x = din("x", [L, D])
    mem = din("mem", [256, D])
    rel_bias = din("rel_bias", [32, 4])
    mem_norm_g = din("mem_norm_g", [1, D])
    gvec = din("gvec", [128, 5 * NL * 16 + 16])
    cvec = din("cvec", [128, 5 * NL * 4])
    cwdw = din("cwdw", [128, NL * 4 * 31])
    w_in = din("w_in", [NL, D, 3584])
    s5_sr = din("s5_sr", [NL, 128, 24])
    s5_bT = din("s5_bT", [NL, 2, 128, 1024])
    s5_cT = din("s5_cT", [NL, 2, 128, 1024])
    s5_w_glu = din("s5_w_glu", [NL, 512, 512])
    w_ua = din("w_ua", [NL, D, 256])
    pool_w = din("pool_w", [NL, 4, 128, 128])
    conv_w_pw = din("conv_w_pw", [NL, 512, 512])
    w_out = din("w_out", [NL, D, D])
    w_xq = din("w_xq", [NL, D, 512])
    w_xk = din("w_xk", [NL, D, 512])
    w_xv = din("w_xv", [NL, D, 512])
    w_xo = din("w_xo", [NL, 512, D])
    w_up = din("w_up", [NL, D, DFF])
    w_down = din("w_down", [NL, DFF, D])
    c_oh = din("c_oh", [32, NEMP])
    c_mult = din("c_mult", [8, NEMP])
    w_qkv = din("w_qkv", [NL, D, 768])
    c_ident = din("c_ident", [128, 128])
    c_J = din("c_J", [128, 128])
    c_invcnt = din("c_invcnt", [128, 16])
    sel_in = din("sel", [128, 2])
    y_out = nc.dram_tensor("y", [L // 2, D], F32, kind="ExternalOutput").ap()

    hx_t = [nc.dram_tensor(f"hx{q}", [1024, 1024], F32) for q in range(4)]
    xin_t = [nc.dram_tensor(f"xin{q}", [512, 1024], F32) for q in range(4)]

    def hxv(q, r):
        return hx_t[q].ap().rearrange("(r ci p) t -> r p ci t", r=2, ci=4, p=128)[r]
    uaT_d = dscr("uaT_d", [256, L], BF16)
    xs_in = nc.dram_tensor("xs_in", [256, L], F32)
    xs_out = nc.dram_tensor("xs_out", [512, L], F32)
    ubT_d = dscr("ubT_d", [512, L])
    hgT_d = dscr("hgT_d", [512, L], BF16)
    qz_d = dscr("qz_d", [4, 128, L], BF16)
    kT_d = dscr("kT_d", [2, 128, L], BF16)
    vz_d = dscr("vz_d", [16, 128, 512], BF16)
    xa_in = nc.dram_tensor("xa_in", [256, L], F32)
    xa_out = nc.dram_tensor("xa_out", [512, L], F32)
    ynT_d = dscr("ynT_d", [D, L], BF16)
    emtab_d = dscr("emtab_d", [4, NEMP])
    em_d = dscr("em_d", [4, 128, 3968], BF16)
    dbg_d = dscr("dbg_d", [D, L]) if debug else None

    with ExitStack() as st:
        ARENA_B = 180 * 1024
        PERS_B = 27 * 1024
        arena_t = st.enter_context(nc.sbuf_tensor("arena", [128, ARENA_B // 4], F32))
        pers_t = st.enter_context(nc.sbuf_tensor("pers", [128, PERS_B // 4], F32))
        ps = st.enter_context(nc.psum_tensor("ps", [128, 8, 512], F32))
        A = Arena(arena_t, ARENA_B)
        PA = Arena(pers_t, PERS_B)

        def PS(b):
            return ps[:, b, :]

        def RP(b):
            return R("ps", b)

        class Alt:
            def __init__(self):
                self.i = 0

            def copy(self, out, in_, reads, writes):
                self.i += 1
                if self.i % 2:
                    P.act(lambda e: e.activation(out=out, in_=in_, func=AF.Copy), reads=reads, writes=writes)
                else:
                    P.dve(lambda e: e.tensor_copy(out=out, in_=in_), reads=reads, writes=writes)

        alt = Alt()

        ident = PA.alloc([128], F32)
        Jm = PA.alloc([128], F32)
        ones_f = PA.alloc([128], F32)
        ones_b = PA.alloc([128], BF16)
        ohh = PA.alloc([2, 128], BF16)
        ident_b = PA.alloc([128], BF16)
        invcnt = PA.alloc([16], F32)
        gv = PA.alloc([5 * NL * 16 + 16], F32)
        cv = PA.alloc([5 * NL * 4], F32)
        cw = PA.alloc([NL * 4 * 31], F32)
        memnT = PA.alloc([16, 256], BF16)
        KxT = PA.alloc([4, 256], BF16)
        Vx = PA.alloc([2, 512], BF16)
        rmask = PA.alloc([2048], F32)
        selt = PA.alloc([2], F32)
        r_const = R("const")

        def gcol(kind, l, c):
            if kind == 4:
                o = 4 * NL * 16 + c
            else:
                o = (kind * NL + l) * 16 + c
            return gv[:, o:o + 1]

        def ccol(kind, l, c):
            o = (kind * NL + l) * 4 + c
            return cv[:, o:o + 1]

        wq = "pool"

        def setup():
            P.dma("sp", lambda e: e.dma_start(out=ident, in_=c_ident), writes=[r_const])
            P.dma("sp", lambda e: e.dma_start(out=Jm, in_=c_J), writes=[r_const])
            P.dma("sp", lambda e: e.dma_start(out=invcnt, in_=c_invcnt), writes=[r_const])
            P.dma("sp", lambda e: e.dma_start(out=gv, in_=gvec), writes=[r_const])
            P.dma("sp", lambda e: e.dma_start(out=cv, in_=cvec), writes=[r_const])
            P.dma("sp", lambda e: e.dma_start(out=cw, in_=cwdw), writes=[r_const])
            P.dma("sp", lambda e: e.dma_start(out=selt, in_=sel_in), writes=[r_const])
            P.dve(lambda e: e.memset(ones_f, 1.0), writes=[r_const])
            P.dve(lambda e: e.memset(ones_b, 1.0), writes=[r_const])
            P.dve(lambda e: e.memset(ohh, 0.0), writes=[r_const])
            P.dve(lambda e: e.memset(ohh[:, 0, 0:64], 1.0), writes=[r_const])
            P.dve(lambda e: e.memset(ohh[:, 1, 64:128], 1.0), writes=[r_const])
            P.dve(lambda e: e.tensor_copy(out=ident_b, in_=ident), reads=[r_const], writes=[r_const])
            P.dve(lambda e: e.memset(rmask, 1.0), writes=[r_const])
            P.dve(lambda e: e.memset(rmask.rearrange("p (j t) -> p j t", t=128)[:, :, 0:1], 0.0), writes=[r_const])
            P.barrier()
            A.reset()
            xt = [A.alloc([4, 2048], F32) for _ in range(2)]
            hst = [A.alloc([16, 512], F32) for _ in range(2)]
            xv = x.rearrange("(g a p) f -> g p a f", a=4, p=128)
            for tg in range(4):
                b = tg % 2
                P.dma("sp", lambda e, b=b, tg=tg: e.dma_start(out=xt[b], in_=xv[tg]), writes=[R("xt", b)])
                for fc in range(16):
                    pb = fc % 4
                    for a in range(4):
                        P.pe(lambda e, b=b, a=a, fc=fc, pb=pb: e.transpose(out=PS(pb)[:, a * 128:(a + 1) * 128],
                                                                        in_=xt[b][:, a, fc * 128:(fc + 1) * 128], identity=ident),
                             reads=[R("xt", b), r_const], writes=[RP(pb)])
                    alt.copy(hst[b][:, fc, :], PS(pb), [RP(pb)], [R("hst", b)])
                for q in range(4):
                    P.dma("sp", lambda e, b=b, tg=tg, q=q: e.dma_start(out=hxv(q, tg // 2)[:, :, (tg % 2) * 512:(tg % 2) * 512 + 512], in_=hst[b][:, 4 * q:4 * q + 4, :]),
                          reads=[R("hst", b)], writes=[R("hx", q)])
            P.barrier()
            A.reset()
            memt = A.alloc([2, 2048], F32)
            gmem = A.alloc([2048], F32)
            tmp = A.alloc([2048], F32)
            memn = A.alloc([2, 2048], F32)
            ss = A.alloc([4], F32)
            r_m = R("memtmp")
            P.dma("sp", lambda e: e.dma_start(out=memt, in_=mem.rearrange("(a p) f -> p a f", p=128)), writes=[r_m])
            P.dma("sp", lambda e: e.dma_start(out=gmem, in_=mem_norm_g.partition_broadcast(128)), writes=[r_m])
            for a in range(2):
                P.act(lambda e, a=a: e.activation(out=tmp, in_=memt[:, a, :], func=AF.Square), reads=[r_m], writes=[r_m])
                P.dve(lambda e, a=a: e.reduce_sum(out=ss[:, a:a + 1], in_=tmp, axis=AX.X), reads=[r_m], writes=[r_m])
                P.dve(lambda e, a=a: e.tensor_scalar(out=ss[:, a:a + 1], in0=ss[:, a:a + 1], scalar1=1.0 / D, scalar2=EPS,
                                                     op0=ALU.mult, op1=ALU.add), reads=[r_m], writes=[r_m])
                P.act(lambda e, a=a: e.activation(out=ss[:, a:a + 1], in_=ss[:, a:a + 1], func=AF.Sqrt), reads=[r_m], writes=[r_m])
                P.dve(lambda e, a=a: e.reciprocal(out=ss[:, a:a + 1], in_=ss[:, a:a + 1]), reads=[r_m], writes=[r_m])
                P.dve(lambda e, a=a: e.scalar_tensor_tensor(out=memn[:, a, :], in0=memt[:, a, :], scalar=ss[:, a:a + 1], in1=gmem,
                                                            op0=ALU.mult, op1=ALU.mult), reads=[r_m], writes=[r_m])
            for c in range(16):
                pb = c % 4
                for a in range(2):
                    P.pe(lambda e, a=a, c=c, pb=pb: e.transpose(out=PS(pb)[:, a * 128:(a + 1) * 128],
                                                              in_=memn[:, a, c * 128:(c + 1) * 128], identity=ident),
                         reads=[r_m, r_const], writes=[RP(pb)])
                alt.copy(memnT[:, c, :], PS(pb)[:, 0:256], [RP(pb)], [R("memnT")])
            P.barrier()
            A.reset()
            rb = A.alloc([8], F32)
            oh = A.alloc([NEMP], F32)
            mt = A.alloc([NEMP], F32)
            emt = A.alloc([NEMP], F32)
            r_e = R("emtmp")
            P.dma("sp", lambda e: e.dma_start(out=rb[0:32, 0:4], in_=rel_bias), writes=[r_e])
            P.dma("sp", lambda e: e.dma_start(out=oh[0:32, :], in_=c_oh), writes=[r_e])
            P.dma("sp", lambda e: e.dma_start(out=mt[0:4, :], in_=c_mult[0:4, :]), writes=[r_e])
            for i in range(16):
                n = min(512, NEMP - i * 512)
                pb = i % 4
                P.pe(lambda e, i=i, n=n, pb=pb: e.matmul(PS(pb)[0:4, 0:n], rb[0:32, 0:4], oh[0:32, i * 512:i * 512 + n], start=True, stop=True),
                     reads=[r_e], writes=[RP(pb)])
                P.act(lambda e, i=i, n=n, pb=pb: e.activation(out=emt[0:4, i * 512:i * 512 + n], in_=PS(pb)[0:4, 0:n], func=AF.Exp),
                      reads=[RP(pb)], writes=[R("emt", i)])
                P.dve(lambda e, i=i, n=n: e.tensor_tensor(out=emt[0:4, i * 512:i * 512 + n], in0=emt[0:4, i * 512:i * 512 + n],
                                                          in1=mt[0:4, i * 512:i * 512 + n], op=ALU.mult),
                      reads=[R("emt", i), r_e], writes=[R("emt", i)])
            P.dma("sp", lambda e: e.dma_start(out=emtab_d, in_=emt[0:4, :]), reads=[R("emt", i) for i in range(16)], writes=[R("emtab_d")])
            hk = [A.alloc([31, 128], F32) for _ in range(2)]
            emh = [A.alloc([3968], BF16) for _ in range(2)]
            for h in range(4):
                b = h % 2
                src = bass.AP(emtab_d.tensor, h * NEMP, [[1, 128], [255, 31], [1, 128]])
                P.dma("sp", lambda e, b=b, src=src: e.dma_start(out=hk[b], in_=src), reads=[R("emtab_d")], writes=[R("hk", b)])
                hkf = hk[b].rearrange("p a b -> p (a b)")
                for q in range(8):
                    n = min(512, 3968 - q * 512)
                    pb = 4 + q % 4
                    P.pe(lambda e, q=q, n=n, pb=pb, hkf=hkf: e.matmul(PS(pb)[:, 0:n], Jm, hkf[:, q * 512:q * 512 + n], start=True, stop=True),
                         reads=[R("hk", b), r_const], writes=[RP(pb)])
                    alt.copy(emh[b][:, q * 512:q * 512 + n], PS(pb)[:, 0:n], [RP(pb)], [R("emh", b)])
                P.dma("sp", lambda e, b=b, h=h: e.dma_start(out=em_d[h], in_=emh[b]), reads=[R("emh", b)], writes=[R("em_d", h)])
            P.barrier()

        def rstd_from_sum(pb, out, tmp, nfeat, nn=512):
            P.dve(lambda e: e.tensor_scalar(out=tmp, in0=PS(pb)[:, 0:nn], scalar1=1.0 / nfeat, scalar2=EPS, op0=ALU.mult, op1=ALU.add),
                  reads=[RP(pb)], writes=[R(id(tmp))])
            P.act(lambda e: e.activation(out=tmp, in_=tmp, func=AF.Sqrt), reads=[R(id(tmp))], writes=[R(id(tmp))])
            P.dve(lambda e: e.reciprocal(out=out, in_=tmp), reads=[R(id(tmp))], writes=[R(id(out))])

        def stage_A(l):
            A.reset()
            xnT = A.alloc([16, 2048], BF16)
            mA = A.mark()
            hin = [A.alloc([16, 512], F32) for _ in range(2)]
            sqt = [A.alloc([512], BF16) for _ in range(3)]
            tmpv = A.alloc([512], F32)
            rstd = A.alloc([512], F32)
            r_xn = [R("xnT", t) for t in range(4)]
            for tt in range(4):
                b = tt % 2
                for q in range(4):
                    P.dma("sp", lambda e, b=b, tt=tt, q=q: e.dma_start(out=hin[b][:, 4 * q:4 * q + 4, :], in_=hxv(q, tt // 2)[:, :, (tt % 2) * 512:(tt % 2) * 512 + 512]),
                          reads=[R("hx", q)], writes=[R("hin", b)])
                for c in range(16):
                    s = c % 3
                    P.act(lambda e, b=b, c=c, s=s: e.activation(out=sqt[s], in_=hin[b][:, c, :], func=AF.Square),
                          reads=[R("hin", b)], writes=[R("sqt", s)])
                    P.pe(lambda e, c=c, s=s: e.matmul(PS(7), ones_b, sqt[s], start=(c == 0), stop=(c == 15)),
                         reads=[R("sqt", s), r_const], writes=[RP(7)])
                rstd_from_sum(7, rstd, tmpv, D)
                for c in range(16):
                    P.dve(lambda e, b=b, c=c, tt=tt: e.scalar_tensor_tensor(out=xnT[:, c, tt * 512:(tt + 1) * 512], in0=hin[b][:, c, :],
                                                                           scalar=gcol(0, l, c), in1=rstd, op0=ALU.mult, op1=ALU.mult),
                          reads=[R("hin", b), R(id(rstd)), r_const], writes=[r_xn[tt]])
            P.barrier()
            A.reset(mA)
            wsl = [A.alloc([16, 512], BF16) for _ in range(2)]
            st_b = [A.alloc([4, 512], BF16) for _ in range(2)]
            st_f = [A.alloc([4, 512], F32) for _ in range(2)]
            qst = [A.alloc([2, 512], BF16) for _ in range(2)]
            vst = [A.alloc([4, 128], BF16) for _ in range(2)]
            sgt = [A.alloc([512], F32) for _ in range(2)]
            wv = w_in[l].rearrange("(k p) n -> p k n", p=128)
            blocks = ["ua", "ub", "vg0", "vg1", "qk", "v"]
            for bi, bn in enumerate(blocks):
                s = bi % 2
                if bn in ("vg0", "vg1"):
                    o = 0 if bn == "vg0" else 256
                    P.dma(wq, lambda e, s=s, o=o: e.dma_start(out=wsl[s][:, :, 0:256], in_=wv[:, :, 1024 + o:1024 + o + 256]), writes=[R("wsl", s)])
                    P.dma(wq, lambda e, s=s, o=o: e.dma_start(out=wsl[s][:, :, 256:512], in_=wv[:, :, 1536 + o:1536 + o + 256]), writes=[R("wsl", s)])
                elif bn == "ua":
                    P.dma(wq, lambda e, s=s: e.dma_start(out=wsl[s][:, :, 0:256], in_=w_ua[l].rearrange("(k p) n -> p k n", p=128)), writes=[R("wsl", s)])
                elif bn == "qk":
                    P.dma(wq, lambda e, s=s: e.dma_start(out=wsl[s], in_=w_qkv[l].rearrange("(k p) n -> p k n", p=128)[:, :, 0:512]), writes=[R("wsl", s)])
                elif bn == "v":
                    P.dma(wq, lambda e, s=s: e.dma_start(out=wsl[s][:, :, 0:256], in_=w_qkv[l].rearrange("(k p) n -> p k n", p=128)[:, :, 512:768]), writes=[R("wsl", s)])
                else:
                    c0 = {"ub": 512}[bn]
                    P.dma(wq, lambda e, s=s, c0=c0: e.dma_start(out=wsl[s], in_=wv[:, :, c0:c0 + 512]), writes=[R("wsl", s)])
                if bn == "v":
                    for rp in range(16):
                        pb = rp % 4
                        sb_ = rp % 2
                        for k in range(16):
                            lhs = xnT[:, k, :].rearrange("p (s r) -> p r s", r=16)[:, rp, :]
                            P.pe(lambda e, k=k, pb=pb, lhs=lhs, s=s: e.matmul(PS(pb)[:, 0:256], lhs, wsl[s][:, k, 0:256], start=(k == 0), stop=(k == 15)),
                                 reads=r_xn + [R("wsl", s)], writes=[RP(pb)])
                        if rp < 2:
                            P.dve(lambda e, sb_=sb_: e.memset(vst[sb_], 0.0), writes=[R("vst", sb_)])
                        psv = PS(pb)[:, 0:256].rearrange("p (c e d) -> p c e d", e=2, d=64)
                        vv = vst[sb_].rearrange("p (c e) d -> p c e d", e=2)
                        P.act(lambda e, psv=psv, vv=vv: e.activation(out=vv[:, :, 0, 0:64], in_=psv[:, :, 0, :], func=AF.Copy),
                              reads=[RP(pb)], writes=[R("vst", sb_)])
                        P.dve(lambda e, psv=psv, vv=vv: e.tensor_copy(out=vv[:, :, 1, 64:128], in_=psv[:, :, 1, :]),
                              reads=[RP(pb)], writes=[R("vst", sb_)])
                        P.dma("sp", lambda e, sb_=sb_, rp=rp: e.dma_start(out=vz_d[rp], in_=vst[sb_].rearrange("p h d -> p (h d)")),
                              reads=[R("vst", sb_)], writes=[R("vz_d", rp)])
                    continue
                for tt in range(4):
                    sb_ = tt % 2
                    perm = bn == "qk"
                    for m in range(2 if bn == "ua" else 4):
                        pb = (tt * 4 + m) % 6
                        for k in range(16):
                            if perm:
                                rhs = xnT[:, k, :].rearrange("p (s r) -> p r s", r=16)[:, 4 * tt:4 * tt + 4, :]
                            else:
                                rhs = xnT[:, k, tt * 512:(tt + 1) * 512]
                            P.pe(lambda e, k=k, pb=pb, rhs=rhs, s=s, m=m: e.matmul(PS(pb), wsl[s][:, k, m * 128:(m + 1) * 128], rhs,
                                                                                start=(k == 0), stop=(k == 15)),
                                 reads=r_xn + [R("wsl", s)], writes=[RP(pb)])
                        if bn == "ua":
                            alt.copy(st_b[sb_][:, m, :], PS(pb), [RP(pb)], [R("st_b", sb_)])
                        elif bn == "ub":
                            alt.copy(st_f[sb_][:, m, :], PS(pb), [RP(pb)], [R("st_f", sb_)])
                        elif bn == "qk" and m >= 2:
                            alt.copy(st_b[sb_][:, m - 2, :], PS(pb), [RP(pb)], [R("st_b", sb_)])
                        elif bn == "qk":
                            qb = (tt * 4 + m) % 2
                            if tt * 4 + m < 2:
                                P.dve(lambda e, qb=qb: e.memset(qst[qb], 0.0), writes=[R("qst", qb)])
                            P.act(lambda e, qb=qb, pb=pb: e.activation(out=qst[qb][0:64, 0, :], in_=PS(pb)[0:64, :], func=AF.Copy),
                                  reads=[RP(pb)], writes=[R("qst", qb)])
                            P.dve(lambda e, qb=qb, pb=pb: e.tensor_copy(out=qst[qb][64:128, 1, :], in_=PS(pb)[64:128, :]),
                                  reads=[RP(pb)], writes=[R("qst", qb)])
                            dst = qz_d.rearrange("h p t -> p h t")[:, 2 * m:2 * m + 2, tt * 512:(tt + 1) * 512]
                            P.dma("sp", lambda e, qb=qb, dst=dst: e.dma_start(out=dst, in_=qst[qb]), reads=[R("qst", qb)], writes=[R("qz_d", m, tt)])
                        else:
                            pass
                    if bn in ("vg0", "vg1"):
                        for mm in range(2):
                            pv = (tt * 4 + mm) % 6
                            pg = (tt * 4 + mm + 2) % 6
                            sg = mm
                            P.act(lambda e, pg=pg, sg=sg: e.activation(out=sgt[sg], in_=PS(pg), func=AF.Sigmoid), reads=[RP(pg)], writes=[R("sgt", sg)])
                            P.dve(lambda e, pv=pv, sg=sg, sb_=sb_, mm=mm: e.tensor_tensor(out=st_b[sb_][:, mm, :], in0=PS(pv), in1=sgt[sg], op=ALU.mult),
                                  reads=[RP(pv), R("sgt", sg)], writes=[R("st_b", sb_)])
                        c0 = 0 if bn == "vg0" else 2
                        dst = hgT_d.rearrange("(c p) t -> p c t", p=128)[:, c0:c0 + 2, tt * 512:(tt + 1) * 512]
                        P.dma("sp", lambda e, sb_=sb_, dst=dst: e.dma_start(out=dst, in_=st_b[sb_][:, 0:2, :]), reads=[R("st_b", sb_)], writes=[R("hgT_d", c0, tt)])
                    elif bn == "ua":
                        dst = uaT_d.rearrange("(c p) t -> p c t", p=128)[:, :, tt * 512:(tt + 1) * 512]
                        P.dma("sp", lambda e, sb_=sb_, dst=dst: e.dma_start(out=dst, in_=st_b[sb_][:, 0:2, :]), reads=[R("st_b", sb_)], writes=[R("uaT_d", tt)])
                    elif bn == "ub":
                        dst = ubT_d.rearrange("(c p) t -> p c t", p=128)[:, :, tt * 512:(tt + 1) * 512]
                        P.dma("sp", lambda e, sb_=sb_, dst=dst: e.dma_start(out=dst, in_=st_f[sb_]), reads=[R("st_f", sb_)], writes=[R("ubT_d", tt)])
                    elif bn == "qk":
                        dst = kT_d.rearrange("c p t -> p c t")[:, :, tt * 512:(tt + 1) * 512]
                        P.dma("sp", lambda e, sb_=sb_, dst=dst: e.dma_start(out=dst, in_=st_b[sb_][:, 0:2, :]), reads=[R("st_b", sb_)], writes=[R("kT_d", tt)])
            P.barrier()

        def group_norm_store(l, mix, yt, r_y, sqt, tmpv, rstd, stb):
            dstv = ynT_d.rearrange("(c p) t -> p c t", p=128)
            for tt in range(4):
                sb_ = tt % 2
                for c in range(4):
                    s = c % 2
                    P.act(lambda e, c=c, s=s, tt=tt: e.activation(out=sqt[s], in_=yt[:, c, tt * 512:(tt + 1) * 512], func=AF.Square),
                          reads=[r_y], writes=[R("gsq", s)])
                    P.pe(lambda e, c=c, s=s: e.matmul(PS(7), ones_f, sqt[s], start=(c == 0), stop=(c == 3)),
                         reads=[R("gsq", s), r_const], writes=[RP(7)])
                rstd_from_sum(7, rstd, tmpv, 512)
                for c in range(4):
                    P.dve(lambda e, c=c, tt=tt, sb_=sb_: e.scalar_tensor_tensor(out=stb[sb_][:, c, :], in0=yt[:, c, tt * 512:(tt + 1) * 512],
                                                                               scalar=gcol(1, l, mix * 4 + c), in1=rstd, op0=ALU.mult, op1=ALU.mult),
                          reads=[r_y, R(id(rstd)), r_const], writes=[R("gstb", sb_)])
                P.dma("sp", lambda e, sb_=sb_, tt=tt: e.dma_start(out=dstv[:, mix * 4:mix * 4 + 4, tt * 512:(tt + 1) * 512], in_=stb[sb_]),
                      reads=[R("gstb", sb_)], writes=[R("ynT_d", mix, tt)])

        def horner(out, y, coefs, r_):
            n = len(coefs) - 1
            P.dve(lambda e: e.tensor_scalar(out=out, in0=y, scalar1=float(coefs[n]), scalar2=None, op0=ALU.mult), reads=[r_], writes=[r_])
            for k in range(n - 1, 0, -1):
                P.dve(lambda e, k=k: e.scalar_tensor_tensor(out=out, in0=out, scalar=float(coefs[k]), in1=y, op0=ALU.add, op1=ALU.mult), reads=[r_], writes=[r_])
            P.dve(lambda e: e.tensor_scalar(out=out, in0=out, scalar1=float(coefs[0]), scalar2=None, op0=ALU.add), reads=[r_], writes=[r_])

        FACT = [1.0]
        for _k in range(1, 14):
            FACT.append(FACT[-1] * _k)

        def exp_acc(out, x, nsq, t, r_):
            P.dve(lambda e: e.tensor_scalar(out=t, in0=x, scalar1=1.0 / (2 ** nsq), scalar2=None, op0=ALU.mult), reads=[r_], writes=[r_])
            horner(out, t, [1.0 / FACT[k] for k in range(10)], r_)
            for _ in range(nsq):
                P.dve(lambda e: e.tensor_tensor(out=out, in0=out, in1=out, op=ALU.mult), reads=[r_], writes=[r_])

        def sincos_acc(ang, co, si, t1, t2, ti, r_):
            TWO_PI = 2 * PI
            C1 = 6.28125
            C2 = TWO_PI - C1
            D_ = lambda fn: P.dve(fn, reads=[r_], writes=[r_])
            D_(lambda e: e.tensor_scalar(out=t1, in0=ang, scalar1=1.0 / TWO_PI, scalar2=None, op0=ALU.mult))
            D_(lambda e: e.tensor_copy(out=ti, in_=t1))
            D_(lambda e: e.tensor_copy(out=t1, in_=ti))
            D_(lambda e: e.scalar_tensor_tensor(out=t2, in0=t1, scalar=-C1, in1=ang, op0=ALU.mult, op1=ALU.add))
            D_(lambda e: e.scalar_tensor_tensor(out=t2, in0=t1, scalar=-C2, in1=t2, op0=ALU.mult, op1=ALU.add))
            D_(lambda e: e.tensor_scalar(out=t2, in0=t2, scalar1=0.25, scalar2=None, op0=ALU.mult))
            D_(lambda e: e.tensor_tensor(out=t1, in0=t2, in1=t2, op=ALU.mult))
            horner(si, t1, [1.0, -1.0 / FACT[3], 1.0 / FACT[5], -1.0 / FACT[7], 1.0 / FACT[9], -1.0 / FACT[11]], r_)
            D_(lambda e: e.tensor_tensor(out=si, in0=si, in1=t2, op=ALU.mult))
            horner(co, t1, [1.0, -1.0 / FACT[2], 1.0 / FACT[4], -1.0 / FACT[6], 1.0 / FACT[8], -1.0 / FACT[10], 1.0 / FACT[12]], r_)
            for _ in range(2):
                D_(lambda e: e.tensor_tensor(out=t1, in0=si, in1=si, op=ALU.mult))
                D_(lambda e: e.scalar_tensor_tensor(out=si, in0=si, scalar=2.0, in1=co, op0=ALU.mult, op1=ALU.mult))
                D_(lambda e: e.tensor_scalar(out=co, in0=t1, scalar1=-2.0, scalar2=1.0, op0=ALU.mult, op1=ALU.add))

        def s5_params(lr, li, ldt, outs, tmps, r_):
            t1, t2, t3, ti = tmps
            mag, co, si, abr, abi, fr, fi = [outs[k] for k in ("mag", "cos", "sin", "abr", "abi", "fr", "fi")]
            TT_ = lambda o, a, b, op: P.dve(lambda e: e.tensor_tensor(out=o, in0=a, in1=b, op=op), reads=[r_], writes=[r_])
            exp_acc(t2, ldt, 4, t1, r_)
            TT_(t3, lr, t2, ALU.mult)
            exp_acc(mag, t3, 0, t1, r_)
            TT_(t3, li, t2, ALU.mult)
            sincos_acc(t3, co, si, t1, fr, ti, r_)
            TT_(abr, mag, co, ALU.mult)
            TT_(abi, mag, si, ALU.mult)
            TT_(t1, lr, lr, ALU.mult)
            TT_(t2, li, li, ALU.mult)
            TT_(t1, t1, t2, ALU.add)
            P.dve(lambda e: e.reciprocal(out=t1, in_=t1), reads=[r_], writes=[r_])
            P.dve(lambda e: e.tensor_scalar(out=t2, in0=abr, scalar1=-1.0, scalar2=None, op0=ALU.add), reads=[r_], writes=[r_])
            TT_(t3, t2, lr, ALU.mult)
            TT_(fr, abi, li, ALU.mult)
            TT_(fr, fr, t3, ALU.add)
            TT_(fr, fr, t1, ALU.mult)
            TT_(t3, t2, li, ALU.mult)
            TT_(fi, abi, lr, ALU.mult)
            TT_(fi, fi, t3, ALU.subtract)
            TT_(fi, fi, t1, ALU.mult)

        def stage_S5(l):
            A.reset()
            r_p = R("s5p")
            cT = A.alloc([8, 128], F32)
            sT = A.alloc([8, 128], F32)
            BR = A.alloc([8, 128], BF16)
            BI = A.alloc([8, 128], BF16)
            CR = A.alloc([8, 128], BF16)
            CI = A.alloc([8, 128], BF16)
            sm = {k: A.alloc([8], F32) for k in ("lr", "li", "ldt", "mag", "cos", "sin", "abr", "abi", "fr", "fi", "t1", "t2", "t3",
                                                  "er", "ei", "e2r", "e2i", "rT", "q1", "q2", "ti")}
            cJ = A.alloc([8, 16], F32)
            sJ = A.alloc([8, 16], F32)
            m0 = A.mark()
            fl = {k: A.alloc([1024], F32) for k in ("fr", "fi", "t1", "t2", "t3", "b0", "b1")}
            srv = s5_sr[l]
            P.dma("sp", lambda e: e.dma_start(out=sm["lr"], in_=srv[:, 0:8]), writes=[r_p])
            P.dma("sp", lambda e: e.dma_start(out=sm["li"], in_=srv[:, 8:16]), writes=[r_p])
            P.dma("sp", lambda e: e.dma_start(out=sm["ldt"], in_=srv[:, 16:24]), writes=[r_p])
            s5_params(sm["lr"], sm["li"], sm["ldt"], sm, (sm["t1"], sm["t2"], sm["t3"], sm["ti"].bitcast(mybir.dt.int32)), r_p)
            TTp = lambda o, a, b, op: P.dve(lambda e: e.tensor_tensor(out=o, in0=a, in1=b, op=op), reads=[r_p], writes=[r_p])
            P.dve(lambda e: e.memset(cT[:, :, 0:1], 1.0), writes=[r_p])
            P.dve(lambda e: e.memset(sT[:, :, 0:1], 0.0), writes=[r_p])
            P.dve(lambda e: e.tensor_copy(out=sm["er"], in_=sm["cos"]), reads=[r_p], writes=[r_p])
            P.dve(lambda e: e.tensor_copy(out=sm["ei"], in_=sm["sin"]), reads=[r_p], writes=[r_p])
            tA = fl["t1"].rearrange("p (a b) -> p a b", a=8)
            tB = fl["t2"].rearrange("p (a b) -> p a b", a=8)

            def cplx_extend(cTab, sTab, n, er, ei, width):
                erb = er.unsqueeze(2).broadcast_to([128, 8, n])
                eib = ei.unsqueeze(2).broadcast_to([128, 8, n])
                TTp(tA[:, :, 0:n], cTab[:, :, 0:n], erb, ALU.mult)
                TTp(tB[:, :, 0:n], sTab[:, :, 0:n], eib, ALU.mult)
                TTp(cTab[:, :, n:2 * n], tA[:, :, 0:n], tB[:, :, 0:n], ALU.subtract)
                TTp(tA[:, :, 0:n], cTab[:, :, 0:n], eib, ALU.mult)
                TTp(tB[:, :, 0:n], sTab[:, :, 0:n], erb, ALU.mult)
                TTp(sTab[:, :, n:2 * n], tA[:, :, 0:n], tB[:, :, 0:n], ALU.add)

            def cplx_square(er, ei, q1, q2):
                TTp(q1, er, er, ALU.mult)
                TTp(q2, ei, ei, ALU.mult)
                TTp(q2, q1, q2, ALU.subtract)
                TTp(q1, er, ei, ALU.mult)
                P.dve(lambda e: e.tensor_scalar(out=ei, in0=q1, scalar1=2.0, scalar2=None, op0=ALU.mult), reads=[r_p], writes=[r_p])
                P.dve(lambda e: e.tensor_copy(out=er, in_=q2), reads=[r_p], writes=[r_p])

            n = 1
            P.dve(lambda e: e.tensor_copy(out=sm["rT"], in_=sm["mag"]), reads=[r_p], writes=[r_p])
            while n < 128:
                cplx_extend(cT, sT, n, sm["er"], sm["ei"], 128)
                cplx_square(sm["er"], sm["ei"], sm["q1"], sm["q2"])
                TTp(sm["rT"], sm["rT"], sm["rT"], ALU.mult)
                n *= 2
            P.dve(lambda e: e.memset(cJ[:, :, 0:1], 1.0), writes=[r_p])
            P.dve(lambda e: e.memset(sJ[:, :, 0:1], 0.0), writes=[r_p])
            n = 1
            while n < 16:
                cplx_extend(cJ, sJ, n, sm["er"], sm["ei"], 16)
                cplx_square(sm["er"], sm["ei"], sm["q1"], sm["q2"])
                n *= 2
            Dg = fl["t3"].rearrange("p (a b) -> p a b", a=8)
            for key, b0_ in (("fr", 0), ("fi", 4)):
                TTp(Dg, ident.unsqueeze(1).broadcast_to([128, 8, 128]), sm[key].unsqueeze(2).broadcast_to([128, 8, 128]), ALU.mult)
                for q in range(2):
                    P.pe(lambda e, q=q, b0_=b0_: e.matmul(PS(b0_ + q), ones_f, fl["t3"][:, q * 512:(q + 1) * 512], start=True, stop=True),
                         reads=[r_p, r_const], writes=[RP(b0_ + q)])
                    P.act(lambda e, q=q, b0_=b0_, key=key: e.activation(out=fl[key][:, q * 512:(q + 1) * 512], in_=PS(b0_ + q), func=AF.Copy),
                          reads=[RP(b0_ + q)], writes=[r_p])
            P.dma("sp", lambda e: e.dma_start(out=fl["b0"], in_=s5_bT[l, 0]), writes=[r_p])
            P.dma("sp", lambda e: e.dma_start(out=fl["b1"], in_=s5_bT[l, 1]), writes=[r_p])
            BRf = BR.rearrange("p a b -> p (a b)")
            BIf = BI.rearrange("p a b -> p (a b)")
            TTp(fl["t1"], fl["b0"], fl["fr"], ALU.mult)
            TTp(fl["t2"], fl["b1"], fl["fi"], ALU.mult)
            TTp(BRf, fl["t1"], fl["t2"], ALU.subtract)
            TTp(fl["t1"], fl["b0"], fl["fi"], ALU.mult)
            TTp(fl["t2"], fl["b1"], fl["fr"], ALU.mult)
            TTp(BIf, fl["t1"], fl["t2"], ALU.add)
            P.dma("sp", lambda e: e.dma_start(out=fl["b0"], in_=s5_cT[l, 0]), reads=[r_p], writes=[r_p])
            P.dma("sp", lambda e: e.dma_start(out=fl["b1"], in_=s5_cT[l, 1]), reads=[r_p], writes=[r_p])
            P.dve(lambda e: e.tensor_copy(out=CR.rearrange("p a b -> p (a b)"), in_=fl["b0"]), reads=[r_p], writes=[r_p])
            P.dve(lambda e: e.tensor_copy(out=CI.rearrange("p a b -> p (a b)"), in_=fl["b1"]), reads=[r_p], writes=[r_p])
            P.barrier()
            A.reset(m0)
            uaT = A.alloc([2, 2048], BF16)
            gf = A.alloc([4, 2048], F32)
            gb = A.alloc([4, 2048], BF16)
            sq2 = [A.alloc([512], F32) for _ in range(2)]
            xR = [A.alloc([2048], BF16) for _ in range(2)]
            xI = [A.alloc([2048], BF16) for _ in range(2)]
            zz = {k: A.alloc([16], F32) for k in ("ZR", "ZI", "WR", "WI", "VR", "VI", "XR", "XI", "a1", "a2", "rTt")}
            mW = A.mark()
            wR = A.alloc([2048], F32)
            wI = A.alloc([2048], F32)
            vR = A.alloc([2048], F32)
            vI = A.alloc([2048], F32)
            t1 = A.alloc([2048], F32)
            t2 = A.alloc([2048], F32)
            rmul = A.alloc([2048], F32)
            r_m = R("s5m")
            P.dma("sp", lambda e: e.dma_start(out=uaT, in_=uaT_d.rearrange("(c p) t -> p c t", p=128)),
                  reads=[R("uaT_d", t) for t in range(4)], writes=[R("uaT")])
            DV = lambda fn, rd, wr: P.dve(fn, reads=rd, writes=wr)
            v3 = lambda ap: ap.rearrange("p (j t) -> p j t", t=128)
            for gp in range(8):
                cc = gp // 4
                xb = gp % 2
                ctb = cT[:, gp, :].unsqueeze(1).broadcast_to([128, 8, 128])
                stb_ = sT[:, gp, :].unsqueeze(1).broadcast_to([128, 8, 128])
                for half in range(2):
                    for tq in range(2):
                        tok = slice(half * 1024 + tq * 512, half * 1024 + tq * 512 + 512)
                        P.pe(lambda e, gp=gp, cc=cc, tq=tq, tok=tok: e.matmul(PS(tq), BR[:, gp, :], uaT[:, cc, tok], start=True, stop=True),
                             reads=[R("uaT"), r_p], writes=[RP(tq)])
                        P.pe(lambda e, gp=gp, cc=cc, tq=tq, tok=tok: e.matmul(PS(2 + tq), BI[:, gp, :], uaT[:, cc, tok], start=True, stop=True),
                             reads=[R("uaT"), r_p], writes=[RP(2 + tq)])
                    for tq in range(2):
                        buR = PS(tq).rearrange("p (j t) -> p j t", t=128)
                        buI = PS(2 + tq).rearrange("p (j t) -> p j t", t=128)
                        ctb4 = cT[:, gp, :].unsqueeze(1).broadcast_to([128, 4, 128])
                        stb4 = sT[:, gp, :].unsqueeze(1).broadcast_to([128, 4, 128])
                        hs = slice(half * 1024 + tq * 512, half * 1024 + tq * 512 + 512)
                        rdp = [RP(tq), RP(2 + tq), r_p]
                        DV(lambda e, buR=buR, ctb4=ctb4, hs=hs: e.tensor_tensor(out=v3(t1[:, hs]), in0=buR, in1=ctb4, op=ALU.mult), rdp, [R("t1")])
                        DV(lambda e, buI=buI, stb4=stb4, hs=hs: e.tensor_tensor(out=v3(t2[:, hs]), in0=buI, in1=stb4, op=ALU.mult), rdp, [R("t2")])
                        DV(lambda e, hs=hs: e.tensor_tensor(out=wR[:, hs], in0=t1[:, hs], in1=t2[:, hs], op=ALU.add), [R("t1"), R("t2")], [R("wR")])
                        DV(lambda e, buI=buI, ctb4=ctb4, hs=hs: e.tensor_tensor(out=v3(t1[:, hs]), in0=buI, in1=ctb4, op=ALU.mult), rdp, [R("t1")])
                        DV(lambda e, buR=buR, stb4=stb4, hs=hs: e.tensor_tensor(out=v3(t2[:, hs]), in0=buR, in1=stb4, op=ALU.mult), rdp, [R("t2")])
                        DV(lambda e, hs=hs: e.tensor_tensor(out=wI[:, hs], in0=t1[:, hs], in1=t2[:, hs], op=ALU.subtract), [R("t1"), R("t2")], [R("wI")])
                DV(lambda e, gp=gp: e.tensor_scalar(out=rmul, in0=rmask, scalar1=sm["mag"][:, gp:gp + 1], scalar2=None, op0=ALU.mult),
                   [r_const, r_p], [R("rmul")])
                DV(lambda e: e.tensor_tensor_scan(out=vR, data0=rmul, data1=wR, initial=0.0, op0=ALU.mult, op1=ALU.add), [R("rmul"), R("wR")], [R("vR")])
                DV(lambda e: e.tensor_tensor_scan(out=vI, data0=rmul, data1=wI, initial=0.0, op0=ALU.mult, op1=ALU.add), [R("rmul"), R("wI")], [R("vI")])
                vRl = v3(vR)[:, :, 127]
                vIl = v3(vI)[:, :, 127]
                c127 = cT[:, gp, 127:128]
                s127 = sT[:, gp, 127:128]
                rz = R("zz")
                DV(lambda e, vIl=vIl, s127=s127: e.tensor_scalar(out=zz["a1"], in0=vIl, scalar1=s127, scalar2=None, op0=ALU.mult), [R("vI"), r_p], [rz])
                DV(lambda e, vRl=vRl, c127=c127: e.scalar_tensor_tensor(out=zz["ZR"], in0=vRl, scalar=c127, in1=zz["a1"], op0=ALU.mult, op1=ALU.subtract),
                   [R("vR"), r_p, rz], [rz])
                DV(lambda e, vIl=vIl, c127=c127: e.tensor_scalar(out=zz["a1"], in0=vIl, scalar1=c127, scalar2=None, op0=ALU.mult), [R("vI"), r_p, rz], [rz])
                DV(lambda e, vRl=vRl, s127=s127: e.scalar_tensor_tensor(out=zz["ZI"], in0=vRl, scalar=s127, in1=zz["a1"], op0=ALU.mult, op1=ALU.add),
                   [R("vR"), r_p, rz], [rz])
                cj = cJ[:, gp, :]
                sj = sJ[:, gp, :]
                Z = lambda fn: DV(fn, [rz, r_p], [rz])
                Z(lambda e, cj=cj, sj=sj: e.tensor_tensor(out=zz["a1"], in0=zz["ZR"], in1=cj, op=ALU.mult))
                Z(lambda e, cj=cj, sj=sj: e.tensor_tensor(out=zz["a2"], in0=zz["ZI"], in1=sj, op=ALU.mult))
                Z(lambda e, cj=cj, sj=sj: e.tensor_tensor(out=zz["WR"], in0=zz["a1"], in1=zz["a2"], op=ALU.add))
                Z(lambda e, cj=cj, sj=sj: e.tensor_tensor(out=zz["a1"], in0=zz["ZI"], in1=cj, op=ALU.mult))
                Z(lambda e, cj=cj, sj=sj: e.tensor_tensor(out=zz["a2"], in0=zz["ZR"], in1=sj, op=ALU.mult))
                Z(lambda e, cj=cj, sj=sj: e.tensor_tensor(out=zz["WI"], in0=zz["a1"], in1=zz["a2"], op=ALU.subtract))
                Z(lambda e, gp=gp: e.tensor_scalar(out=zz["rTt"], in0=rmask[:, 1:17], scalar1=sm["rT"][:, gp:gp + 1], scalar2=None, op0=ALU.mult))
                Z(lambda e: e.tensor_tensor_scan(out=zz["VR"], data0=zz["rTt"], data1=zz["WR"], initial=0.0, op0=ALU.mult, op1=ALU.add))
                Z(lambda e: e.tensor_tensor_scan(out=zz["VI"], data0=zz["rTt"], data1=zz["WI"], initial=0.0, op0=ALU.mult, op1=ALU.add))
                Z(lambda e, cj=cj, sj=sj: e.tensor_tensor(out=zz["a1"], in0=zz["VR"], in1=cj, op=ALU.mult))
                Z(lambda e, cj=cj, sj=sj: e.tensor_tensor(out=zz["a2"], in0=zz["VI"], in1=sj, op=ALU.mult))
                Z(lambda e, cj=cj, sj=sj: e.tensor_tensor(out=zz["XR"], in0=zz["a1"], in1=zz["a2"], op=ALU.subtract))
                Z(lambda e, cj=cj, sj=sj: e.tensor_tensor(out=zz["a1"], in0=zz["VR"], in1=sj, op=ALU.mult))
                Z(lambda e, cj=cj, sj=sj: e.tensor_tensor(out=zz["a2"], in0=zz["VI"], in1=cj, op=ALU.mult))
                Z(lambda e, cj=cj, sj=sj: e.tensor_tensor(out=zz["XI"], in0=zz["a1"], in1=zz["a2"], op=ALU.add))
                if debug and gp == 0 and on("dbgS5"):
                    for i_, k_ in enumerate(("ZR", "ZI", "WR", "WI", "VR", "VI", "XR", "XI", "rTt")):
                        P.dma("sp", lambda e, i_=i_, k_=k_: e.dma_start(out=dbg_d[1024:1152, i_ * 16:(i_ + 1) * 16], in_=zz[k_]), reads=[rz], writes=[R("dbg2")])
                    P.dma("sp", lambda e: e.dma_start(out=dbg_d[1024:1152, 160:176], in_=cJ[:, 0, :]), reads=[r_p], writes=[R("dbg2")])
                    P.dma("sp", lambda e: e.dma_start(out=dbg_d[1024:1152, 176:192], in_=sJ[:, 0, :]), reads=[r_p], writes=[R("dbg2")])
                    P.dma("sp", lambda e: e.dma_start(out=dbg_d[1024:1152, 192:208], in_=sm["rT"]), reads=[r_p], writes=[R("dbg2")])
                    P.dma("sp", lambda e: e.dma_start(out=dbg_d[1024:1152, 208:224], in_=sm["mag"]), reads=[r_p], writes=[R("dbg2")])
                    P.dma("sp", lambda e: e.dma_start(out=dbg_d[1024:1152, 224:240], in_=sm["cos"]), reads=[r_p], writes=[R("dbg2")])
                    P.dma("sp", lambda e: e.dma_start(out=dbg_d[1024:1152, 240:256], in_=sm["sin"]), reads=[r_p], writes=[R("dbg2")])
                    P.dma("sp", lambda e: e.dma_start(out=dbg_d[1152:1280, 0:128], in_=cT[:, 0, :]), reads=[r_p], writes=[R("dbg2")])
                    P.dma("sp", lambda e: e.dma_start(out=dbg_d[1152:1280, 128:256], in_=sT[:, 0, :]), reads=[r_p], writes=[R("dbg2")])
                abr = sm["abr"][:, gp:gp + 1]
                abi = sm["abi"][:, gp:gp + 1]
                Z(lambda e, abi=abi: e.tensor_scalar(out=zz["a1"], in0=zz["XI"], scalar1=abi, scalar2=None, op0=ALU.mult))
                Z(lambda e, abr=abr: e.scalar_tensor_tensor(out=zz["a2"], in0=zz["XR"], scalar=abr, in1=zz["a1"], op0=ALU.mult, op1=ALU.subtract))
                DV(lambda e: e.tensor_tensor(out=v3(wR)[:, 1:16, 0], in0=v3(wR)[:, 1:16, 0], in1=zz["a2"][:, 0:15], op=ALU.add), [rz, R("wR")], [R("wR")])
                Z(lambda e, abr=abr: e.tensor_scalar(out=zz["a1"], in0=zz["XI"], scalar1=abr, scalar2=None, op0=ALU.mult))
                Z(lambda e, abi=abi: e.scalar_tensor_tensor(out=zz["a2"], in0=zz["XR"], scalar=abi, in1=zz["a1"], op0=ALU.mult, op1=ALU.add))
                DV(lambda e: e.tensor_tensor(out=v3(wI)[:, 1:16, 0], in0=v3(wI)[:, 1:16, 0], in1=zz["a2"][:, 0:15], op=ALU.add), [rz, R("wI")], [R("wI")])
                DV(lambda e: e.tensor_tensor_scan(out=vR, data0=rmul, data1=wR, initial=0.0, op0=ALU.mult, op1=ALU.add), [R("rmul"), R("wR")], [R("vR")])
                DV(lambda e: e.tensor_tensor_scan(out=vI, data0=rmul, data1=wI, initial=0.0, op0=ALU.mult, op1=ALU.add), [R("rmul"), R("wI")], [R("vI")])
                ct16 = cT[:, gp, :].unsqueeze(1).broadcast_to([128, 16, 128])
                st16 = sT[:, gp, :].unsqueeze(1).broadcast_to([128, 16, 128])
                DV(lambda e, ct16=ct16: e.tensor_tensor(out=v3(t1), in0=v3(vR), in1=ct16, op=ALU.mult), [R("vR"), r_p], [R("t1")])
                DV(lambda e, st16=st16: e.tensor_tensor(out=v3(t2), in0=v3(vI), in1=st16, op=ALU.mult), [R("vI"), r_p], [R("t2")])
                DV(lambda e, xb=xb: e.tensor_tensor(out=xR[xb], in0=t1, in1=t2, op=ALU.subtract), [R("t1"), R("t2")], [R("xR", xb)])
                DV(lambda e, st16=st16: e.tensor_tensor(out=v3(t1), in0=v3(vR), in1=st16, op=ALU.mult), [R("vR"), r_p], [R("t1")])
                DV(lambda e, ct16=ct16: e.tensor_tensor(out=v3(t2), in0=v3(vI), in1=ct16, op=ALU.mult), [R("vI"), r_p], [R("t2")])
                DV(lambda e, xb=xb: e.scalar_tensor_tensor(out=xI[xb], in0=t1, scalar=-1.0, in1=t2, op0=ALU.mult, op1=ALU.subtract), [R("t1"), R("t2")], [R("xI", xb)])
                for tt in range(4):
                    tok = slice(tt * 512, tt * 512 + 512)
                    P.pe(lambda e, gp=gp, xb=xb, tt=tt, tok=tok: e.matmul(PS(4 + tt), CR[:, gp, :], xR[xb][:, tok], start=(gp % 4 == 0), stop=False),
                         reads=[R("xR", xb), r_p], writes=[RP(4 + tt)])
                    P.pe(lambda e, gp=gp, xb=xb, tt=tt, tok=tok: e.matmul(PS(4 + tt), CI[:, gp, :], xI[xb][:, tok], start=False, stop=(gp % 4 == 3)),
                         reads=[R("xI", xb), r_p], writes=[RP(4 + tt)])
                if gp % 4 == 3:
                    for tt in range(4):
                        tok = slice(tt * 512, tt * 512 + 512)
                        yv = gf[:, cc, tok]
                        rg = R("gf", cc, tt)
                        DV(lambda e, cc=cc, tok=tok, tt=tt, yv=yv: e.scalar_tensor_tensor(out=yv, in0=uaT[:, cc, tok], scalar=ccol(0, l, cc), in1=PS(4 + tt),
                                                                                      op0=ALU.mult, op1=ALU.add), [R("uaT"), RP(4 + tt), r_const], [rg])
            P.dma("sp", lambda e: e.dma_start(out=xs_in.ap().rearrange("(c p) t -> p c t", p=128), in_=gf[:, 0:2, :]),
                  reads=[R("gf", c_, t_) for c_ in range(2) for t_ in range(4)], writes=[R("xs_in")])
            P.cc(lambda e: e.collective_compute("AllGather", ALU.bypass, replica_groups=[[0, 1], [2, 3], [4, 5], [6, 7]],
                                                ins=[xs_in.ap().opt()], outs=[xs_out.ap().opt()]),
                 reads=[R("xs_in")], writes=[R("xs_out")])
            P.dma("sp", lambda e: e.dma_start(out=gf, in_=xs_out.ap().rearrange("(c p) t -> p c t", p=128)),
                  reads=[R("xs_out")], writes=[R("gf", c_, t_) for c_ in range(4) for t_ in range(4)])
            for cc in range(4):
                for tt in range(4):
                    tok = slice(tt * 512, tt * 512 + 512)
                    yv = gf[:, cc, tok]
                    rg = R("gf", cc, tt)
                    s = tt % 2
                    P.act(lambda e, yv=yv, s=s: e.activation(out=sq2[s], in_=yv, func=AF.Square), reads=[rg], writes=[R("sq2", s)])
                    DV(lambda e, s=s: e.tensor_scalar(out=sq2[s], in0=sq2[s], scalar1=0.044715, scalar2=1.0, op0=ALU.mult, op1=ALU.add), [R("sq2", s)], [R("sq2", s)])
                    DV(lambda e, s=s, yv=yv: e.tensor_tensor(out=sq2[s], in0=sq2[s], in1=yv, op=ALU.mult), [R("sq2", s), rg], [R("sq2", s)])
                    P.act(lambda e, s=s: e.activation(out=sq2[s], in_=sq2[s], func=AF.Sigmoid, scale=1.5957691216057308), reads=[R("sq2", s)], writes=[R("sq2", s)])
                    DV(lambda e, s=s, yv=yv: e.tensor_tensor(out=yv, in0=sq2[s], in1=yv, op=ALU.mult), [R("sq2", s), rg], [rg])
                    P.act(lambda e, yv=yv, cc=cc, tok=tok: e.activation(out=gb[:, cc, tok], in_=yv, func=AF.Copy), reads=[rg], writes=[R("gb", cc, tt)])
            P.barrier()
            A.reset(mW)
            wgl = A.alloc([4, 512], BF16)
            tmpv = A.alloc([512], F32)
            rstd = A.alloc([512], F32)
            stb = [A.alloc([4, 512], BF16) for _ in range(2)]
            P.dma(wq, lambda e: e.dma_start(out=wgl, in_=s5_w_glu[l].rearrange("(k p) n -> p k n", p=128)), writes=[R("wgl")])
            for tt in range(4):
                tok = slice(tt * 512, tt * 512 + 512)
                for m in range(4):
                    pb = (tt * 4 + m) % 4
                    for k in range(4):
                        P.pe(lambda e, k=k, m=m, pb=pb, tok=tok: e.matmul(PS(pb), wgl[:, k, m * 128:(m + 1) * 128], gb[:, k, tok], start=(k == 0), stop=(k == 3)),
                             reads=[R("wgl")] + [R("gb", k, tt)], writes=[RP(pb)])
                    s = m % 2
                    P.act(lambda e, pb=pb, s=s: e.activation(out=sq2[s], in_=PS(pb), func=AF.Sigmoid), reads=[RP(pb)], writes=[R("sq2", s)])
                    DV(lambda e, s=s, m=m, tok=tok: e.tensor_tensor(out=gf[:, m, tok], in0=gf[:, m, tok], in1=sq2[s], op=ALU.mult),
                       [R("sq2", s), R("gf", m, tt)], [R("gf", m, tt), R("ya")])
            if debug and on("dbgS5"):
                P.dma("sp", lambda e: e.dma_start(out=dbg_d.rearrange("(c p) t -> p c t", p=128)[:, 0:4, :], in_=gf), reads=[R("ya")], writes=[R("dbg")])
            group_norm_store(l, 0, gf, R("ya"), sq2, tmpv, rstd, stb)
            P.barrier()

        def stage_pool(l):
            A.reset()
            ub = A.alloc([4, 2048], F32)
            ta = A.alloc([2048], F32)
            tb = A.alloc([2048], F32)
            pb16 = A.alloc([4, 2048], BF16)
            yb = A.alloc([4, 2048], F32)
            pw = A.alloc([4, 128], BF16)
            sq2 = [A.alloc([512], F32) for _ in range(2)]
            tmpv = A.alloc([512], F32)
            rstd = A.alloc([512], F32)
            stb = [A.alloc([4, 512], BF16) for _ in range(2)]
            P.dma("sp", lambda e: e.dma_start(out=ub, in_=ubT_d.rearrange("(c p) t -> p c t", p=128)), reads=[R("ubT_d", t) for t in range(4)], writes=[R("ub")])
            P.dma(wq, lambda e: e.dma_start(out=pw, in_=pool_w[l].rearrange("g c d -> c g d")), writes=[R("pw")])
            for gi in range(4):
                w = 2 ** (gi + 1)
                src = ub[:, gi, :]
                bufs = [ta, tb]
                rn = ["ta", "tb"]
                sh = 1
                i = 0
                cur, rc = src, R("ub")
                while sh < w:
                    dst, rd = bufs[i % 2], R(rn[i % 2])
                    P.dve(lambda e, dst=dst, cur=cur, sh=sh: e.tensor_tensor(out=dst[:, sh:], in0=cur[:, sh:], in1=cur[:, :2048 - sh], op=ALU.add), reads=[rc], writes=[rd])
                    P.dve(lambda e, dst=dst, cur=cur, sh=sh: e.tensor_copy(out=dst[:, 0:sh], in_=cur[:, 0:sh]), reads=[rc], writes=[rd])
                    cur, rc = dst, rd
                    sh *= 2
                    i += 1
                P.dve(lambda e, cur=cur, src=src, gi=gi, w=w: e.scalar_tensor_tensor(out=pb16[:, gi, :], in0=cur, scalar=1.0 / w, in1=src, op0=ALU.mult, op1=ALU.subtract),
                      reads=[rc, R("ub")], writes=[R("pb16", gi)])
                other = bufs[i % 2]
                ro = R(rn[i % 2])
                P.dve(lambda e, cur=cur, other=other, w=w: e.tensor_tensor(out=other[:, 0:w], in0=cur[:, 0:w], in1=invcnt[:, 0:w], op=ALU.mult), reads=[rc, r_const], writes=[ro])
                P.dve(lambda e, other=other, src=src, gi=gi, w=w: e.tensor_tensor(out=pb16[:, gi, 0:w], in0=other[:, 0:w], in1=src[:, 0:w], op=ALU.subtract),
                      reads=[ro, R("ub")], writes=[R("pb16", gi)])
                for tt in range(4):
                    tok = slice(tt * 512, tt * 512 + 512)
                    pbk = (gi * 4 + tt) % 6
                    P.pe(lambda e, gi=gi, tok=tok, pbk=pbk: e.matmul(PS(pbk), pw[:, gi, :], pb16[:, gi, tok], start=True, stop=True),
                         reads=[R("pw"), R("pb16", gi)], writes=[RP(pbk)])
                    P.act(lambda e, gi=gi, tok=tok, pbk=pbk: e.activation(out=yb[:, gi, tok], in_=PS(pbk), func=AF.Identity, scale=ccol(1, l, gi)),
                          reads=[RP(pbk), r_const], writes=[R("yb")])
            if debug and on("dbgPool"):
                P.dma("sp", lambda e: e.dma_start(out=dbg_d.rearrange("(c p) t -> p c t", p=128)[:, 4:8, :], in_=yb), reads=[R("yb")], writes=[R("dbg")])
            group_norm_store(l, 1, yb, R("yb"), sq2, tmpv, rstd, stb)
            P.barrier()

        def stage_conv(l):
            A.reset()
            hg = A.alloc([4, 2080], BF16)
            dg = A.alloc([124, 128], BF16)
            z = A.alloc([4, 2048], F32)
            hs = A.alloc([4, 2048], BF16)
            wpw = A.alloc([4, 512], BF16)
            sq2 = [A.alloc([512], F32) for _ in range(2)]
            mean = A.alloc([512], F32)
            tmpv = A.alloc([512], F32)
            rstd = A.alloc([512], F32)
            stb = [A.alloc([4, 512], BF16) for _ in range(2)]
            P.dve(lambda e: e.memset(hg[:, :, 0:32], 0.0), writes=[R("hg")])
            P.dma("sp", lambda e: e.dma_start(out=hg[:, :, 32:2080], in_=hgT_d.rearrange("(c p) t -> p c t", p=128)),
                  reads=[R("hgT_d", c0, t) for c0 in (0, 2) for t in range(4)], writes=[R("hg")])
            P.dma(wq, lambda e: e.dma_start(out=wpw, in_=conv_w_pw[l].rearrange("(k p) n -> p k n", p=128)), writes=[R("wpw")])
            for c in range(4):
                for j in range(31):
                    o = (l * 4 + c) * 31 + j
                    P.dve(lambda e, c=c, j=j, o=o: e.tensor_scalar(out=dg[:, c * 31 + j, :], in0=ident, scalar1=cw[:, o:o + 1], scalar2=None, op0=ALU.mult),
                          reads=[r_const], writes=[R("dg", c)])
            for c in range(4):
                for tt in range(4):
                    pbk = (c * 4 + tt) % 4
                    for j in range(31):
                        off = 32 + tt * 512 - 30 + j
                        P.pe(lambda e, c=c, j=j, off=off, pbk=pbk: e.matmul(PS(pbk), dg[:, c * 31 + j, :], hg[:, c, off:off + 512], start=(j == 0), stop=(j == 30)),
                             reads=[R("dg", c), R("hg")], writes=[RP(pbk)])
                    P.act(lambda e, c=c, tt=tt, pbk=pbk: e.activation(out=z[:, c, tt * 512:(tt + 1) * 512], in_=PS(pbk), func=AF.Identity, bias=ccol(2, l, c)),
                          reads=[RP(pbk), r_const], writes=[R("z", tt)])
            for tt in range(4):
                tok = slice(tt * 512, tt * 512 + 512)
                for c in range(4):
                    P.pe(lambda e, c=c, tok=tok: e.matmul(PS(4), ones_f, z[:, c, tok], start=(c == 0), stop=(c == 3)), reads=[R("z", tt), r_const], writes=[RP(4)])
                for c in range(4):
                    s = c % 2
                    P.act(lambda e, c=c, s=s, tok=tok: e.activation(out=sq2[s], in_=z[:, c, tok], func=AF.Square), reads=[R("z", tt)], writes=[R("sq2", s)])
                    P.pe(lambda e, c=c, s=s: e.matmul(PS(5), ones_f, sq2[s], start=(c == 0), stop=(c == 3)), reads=[R("sq2", s), r_const], writes=[RP(5)])
                rm = R("lnstat")
                P.dve(lambda e: e.tensor_scalar(out=mean, in0=PS(4), scalar1=1.0 / 512, scalar2=None, op0=ALU.mult), reads=[RP(4)], writes=[rm])
                P.dve(lambda e: e.tensor_tensor(out=tmpv, in0=mean, in1=mean, op=ALU.mult), reads=[rm], writes=[rm])
                P.dve(lambda e: e.scalar_tensor_tensor(out=tmpv, in0=PS(5), scalar=1.0 / 512, in1=tmpv, op0=ALU.mult, op1=ALU.subtract), reads=[RP(5), rm], writes=[rm])
                P.dve(lambda e: e.tensor_scalar(out=tmpv, in0=tmpv, scalar1=EPS, scalar2=None, op0=ALU.add), reads=[rm], writes=[rm])
                P.act(lambda e: e.activation(out=tmpv, in_=tmpv, func=AF.Sqrt), reads=[rm], writes=[rm])
                P.dve(lambda e: e.reciprocal(out=rstd, in_=tmpv), reads=[rm], writes=[rm])
                for c in range(4):
                    s = c % 2
                    P.dve(lambda e, c=c, s=s, tok=tok: e.tensor_tensor(out=sq2[s], in0=z[:, c, tok], in1=mean, op=ALU.subtract), reads=[R("z", tt), rm], writes=[R("sq2", s)])
                    P.dve(lambda e, s=s: e.tensor_tensor(out=sq2[s], in0=sq2[s], in1=rstd, op=ALU.mult), reads=[R("sq2", s), rm], writes=[R("sq2", s)])
                    P.act(lambda e, c=c, s=s, tok=tok: e.activation(out=hs[:, c, tok], in_=sq2[s], func=AF.Silu, scale=ccol(3, l, c), bias=ccol(4, l, c)),
                          reads=[R("sq2", s), r_const], writes=[R("hs", tt)])
            for tt in range(4):
                tok = slice(tt * 512, tt * 512 + 512)
                for m in range(4):
                    pbk = (tt * 4 + m) % 4
                    for k in range(4):
                        P.pe(lambda e, k=k, m=m, pbk=pbk, tok=tok: e.matmul(PS(pbk), wpw[:, k, m * 128:(m + 1) * 128], hs[:, k, tok], start=(k == 0), stop=(k == 3)),
                             reads=[R("wpw"), R("hs", tt)], writes=[RP(pbk)])
                    alt.copy(z[:, m, tok], PS(pbk), [RP(pbk)], [R("z", tt), R("yc")])
            if debug and on("dbgConv"):
                P.dma("sp", lambda e: e.dma_start(out=dbg_d.rearrange("(c p) t -> p c t", p=128)[:, 8:12, :], in_=z), reads=[R("yc")], writes=[R("dbg")])
            group_norm_store(l, 2, z, R("yc"), sq2, tmpv, rstd, stb)
            P.barrier()

        def stage_attn(l):
            A.reset()
            em = [A.alloc([3968], BF16) for _ in range(4)]
            qz = [A.alloc([2048], BF16) for _ in range(4)]
            kT = [A.alloc([2048], BF16) for _ in range(2)]
            vz = [A.alloc([16, 128], BF16) for _ in range(4)]
            ex = [A.alloc([512], BF16) for _ in range(4)]
            pt = [A.alloc([512], BF16) for _ in range(4)]
            rcp = [A.alloc([512], F32) for _ in range(2)]
            yd = A.alloc([4, 2048], F32)
            sq2 = [A.alloc([512], F32) for _ in range(2)]
            tmpv = A.alloc([512], F32)
            rstd = A.alloc([512], F32)
            stb = [A.alloc([4, 512], BF16) for _ in range(2)]
            items = []

            def load(c):
                kb = c % 2
                P.dma("sp", lambda e, c=c, kb=kb: e.dma_start(out=kT[kb], in_=kT_d[c]), reads=[R("kT_d", t) for t in range(4)], writes=[R("kT", kb)])
                for e_ in range(2):
                    h = 2 * c + e_
                    hb = h % 4
                    P.dma("sp", lambda e, h=h, hb=hb: e.dma_start(out=em[hb], in_=em_d[h]), reads=[R("em_d", h)], writes=[R("em", hb)])
                    P.dma("sp", lambda e, h=h, hb=hb: e.dma_start(out=qz[hb], in_=qz_d[h]), reads=[R("qz_d", c, t) for t in range(4)], writes=[R("qz", hb)])
                    P.dma("sp", lambda e, h=h, hb=hb: e.dma_start(out=vz[hb], in_=vz_d.rearrange("r p (h d) -> h p r d", d=128)[h]),
                          reads=[R("vz_d", rp) for rp in range(16)], writes=[R("vz", hb)])

            for c in range(2):
                kb = c % 2
                for j in range(4):
                    for e_ in range(2):
                        for rp in range(16):
                            items.append((c, j, e_, rp, kb))
            NIT = len(items)
            LA = 2

            def front(i):
                c, j, e_, rp, kb = items[i]
                hb = (2 * c + e_) % 4
                sb_ = i % 4
                xb = i % 4
                emv = em[hb].rearrange("p (a b) -> p a b", b=128)
                P.pe(lambda e, kb=kb, rp=rp, hb=hb, j=j, sb_=sb_: e.matmul(PS(sb_), kT[kb][:, rp * 128:(rp + 1) * 128], qz[hb][:, j * 512:(j + 1) * 512],
                                                                         start=True, stop=True),
                     reads=[R("kT", kb), R("qz", hb)], writes=[RP(sb_)])
                P.act(lambda e, sb_=sb_, xb=xb: e.activation(out=ex[xb], in_=PS(sb_), func=AF.Exp, scale=0.125), reads=[RP(sb_)], writes=[R("ex", xb)])
                d0 = 4 * j - rp + 15
                emslice = emv[:, d0:d0 + 4, :].rearrange("p a b -> p (a b)")
                P.dve(lambda e, xb=xb, emslice=emslice: e.tensor_tensor(out=pt[xb], in0=ex[xb], in1=emslice, op=ALU.mult),
                      reads=[R("ex", xb), R("em", hb)], writes=[R("pt", xb)])

            def back(i):
                c, j, e_, rp, kb = items[i]
                hb = (2 * c + e_) % 4
                xb = i % 4
                nb = 4 + (c * 4 + j) % 2
                db = 6 + (c * 4 + j) % 2
                first = (e_ == 0 and rp == 0)
                last = (e_ == 1 and rp == 15)
                P.pe(lambda e, hb=hb, rp=rp, xb=xb, nb=nb, first=first, last=last: e.matmul(PS(nb), vz[hb][:, rp, :], pt[xb], start=first, stop=last),
                     reads=[R("vz", hb), R("pt", xb)], writes=[RP(nb)])
                P.pe(lambda e, e_=e_, xb=xb, db=db, first=first, last=last: e.matmul(PS(db), ohh[:, e_, :], pt[xb], start=first, stop=last),
                     reads=[r_const, R("pt", xb)], writes=[RP(db)])
                if last:
                    rb_ = (c * 4 + j) % 2
                    P.dve(lambda e, db=db, rb_=rb_: e.reciprocal(out=rcp[rb_], in_=PS(db)), reads=[RP(db)], writes=[R("rcp", rb_)])
                    dst = yd[:, c, :].rearrange("p (s r) -> p r s", r=16)[:, 4 * j:4 * j + 4, :]
                    P.dve(lambda e, nb=nb, rb_=rb_, dst=dst: e.tensor_tensor(out=dst, in0=PS(nb).rearrange("p (r s) -> p r s", r=4),
                                                                           in1=rcp[rb_].rearrange("p (r s) -> p r s", r=4), op=ALU.mult),
                          reads=[RP(nb), R("rcp", rb_)], writes=[R("yd")])

            load(0)
            for i in range(NIT + LA):
                if i < NIT and i % 128 == LA + 1 and i // 128 + 1 < 2:
                    load(i // 128 + 1)
                if i < NIT:
                    front(i)
                if i - LA >= 0:
                    back(i - LA)
            P.dma("sp", lambda e: e.dma_start(out=xa_in.ap().rearrange("(c p) t -> p c t", p=128), in_=yd[:, 0:2, :]), reads=[R("yd")], writes=[R("xa_in")])
            P.cc(lambda e: e.collective_compute("AllGather", ALU.bypass, replica_groups=[[0, 1], [2, 3], [4, 5], [6, 7]],
                                                ins=[xa_in.ap().opt()], outs=[xa_out.ap().opt()]),
                 reads=[R("xa_in")], writes=[R("xa_out")])
            P.dma("sp", lambda e: e.dma_start(out=yd, in_=xa_out.ap().rearrange("(c p) t -> p c t", p=128)), reads=[R("xa_out")], writes=[R("yd")])
            if debug and on("dbgAttn"):
                P.dma("sp", lambda e: e.dma_start(out=dbg_d.rearrange("(c p) t -> p c t", p=128)[:, 12:16, :], in_=yd), reads=[R("yd")], writes=[R("dbg")])
            group_norm_store(l, 3, yd, R("yd"), sq2, tmpv, rstd, stb)
            P.barrier()

        def stage_post(l, last):
            A.reset()
            T0 = 0
            acc = A.alloc([16, 1024], F32)
            actb = A.alloc([16, 1024], BF16)
            NS = 3
            wsl = [A.alloc([8192], BF16) for _ in range(NS)]
            sqt = [A.alloc([512], BF16) for _ in range(3)]
            tmpv = A.alloc([512], F32)
            rstd = A.alloc([512], F32)
            ynv = ynT_d.rearrange("(c p) t -> p c t", p=128)
            racc = [R("acc", t) for t in range(2)]
            ract = [R("actb", t) for t in range(2)]
            m1 = A.mark()
            hs0 = [A.alloc([2, 512], F32) for _ in range(2)]
            hs1 = [A.alloc([2, 512], F32) for _ in range(2)]
            ys0 = [A.alloc([2, 512], BF16) for _ in range(2)]
            ys1 = [A.alloc([2, 512], BF16) for _ in range(2)]
            si_ = 0
            for t in range(2):
                for q8 in range(8):
                    q = q8 // 2
                    c0 = (q8 % 2) * 2
                    cg = 2 * q8
                    b = si_ % 2
                    si_ += 1
                    tsl = slice(t * 512, (t + 1) * 512)
                    P.dma("sp", lambda e, b=b, q=q, c0=c0, tsl=tsl: e.dma_start(out=hs0[b], in_=hxv(q, 0)[:, c0:c0 + 2, tsl]), reads=[R("hx", q)], writes=[R("hs0", b)])
                    P.dma("sp", lambda e, b=b, q=q, c0=c0, tsl=tsl: e.dma_start(out=hs1[b], in_=hxv(q, 1)[:, c0:c0 + 2, tsl]), reads=[R("hx", q)], writes=[R("hs1", b)])
                    P.dve(lambda e, b=b, cg=cg, tsl=tsl: e.tensor_scalar(out=acc[:, cg:cg + 2, tsl], in0=hs0[b], scalar1=selt[:, 0:1], scalar2=None, op0=ALU.mult),
                          reads=[R("hs0", b), r_const], writes=[racc[t]])
                    P.dve(lambda e, b=b, cg=cg, tsl=tsl: e.scalar_tensor_tensor(out=acc[:, cg:cg + 2, tsl], in0=hs1[b], scalar=selt[:, 1:2], in1=acc[:, cg:cg + 2, tsl],
                                                                              op0=ALU.mult, op1=ALU.add),
                          reads=[R("hs1", b), r_const, racc[t]], writes=[racc[t]])
                    P.dma("sp", lambda e, b=b, cg=cg, t=t: e.dma_start(out=ys0[b], in_=ynv[:, cg:cg + 2, t * 512:(t + 1) * 512]),
                          reads=[R("ynT_d", q, tt_) for tt_ in range(4)], writes=[R("ys0", b)])
                    P.dma("sp", lambda e, b=b, cg=cg, t=t: e.dma_start(out=ys1[b], in_=ynv[:, cg:cg + 2, 1024 + t * 512:1024 + (t + 1) * 512]),
                          reads=[R("ynT_d", q, tt_) for tt_ in range(4)], writes=[R("ys1", b)])
                    P.dve(lambda e, b=b, cg=cg, tsl=tsl: e.tensor_scalar(out=actb[:, cg:cg + 2, tsl], in0=ys0[b], scalar1=selt[:, 0:1], scalar2=None, op0=ALU.mult),
                          reads=[R("ys0", b), r_const], writes=[ract[t]])
                    P.dve(lambda e, b=b, cg=cg, tsl=tsl: e.scalar_tensor_tensor(out=actb[:, cg:cg + 2, tsl], in0=ys1[b], scalar=selt[:, 1:2], in1=actb[:, cg:cg + 2, tsl],
                                                                              op0=ALU.mult, op1=ALU.add),
                          reads=[R("ys1", b), r_const, ract[t]], writes=[ract[t]])
            wctr = [0]

            def wload(src_ap, view):
                s = wctr[0] % NS
                wctr[0] += 1
                dst = view(wsl[s])
                P.dma(wq, lambda e, dst=dst, src_ap=src_ap: e.dma_start(out=dst, in_=src_ap), writes=[R("wslp", s)])
                return dst, R("wslp", s)

            v16 = lambda t: t.rearrange("p (k n) -> p k n", k=16)
            v4 = lambda t: t.rearrange("p (k n) -> p k n", k=4)
            pctr = [0]

            def nextbank(lo=0, n=6):
                b = lo + pctr[0] % n
                pctr[0] += 1
                return b

            def accum(o, t, pb):
                P.dve(lambda e, o=o, t=t, pb=pb: e.tensor_tensor(out=acc[:, o, t * 512:(t + 1) * 512], in0=acc[:, o, t * 512:(t + 1) * 512], in1=PS(pb), op=ALU.add),
                      reads=[RP(pb), racc[t]], writes=[racc[t]])

            def norm_to_actb(kind):
                for t in range(2):
                    for c in range(16):
                        s = c % 3
                        P.act(lambda e, c=c, s=s, t=t: e.activation(out=sqt[s], in_=acc[:, c, t * 512:(t + 1) * 512], func=AF.Square), reads=[racc[t]], writes=[R("sqt", s)])
                        P.pe(lambda e, c=c, s=s: e.matmul(PS(7), ones_b, sqt[s], start=(c == 0), stop=(c == 15)), reads=[R("sqt", s), r_const], writes=[RP(7)])
                    rstd_from_sum(7, rstd, tmpv, D)
                    for c in range(16):
                        P.dve(lambda e, c=c, t=t: e.scalar_tensor_tensor(out=actb[:, c, t * 512:(t + 1) * 512], in0=acc[:, c, t * 512:(t + 1) * 512],
                                                                      scalar=gcol(kind, l, c), in1=rstd, op0=ALU.mult, op1=ALU.mult),
                              reads=[racc[t], R(id(rstd)), r_const], writes=[ract[t]])

            wo = w_out[l].rearrange("(k p) n -> p k n", p=128)
            for ob in range(4):
                wt, rw = wload(wo[:, :, ob * 512:(ob + 1) * 512], v16)
                for t in range(2):
                    for m in range(4):
                        pb = nextbank()
                        for k in range(16):
                            P.pe(lambda e, k=k, m=m, t=t, pb=pb, wt=wt: e.matmul(PS(pb), wt[:, k, m * 128:(m + 1) * 128], actb[:, k, t * 512:(t + 1) * 512],
                                                                              start=(k == 0), stop=(k == 15)),
                                 reads=[rw, ract[t]], writes=[RP(pb)])
                        accum(ob * 4 + m, t, pb)
            if debug and on("dbgC") and l == 0:
                P.dma("sp", lambda e: e.dma_start(out=dbg_d.rearrange("(c p) t -> p c t", p=128)[:, :, T0:T0 + 1024], in_=acc), reads=racc, writes=[R("dbg")])
            P.barrier()
            A.reset(m1)
            qx = A.alloc([4, 1024], BF16)
            ox = A.alloc([4, 1024], BF16)
            pxt = [A.alloc([512], BF16) for _ in range(4)]
            rcp = [A.alloc([512], F32) for _ in range(2)]
            if True:
                wt, rw = wload(w_xk[l].rearrange("(k p) n -> p k n", p=128), v16)
                for hh in range(4):
                    pb = nextbank()
                    for k in range(16):
                        P.pe(lambda e, k=k, hh=hh, pb=pb, wt=wt: e.matmul(PS(pb)[:, 0:256], wt[:, k, hh * 128:(hh + 1) * 128], memnT[:, k, :], start=(k == 0), stop=(k == 15)),
                             reads=[rw, R("memnT")], writes=[RP(pb)])
                    alt.copy(KxT[:, hh, :], PS(pb)[:, 0:256], [RP(pb)], [R("KxT")])
                wt, rw = wload(w_xv[l].rearrange("(k p) n -> p k n", p=128), v16)
                for mb in range(2):
                    pb = nextbank()
                    for k in range(16):
                        P.pe(lambda e, k=k, mb=mb, pb=pb, wt=wt: e.matmul(PS(pb), memnT[:, k, mb * 128:(mb + 1) * 128], wt[:, k, :], start=(k == 0), stop=(k == 15)),
                             reads=[rw, R("memnT")], writes=[RP(pb)])
                    alt.copy(Vx[:, mb, :], PS(pb), [RP(pb)], [R("Vx")])
            norm_to_actb(2)
            wt, rw = wload(w_xq[l].rearrange("(k p) n -> p k n", p=128), v16)
            for t in range(2):
                for hh in range(4):
                    pb = nextbank()
                    for k in range(16):
                        P.pe(lambda e, k=k, hh=hh, t=t, pb=pb, wt=wt: e.matmul(PS(pb), wt[:, k, hh * 128:(hh + 1) * 128], actb[:, k, t * 512:(t + 1) * 512],
                                                                           start=(k == 0), stop=(k == 15)),
                             reads=[rw, ract[t]], writes=[RP(pb)])
                    alt.copy(qx[:, hh, t * 512:(t + 1) * 512], PS(pb), [RP(pb)], [R("qx", hh, t)])
            xs = 128 ** -0.5
            ip = 0
            for t in range(2):
                for hh in range(4):
                    pn = nextbank()
                    pd = nextbank()
                    for mb in range(2):
                        pb = nextbank()
                        xb = ip % 4
                        ip += 1
                        P.pe(lambda e, hh=hh, mb=mb, t=t, pb=pb: e.matmul(PS(pb), KxT[:, hh, mb * 128:(mb + 1) * 128], qx[:, hh, t * 512:(t + 1) * 512], start=True, stop=True),
                             reads=[R("KxT"), R("qx", hh, t)], writes=[RP(pb)])
                        P.act(lambda e, pb=pb, xb=xb: e.activation(out=pxt[xb], in_=PS(pb), func=AF.Exp, scale=xs), reads=[RP(pb)], writes=[R("pxt", xb)])
                        P.pe(lambda e, hh=hh, mb=mb, xb=xb, pn=pn: e.matmul(PS(pn), Vx[:, mb, hh * 128:(hh + 1) * 128], pxt[xb], start=(mb == 0), stop=(mb == 1)),
                             reads=[R("Vx"), R("pxt", xb)], writes=[RP(pn)])
                        P.pe(lambda e, mb=mb, xb=xb, pd=pd: e.matmul(PS(pd), ones_b, pxt[xb], start=(mb == 0), stop=(mb == 1)),
                             reads=[r_const, R("pxt", xb)], writes=[RP(pd)])
                    rb_ = (t * 4 + hh) % 2
                    P.dve(lambda e, pd=pd, rb_=rb_: e.reciprocal(out=rcp[rb_], in_=PS(pd)), reads=[RP(pd)], writes=[R("rcpx", rb_)])
                    P.dve(lambda e, pn=pn, rb_=rb_, hh=hh, t=t: e.tensor_tensor(out=ox[:, hh, t * 512:(t + 1) * 512], in0=PS(pn), in1=rcp[rb_], op=ALU.mult),
                          reads=[RP(pn), R("rcpx", rb_)], writes=[R("ox", t)])
            wt, rw = wload(w_xo[l].rearrange("(k p) n -> p k n", p=128), v4)
            for t in range(2):
                for o in range(16):
                    pb = nextbank()
                    for k in range(4):
                        P.pe(lambda e, k=k, o=o, t=t, pb=pb, wt=wt: e.matmul(PS(pb), wt[:, k, o * 128:(o + 1) * 128], ox[:, k, t * 512:(t + 1) * 512], start=(k == 0), stop=(k == 3)),
                             reads=[rw, R("ox", t)], writes=[RP(pb)])
                    accum(o, t, pb)
            if debug and on("dbgD") and l == 0:
                P.dma("sp", lambda e: e.dma_start(out=dbg_d.rearrange("(c p) t -> p c t", p=128)[:, :, T0:T0 + 1024], in_=acc), reads=racc, writes=[R("dbg")])
            P.barrier()
            A.reset(m1)
            hid = [A.alloc([4, 512], BF16) for _ in range(2)]
            rl = [A.alloc([512], F32) for _ in range(2)]
            norm_to_actb(3)
            wu = w_up[l].rearrange("(k p) n -> p k n", p=128)
            wd = w_down[l].rearrange("(k p) n -> p k n", p=128)
            ih = 0
            for fb in range(16):
                wtu, rwu = wload(wu[:, :, fb * 512:(fb + 1) * 512], v16)
                wtd, rwd = wload(wd[:, fb * 4:(fb + 1) * 4, :], v4)
                for t in range(2):
                    hb = ih % 2
                    ih += 1
                    for m in range(4):
                        pb = nextbank(0, 4)
                        for k in range(16):
                            P.pe(lambda e, k=k, m=m, t=t, pb=pb, wtu=wtu: e.matmul(PS(pb), wtu[:, k, m * 128:(m + 1) * 128], actb[:, k, t * 512:(t + 1) * 512],
                                                                                 start=(k == 0), stop=(k == 15)),
                                 reads=[rwu, ract[t]], writes=[RP(pb)])
                        s = m % 2
                        P.act(lambda e, pb=pb, s=s: e.activation(out=rl[s], in_=PS(pb), func=AF.Relu), reads=[RP(pb)], writes=[R("rl", s)])
                        P.dve(lambda e, s=s, hb=hb, m=m: e.tensor_tensor(out=hid[hb][:, m, :], in0=rl[s], in1=rl[s], op=ALU.mult), reads=[R("rl", s)], writes=[R("hid", hb)])
                    for o in range(16):
                        pb = 4 + (o % 3)
                        for k in range(4):
                            P.pe(lambda e, k=k, o=o, hb=hb, pb=pb, wtd=wtd: e.matmul(PS(pb), wtd[:, k, o * 128:(o + 1) * 128], hid[hb][:, k, :], start=(k == 0), stop=(k == 3)),
                                 reads=[rwd, R("hid", hb)], writes=[RP(pb)])
                        accum(o, t, pb)
            if not last:
                for q in range(4):
                    P.dma("sp", lambda e, q=q: e.dma_start(out=xin_t[q].ap().rearrange("(ci p) t -> p ci t", p=128), in_=acc[:, 4 * q:4 * q + 4, :]),
                          reads=racc, writes=[R("xin", q)])
                    P.cc(lambda e, q=q: e.collective_compute("AllGather", ALU.bypass, replica_groups=[[0, 1], [2, 3], [4, 5], [6, 7]],
                                                             ins=[xin_t[q].ap().opt()], outs=[hx_t[q].ap().opt()]),
                         reads=[R("xin", q)], writes=[R("hx", q)])
            else:
                P.barrier()
                A.reset(m1)
                ot = [A.alloc([2048], F32) for _ in range(2)]
                for t in range(2):
                    for c in range(16):
                        s = c % 3
                        P.act(lambda e, c=c, s=s, t=t: e.activation(out=sqt[s], in_=acc[:, c, t * 512:(t + 1) * 512], func=AF.Square), reads=[racc[t]], writes=[R("sqt", s)])
                        P.pe(lambda e, c=c, s=s: e.matmul(PS(7), ones_b, sqt[s], start=(c == 0), stop=(c == 15)), reads=[R("sqt", s), r_const], writes=[RP(7)])
                    rstd_from_sum(7, rstd, tmpv, D)
                    for c in range(16):
                        P.dve(lambda e, c=c, t=t: e.scalar_tensor_tensor(out=acc[:, c, t * 512:(t + 1) * 512], in0=acc[:, c, t * 512:(t + 1) * 512],
                                                                      scalar=gcol(4, 0, c), in1=rstd, op0=ALU.mult, op1=ALU.mult),
                              reads=[racc[t], R(id(rstd)), r_const], writes=[racc[t]])
                for tk in range(8):
                    ob = tk % 2
                    for cg in range(4):
                        pb = nextbank()
                        for ci in range(4):
                            c = cg * 4 + ci
                            P.pe(lambda e, c=c, ci=ci, tk=tk, pb=pb: e.transpose(out=PS(pb)[:, ci * 128:(ci + 1) * 128], in_=acc[:, c, tk * 128:(tk + 1) * 128], identity=ident),
                                 reads=[racc[tk // 4], r_const], writes=[RP(pb)])
                        alt.copy(ot[ob][:, cg * 512:(cg + 1) * 512], PS(pb), [RP(pb)], [R("ot", ob)])
                    P.dma("sp", lambda e, ob=ob, tk=tk: e.dma_start(out=y_out[T0 + tk * 128:T0 + (tk + 1) * 128, :], in_=ot[ob]), reads=[R("ot", ob)], writes=[R("y", tk)])
            P.barrier()

        if on("setup"):
            setup()
        for l in range(n_layers):
            if on("A"):
                stage_A(l)
            if on("S5"):
                stage_S5(l)
            if on("pool"):
                stage_pool(l)
            if on("conv"):
                stage_conv(l)
            if on("attn"):
                stage_attn(l)
            if on("post"):
                stage_post(l, last=(l == n_layers - 1))
        nops = P.emit()
    return nc, nops


def _pc(v, nch):
    v = np.asarray(v, np.float32)
    lead = v.shape[:-1]
    v = v.reshape(lead + (nch, 128))
    return np.moveaxis(v, -1, 0)


def prep_shared(inputs, n_layers=NL):
    I = {k: np.asarray(v) for k, v in inputs.items()}
    f32 = np.float32
    sh = {}
    sh.update(_static_tables())
    sh["mem_norm_g"] = I["mem_norm_g"].astype(f32).reshape(1, D)
    g = [_pc(I[k], 16).reshape(128, NL * 16) for k in ("norm_mix_g", "grp_norm_g", "norm_x_g", "norm_mlp_g")]
    g.append(_pc(I["norm_final_g"], 16).reshape(128, 16))
    gv = np.zeros((128, 5 * NL * 16 + 16), f32)
    for i in range(4):
        gv[:, i * NL * 16:(i + 1) * NL * 16] = g[i]
    gv[:, 4 * NL * 16:4 * NL * 16 + 16] = g[4]
    sh["gvec"] = gv
    c = [_pc(I[k], 4).reshape(128, NL * 4) for k in ("s5_d", "pool_scale", "conv_b_dw", "conv_ln_g", "conv_ln_b")]
    sh["cvec_full"] = np.ascontiguousarray(np.stack([v.reshape(128, NL, 4) for v in c], axis=1), f32)
    wdw = I["conv_w_dw"].astype(f32)
    wdw = wdw.reshape(NL, 31, 4, 128).transpose(3, 0, 2, 1)
    sh["cwdw"] = np.ascontiguousarray(wdw.reshape(128, NL * 4 * 31))
    lr = I["s5_lam_re"].astype(f32)
    li = I["s5_lam_im"].astype(f32)
    ldt = np.repeat(I["s5_log_dt"].astype(f32)[:, :, None], 64, axis=2)

    def sr(a):
        return a.reshape(NL, 16, 2, 64).transpose(0, 2, 3, 1).reshape(NL, 128, 16)
    sh["s5_sr_full"] = (sr(lr), sr(li), sr(ldt))
    bT = np.zeros((NL, 2, 128, 16, 128), f32)
    cT = np.zeros((NL, 2, 128, 16, 128), f32)
    for ri, (bk, ck) in enumerate((("s5_b_re", "s5_c_re"), ("s5_b_im", "s5_c_im"))):
        Bv = I[bk].astype(f32)
        Cv = I[ck].astype(f32)
        for gp in range(16):
            for gl in range(2):
                gg = 2 * gp + gl
                r0 = (gp % 4) * 32 + gl * 16
                bT[:, ri, r0:r0 + 16, gp, gl * 64:(gl + 1) * 64] = Bv[:, gg].transpose(0, 2, 1)
                cT[:, ri, gl * 64:(gl + 1) * 64, gp, r0:r0 + 16] = Cv[:, gg].transpose(0, 2, 1)
    sh["s5_bT_full"] = bT
    sh["s5_cT_full"] = cT
    for k in ("w_in", "s5_w_glu", "pool_w", "conv_w_pw", "w_out", "w_xq", "w_xk", "w_xv", "w_xo", "w_up", "w_down"):
        sh[k] = np.ascontiguousarray(I[k], f32)
    return sh


def per_core(sh, inputs, r):
    o = {}
    a, b, c = sh["s5_sr_full"]
    o["s5_sr"] = np.ascontiguousarray(np.concatenate([a[:, :, 8 * r:8 * r + 8], b[:, :, 8 * r:8 * r + 8], c[:, :, 8 * r:8 * r + 8]], axis=2))
    o["s5_bT"] = np.ascontiguousarray(sh["s5_bT_full"][:, :, :, 8 * r:8 * r + 8, :].reshape(NL, 2, 128, 1024))
    o["s5_cT"] = np.ascontiguousarray(sh["s5_cT_full"][:, :, :, 8 * r:8 * r + 8, :].reshape(NL, 2, 128, 1024))
    win = np.asarray(inputs["w_in"], np.float32)
    o["w_ua"] = np.ascontiguousarray(win[:, :, 256 * r:256 * r + 256])
    o["w_qkv"] = np.ascontiguousarray(np.concatenate([win[:, :, 2048 + 256 * r:2048 + 256 * r + 256], win[:, :, 2560 + 256 * r:2560 + 256 * r + 256],
                                                      win[:, :, 3072 + 256 * r:3072 + 256 * r + 256]], axis=2))
    o["rel_bias"] = np.ascontiguousarray(np.asarray(inputs["rel_bias"], np.float32)[:, 4 * r:4 * r + 4])
    cvv = sh["cvec_full"].copy()
    d = cvv[:, 0].copy()
    cvv[:, 0, :, 0:2] = d[:, :, 2 * r:2 * r + 2]
    o["cvec"] = np.ascontiguousarray(cvv.reshape(128, 5 * NL * 4))
    return o


_CACHE = {}


def kernel(**inputs):
    if "nc" not in _CACHE:
        _CACHE["nc"] = build()[0]
    nc = _CACHE["nc"]
    sh = prep_shared(inputs)
    x = np.asarray(inputs["x"], np.float32)
    mem = np.asarray(inputs["mem"], np.float32)
    in_maps = []
    for c in range(8):
        m = dict(sh)
        m["x"] = np.ascontiguousarray(x[c // 2])
        m["mem"] = np.ascontiguousarray(mem[c // 2])
        sel = np.zeros((128, 2), np.float32)
        sel[:, c % 2] = 1.0
        m["sel"] = sel
        m.update(per_core(sh, inputs, c % 2))
        for k in ("s5_sr_full", "s5_bT_full", "s5_cT_full", "cvec_full"):
            m.pop(k, None)
        in_maps.append(m)
    res = run_bass_kernel_spmd(nc, in_maps, core_ids=list(range(8)))
    out = np.empty((4, L, D), np.float32)
    for c in range(8):
        out[c // 2, (c % 2) * 1024:(c % 2 + 1) * 1024] = np.asarray(res.results[c]["y"], np.float32)
    return out
```
